# Optimizing a Trainium2 kernel written in Bass

```python
import jax, jax.numpy as jnp
from jax import lax
import numpy as np

D_MODEL = 1024
BATCH = 8
SEQ = 4096
DEPTH = 2
DEC_BATCH = 128
DEC_SEQ = 4
PAST_LEN = 16384
PAGE_SIZE = 128

POOL_WIDTH = D_MODEL // 2
POOL_WINDOWS = (2, 4, 8, 16)
POOL_GROUPS = len(POOL_WINDOWS)
POOL_GROUP_DIM = POOL_WIDTH // POOL_GROUPS
POOL_CTX = max(POOL_WINDOWS) - 1
MLA_HEADS = 8
QK_NOPE_DIM = 64
QK_ROPE_DIM = 32
V_HEAD_DIM = 64
MLA_WIDTH = MLA_HEADS * V_HEAD_DIM
Q_RANK = D_MODEL // 4
KV_RANK = D_MODEL // 8
ROPE_THETA = 10000.0
SM_SCALE = (QK_NOPE_DIM + QK_ROPE_DIM) ** -0.5
Q_BLOCK = 128
D_IN = POOL_WIDTH + Q_RANK + KV_RANK + QK_ROPE_DIM
D_MIX = POOL_WIDTH + MLA_WIDTH
N_MEM = 256
MEM_HEADS = 4
MEM_HEAD_DIM = D_MODEL // MEM_HEADS
D_FF = 4 * D_MODEL
ALPHA = (2.0 * DEPTH) ** 0.25
BETA = (8.0 * DEPTH) ** -0.25
LN_EPS = 1e-5
RMS_EPS = 1e-6

kernel_name = 'hymba_pool_mla_deepnorm_step'


def layer_norm(x, g, b):
    xf = x.astype(jnp.float32)
    mu = jnp.mean(xf, axis=-1, keepdims=True)
    var = jnp.mean(jnp.square(xf - mu), axis=-1, keepdims=True)
    y = (xf - mu) * lax.rsqrt(var + LN_EPS) * g.astype(jnp.float32) + b.astype(jnp.float32)
    return y.astype(x.dtype)


def rms_norm(x, g):
    xf = x.astype(jnp.float32)
    y = xf * lax.rsqrt(jnp.mean(jnp.square(xf), axis=-1, keepdims=True) + RMS_EPS) * g.astype(jnp.float32)
    return y.astype(x.dtype)


def rope(x, pos):
    half = QK_ROPE_DIM // 2
    inv_freq = ROPE_THETA ** (-jnp.arange(half, dtype=jnp.float32) / half)
    ang = pos.astype(jnp.float32)[:, None] * inv_freq[None, :]
    ang = ang.reshape((ang.shape[0],) + (1,) * (x.ndim - 3) + (half,))
    cos, sin = jnp.cos(ang), jnp.sin(ang)
    xf = x.astype(jnp.float32)
    x1, x2 = xf[..., :half], xf[..., half:]
    return jnp.concatenate([x1 * cos - x2 * sin, x1 * sin + x2 * cos], axis=-1).astype(x.dtype)


def pool_mix(u_ctx, u, pos, w_pool, pool_scale):
    B, T, _ = u.shape
    ext = jnp.concatenate([u_ctx, u], axis=1)
    cs = jnp.cumsum(ext.astype(jnp.float32), axis=1)
    cs = jnp.concatenate([jnp.zeros_like(cs[:, :1]), cs], axis=1)
    end = POOL_CTX + 1
    means = []
    for g, w in enumerate(POOL_WINDOWS):
        c0, c1 = g * POOL_GROUP_DIM, (g + 1) * POOL_GROUP_DIM
        s = cs[:, end:end + T, c0:c1] - cs[:, end - w:end - w + T, c0:c1]
        cnt = jnp.minimum(w, pos + 1).astype(jnp.float32)[None, :, None]
        means.append(s / cnt)
    pooled = jnp.concatenate(means, axis=-1) - u.astype(jnp.float32)
    pooled = pooled.reshape(B, T, POOL_GROUPS, POOL_GROUP_DIM).astype(u.dtype)
    y = jnp.einsum('btgc,gcd->btgd', pooled, w_pool).reshape(B, T, POOL_WIDTH)
    return y * pool_scale, ext[:, -POOL_CTX:]


def mla_attend_prompt(q_lat, q_rope, ckv, k_rope):
    B, S = q_lat.shape[:2]
    nblk = S // Q_BLOCK
    ql = q_lat.reshape(B, nblk, Q_BLOCK, MLA_HEADS, KV_RANK).transpose(1, 0, 2, 3, 4)
    qr = q_rope.reshape(B, nblk, Q_BLOCK, MLA_HEADS, QK_ROPE_DIM).transpose(1, 0, 2, 3, 4)
    starts = jnp.arange(nblk, dtype=jnp.int32) * Q_BLOCK
    kpos = jnp.arange(S, dtype=jnp.int32)

    def block(args):
        qlb, qrb, start = args
        s = jnp.einsum('bqhc,bkc->bhqk', qlb, ckv) + jnp.einsum('bqhr,bkr->bhqk', qrb, k_rope)
        s = s.astype(jnp.float32) * SM_SCALE
        qpos = start + jnp.arange(Q_BLOCK, dtype=jnp.int32)
        s = jnp.where(qpos[:, None] >= kpos[None, :], s, -jnp.inf)
        p = jax.nn.softmax(s, axis=-1).astype(ckv.dtype)
        return jnp.einsum('bhqk,bkc->bqhc', p, ckv)

    o = lax.map(block, (ql, qr, starts))
    return o.transpose(1, 0, 2, 3, 4).reshape(B, S, MLA_HEADS, KV_RANK)


def mla_attend_sample(q_lat, q_rope, ckv_new, krope_new, ckv_past, krope_past):
    T = q_lat.shape[1]
    P = ckv_past.shape[1]
    s_past = jnp.einsum('bqhc,bkc->bhqk', q_lat, ckv_past) + jnp.einsum('bqhr,bkr->bhqk', q_rope, krope_past)
    s_new = jnp.einsum('bqhc,bkc->bhqk', q_lat, ckv_new) + jnp.einsum('bqhr,bkr->bhqk', q_rope, krope_new)
    causal = jnp.arange(T)[:, None] >= jnp.arange(T)[None, :]
    s_new = jnp.where(causal, s_new.astype(jnp.float32) * SM_SCALE, -jnp.inf)
    s = jnp.concatenate([s_past.astype(jnp.float32) * SM_SCALE, s_new], axis=-1)
    p = jax.nn.softmax(s, axis=-1).astype(ckv_new.dtype)
    return (jnp.einsum('bhqk,bkc->bqhc', p[..., :P], ckv_past)
            + jnp.einsum('bhqk,bkc->bqhc', p[..., P:], ckv_new))


def token_mixer(x, pos, pool_ctx, past, w_in, g_q, g_kv, w_uq, w_uk, w_uv, w_pool, pool_scale, w_o):
    B, T, _ = x.shape
    h = x @ w_in
    o0 = POOL_WIDTH
    o1 = o0 + Q_RANK
    o2 = o1 + KV_RANK
    u, c_q, c_kv, k_r = h[..., :o0], h[..., o0:o1], h[..., o1:o2], h[..., o2:]
    pool_y, pool_state = pool_mix(pool_ctx, u, pos, w_pool, pool_scale)
    q = (rms_norm(c_q, g_q) @ w_uq).reshape(B, T, MLA_HEADS, QK_NOPE_DIM + QK_ROPE_DIM)
    q_lat = jnp.einsum('bthd,chd->bthc', q[..., :QK_NOPE_DIM], w_uk)
    q_rope = rope(q[..., QK_NOPE_DIM:], pos)
    ckv = rms_norm(c_kv, g_kv)
    k_rope = rope(k_r, pos)
    if past is None:
        o_lat = mla_attend_prompt(q_lat, q_rope, ckv, k_rope)
    else:
        o_lat = mla_attend_sample(q_lat, q_rope, ckv, k_rope, past[0], past[1])
    o = jnp.einsum('bthc,chd->bthd', o_lat, w_uv).reshape(B, T, MLA_WIDTH)
    y = jnp.concatenate([pool_y, o], axis=-1) @ w_o
    return y, pool_state, ckv, k_rope


def mem_attend(x, mem_k, mem_v, w_mq, w_mo):
    B, T, _ = x.shape
    q = (x @ w_mq).reshape(B, T, MEM_HEADS, MEM_HEAD_DIM)
    s = jnp.einsum('bthd,bmhd->bhtm', q, mem_k).astype(jnp.float32) * (MEM_HEAD_DIM ** -0.5)
    p = jax.nn.softmax(s, axis=-1).astype(x.dtype)
    o = jnp.einsum('bhtm,bmhd->bthd', p, mem_v).reshape(B, T, D_MODEL)
    return o @ w_mo


def sq_relu_mlp(x, w1, w2):
    return jnp.square(jax.nn.relu(x @ w1)) @ w2


def setup_inputs(seed: int = 0) -> dict:
    key = jax.random.key(seed)
    ks = jax.random.split(key, 32)
    f32 = jnp.float32

    def nrm(k, shape, scale=1.0):
        return jax.random.normal(k, shape, f32) * scale

    n_pages = PAST_LEN // PAGE_SIZE
    n_phys = (DEC_BATCH * n_pages * 5) // 4
    page_table = jax.random.permutation(ks[7], n_phys)[:DEC_BATCH * n_pages].reshape(DEC_BATCH, n_pages).astype(jnp.int32)
    return {
        'x_prompt': nrm(ks[0], (BATCH, SEQ, D_MODEL)),
        'x_sample': nrm(ks[1], (DEC_BATCH, DEC_SEQ, D_MODEL)),
        'cache_ckv': nrm(ks[2], (DEPTH, n_phys, PAGE_SIZE, KV_RANK)),
        'cache_krope': nrm(ks[3], (DEPTH, n_phys, PAGE_SIZE, QK_ROPE_DIM)),
        'cache_mem_k': nrm(ks[4], (DEPTH, DEC_BATCH, N_MEM, MEM_HEADS, MEM_HEAD_DIM)),
        'cache_mem_v': nrm(ks[5], (DEPTH, DEC_BATCH, N_MEM, MEM_HEADS, MEM_HEAD_DIM)),
        'state_pool': nrm(ks[6], (DEPTH, DEC_BATCH, POOL_CTX, POOL_WIDTH)),
        'page_table': page_table,
        'mem_prompt': nrm(ks[8], (BATCH, N_MEM, D_MODEL)),
        'w_in': nrm(ks[9], (DEPTH, D_MODEL, D_IN), D_MODEL ** -0.5),
        'g_q': 1.0 + nrm(ks[10], (DEPTH, Q_RANK), 0.05),
        'g_kv': 1.0 + nrm(ks[11], (DEPTH, KV_RANK), 0.05),
        'w_uq': nrm(ks[12], (DEPTH, Q_RANK, MLA_HEADS * (QK_NOPE_DIM + QK_ROPE_DIM)), Q_RANK ** -0.5),
        'w_uk': nrm(ks[13], (DEPTH, KV_RANK, MLA_HEADS, QK_NOPE_DIM), KV_RANK ** -0.5),
        'w_uv': nrm(ks[14], (DEPTH, KV_RANK, MLA_HEADS, V_HEAD_DIM), KV_RANK ** -0.5),
        'w_pool': nrm(ks[15], (DEPTH, POOL_GROUPS, POOL_GROUP_DIM, POOL_GROUP_DIM), POOL_GROUP_DIM ** -0.5),
        'pool_scale': 1.0 + nrm(ks[16], (DEPTH, POOL_WIDTH), 0.1),
        'w_o': nrm(ks[17], (DEPTH, D_MIX, D_MODEL), BETA * D_MIX ** -0.5),
        'ln1_g': 1.0 + nrm(ks[18], (DEPTH, D_MODEL), 0.05),
        'ln1_b': nrm(ks[19], (DEPTH, D_MODEL), 0.02),
        'w_mq': nrm(ks[20], (DEPTH, D_MODEL, D_MODEL), D_MODEL ** -0.5),
        'w_mk': nrm(ks[21], (DEPTH, D_MODEL, D_MODEL), D_MODEL ** -0.5),
        'w_mv': nrm(ks[22], (DEPTH, D_MODEL, D_MODEL), D_MODEL ** -0.5),
        'w_mo': nrm(ks[23], (DEPTH, D_MODEL, D_MODEL), BETA * D_MODEL ** -0.5),
        'ln2_g': 1.0 + nrm(ks[24], (DEPTH, D_MODEL), 0.05),
        'ln2_b': nrm(ks[25], (DEPTH, D_MODEL), 0.02),
        'w1': nrm(ks[26], (DEPTH, D_MODEL, D_FF), D_MODEL ** -0.5),
        'w2': nrm(ks[27], (DEPTH, D_FF, D_MODEL), BETA * D_FF ** -0.5),
        'ln3_g': 1.0 + nrm(ks[28], (DEPTH, D_MODEL), 0.05),
        'ln3_b': nrm(ks[29], (DEPTH, D_MODEL), 0.02),
    }


def reference(x_prompt, x_sample, cache_ckv, cache_krope, cache_mem_k, cache_mem_v, state_pool, page_table,
              mem_prompt, w_in, g_q, g_kv, w_uq, w_uk, w_uv, w_pool, pool_scale, w_o, ln1_g, ln1_b,
              w_mq, w_mk, w_mv, w_mo, ln2_g, ln2_b, w1, w2, ln3_g, ln3_b):
    B, S, _ = x_prompt.shape
    DB, T, _ = x_sample.shape
    n_pages = page_table.shape[1]
    past_len = n_pages * PAGE_SIZE
    pos_p = jnp.arange(S, dtype=jnp.int32)
    pos_s = past_len + jnp.arange(T, dtype=jnp.int32)
    xp, xs = x_prompt, x_sample
    ckv_p_l, kr_p_l, mk_p_l, mv_p_l, pst_p_l = [], [], [], [], []
    ckv_s_l, kr_s_l, pst_s_l = [], [], []
    for l in range(DEPTH):
        mix_w = (w_in[l], g_q[l], g_kv[l], w_uq[l], w_uk[l], w_uv[l], w_pool[l], pool_scale[l], w_o[l])
        zero_ctx = jnp.zeros((B, POOL_CTX, POOL_WIDTH), xp.dtype)
        y_p, pst_p, ckv_p, kr_p = token_mixer(xp, pos_p, zero_ctx, None, *mix_w)
        xp = layer_norm(ALPHA * xp + y_p, ln1_g[l], ln1_b[l])
        ckv_past = cache_ckv[l, page_table].reshape(DB, past_len, KV_RANK)
        kr_past = cache_krope[l, page_table].reshape(DB, past_len, QK_ROPE_DIM)
        y_s, pst_s, ckv_s, kr_s = token_mixer(xs, pos_s, state_pool[l], (ckv_past, kr_past), *mix_w)
        xs = layer_norm(ALPHA * xs + y_s, ln1_g[l], ln1_b[l])
        mk_p = (mem_prompt @ w_mk[l]).reshape(B, N_MEM, MEM_HEADS, MEM_HEAD_DIM)
        mv_p = (mem_prompt @ w_mv[l]).reshape(B, N_MEM, MEM_HEADS, MEM_HEAD_DIM)
        xp = layer_norm(ALPHA * xp + mem_attend(xp, mk_p, mv_p, w_mq[l], w_mo[l]), ln2_g[l], ln2_b[l])
        xs = layer_norm(ALPHA * xs + mem_attend(xs, cache_mem_k[l], cache_mem_v[l], w_mq[l], w_mo[l]), ln2_g[l], ln2_b[l])
        xp = layer_norm(ALPHA * xp + sq_relu_mlp(xp, w1[l], w2[l]), ln3_g[l], ln3_b[l])
        xs = layer_norm(ALPHA * xs + sq_relu_mlp(xs, w1[l], w2[l]), ln3_g[l], ln3_b[l])
        ckv_p_l.append(ckv_p)
        kr_p_l.append(kr_p)
        mk_p_l.append(mk_p)
        mv_p_l.append(mv_p)
        pst_p_l.append(pst_p)
        ckv_s_l.append(ckv_s)
        kr_s_l.append(kr_s)
        pst_s_l.append(pst_s)
    return (xp, xs,
            jnp.stack(ckv_p_l), jnp.stack(kr_p_l), jnp.stack(mk_p_l), jnp.stack(mv_p_l), jnp.stack(pst_p_l),
            jnp.stack(ckv_s_l), jnp.stack(kr_s_l), jnp.stack(pst_s_l))
```

```python
import numpy as np
from contextlib import ExitStack
import concourse.bass as bass
import concourse.mybir as mybir
from concourse.bass_utils import run_bass_kernel_spmd

F32 = mybir.dt.float32
BF16 = mybir.dt.bfloat16
I32 = mybir.dt.int32
AF = mybir.ActivationFunctionType
ALU = mybir.AluOpType
AX = mybir.AxisListType

ALPHA = 4.0 ** 0.25
SM_SCALE = 96.0 ** -0.5
LN_EPS = 1e-5
RMS_EPS = 1e-6
NCORES = 8


class Buf:
    __slots__ = ("name", "w", "r", "dsem", "dcount", "excl")

    def __init__(self, name, excl=False):
        self.name = name
        self.excl = excl
        self.w = None
        self.r = {}
        self.dsem = None
        self.dcount = 0


class Eng:
    def __init__(self, name, eng, sem):
        self.name, self.eng, self.sem = name, eng, sem
        self.count = 0
        self.seen = {}
        self.thunks = []


class MK:
    def __init__(self, nc, stack):
        self.nc, self.stack = nc, stack
        self.engs = {}
        for name, eng in (("pe", nc.tensor), ("act", nc.scalar), ("dve", nc.vector),
                          ("pool", nc.gpsimd), ("sp", nc.sync)):
            sem = stack.enter_context(nc.semaphore("sem_" + name))
            self.engs[name] = Eng(name, eng, sem)
        self.out_dmas = []
        self.nb = 0

    def sbuf(self, name, shape, dtype):
        return self.stack.enter_context(self.nc.sbuf_tensor(name, shape, dtype))

    def psum(self, name, shape, dtype):
        return self.stack.enter_context(self.nc.psum_tensor(name, shape, dtype))

    def buf(self, name, excl=False):
        self.nb += 1
        return Buf("%s_%d" % (name, self.nb), excl)

    def _wait(self, e, key, sem, val):
        if e.seen.get(key, 0) >= val:
            return
        e.seen[key] = val
        eng = e.eng
        e.thunks.append(lambda: eng.wait_ge(sem, val))

    def _wait_on(self, e, who, cnt, b):
        if who == "dma":
            self._wait(e, ("d", id(b)), b.dsem, cnt)
        else:
            if who == e.name and e.name in ("pe", "sp"):
                return
            self._wait(e, who, self.engs[who].sem, cnt)

    def _deps(self, e, ins, outs):
        for b in ins:
            if b.w is not None:
                self._wait_on(e, b.w[0], b.w[1], b)
            if b.excl:
                for who, cnt in b.r.items():
                    if who != e.name:
                        self._wait_on(e, who, cnt, b)
        for b in outs:
            if b.w is not None:
                self._wait_on(e, b.w[0], b.w[1], b)
            for who, cnt in b.r.items():
                self._wait_on(e, who, cnt, b)

    def op(self, ename, meth, kw, ins=(), outs=()):
        e = self.engs[ename]
        self._deps(e, ins, outs)
        e.count += 1
        c = e.count
        eng, sem = e.eng, e.sem
        e.thunks.append(lambda: getattr(eng, meth)(**kw).then_inc(sem, 1))
        for b in ins:
            b.r[ename] = c
        for b in outs:
            b.w = (ename, c)
            b.r = {}

    def dma(self, qname, meth, kw, ins=(), outs=(), deps=(), is_output=False):
        e = self.engs[qname]
        self._deps(e, list(ins) + list(deps), outs)
        eng = e.eng
        main = outs[0] if outs else ins[0]
        if main.dsem is None:
            main.dsem = self.stack.enter_context(self.nc.semaphore("d_" + main.name))
        sem = main.dsem
        main.dcount += 16
        cnt = main.dcount
        e.thunks.append(lambda: getattr(eng, meth)(**kw).then_inc(sem, 16))
        for b in outs:
            assert b is main
            b.w = ("dma", cnt)
            b.r = {}
        for b in ins:
            assert b is main
            b.r["dma"] = cnt
        if is_output:
            self.out_dmas.append((main, cnt))

    def finish(self):
        e = self.engs["sp"]
        for b, cnt in self.out_dmas:
            self._wait(e, ("d", id(b)), b.dsem, cnt)
        for name in ("pe", "act", "dve", "pool"):
            o = self.engs[name]
            if o.count:
                self._wait(e, name, o.sem, o.count)

    def emit(self):
        with self.nc.Block() as block:
            @block.tensor
            def _(x):
                for t in self.engs["pe"].thunks:
                    t()

            @block.scalar
            def _(x):
                for t in self.engs["act"].thunks:
                    t()

            @block.vector
            def _(x):
                for t in self.engs["dve"].thunks:
                    t()

            @block.gpsimd
            def _(x):
                for t in self.engs["pool"].thunks:
                    t()

            @block.sync
            def _(x):
                for t in self.engs["sp"].thunks:
                    t()


def build(S, NSB, NPHYS):
    NB = S // 512
    TS = 4 * NSB
    NT = S // 128
    nc = bass.Bass("TRN2", target_bir_lowering=False)

    def din(name, shape, dt=F32):
        return nc.dram_tensor(name, list(shape), dt, kind="ExternalInput").ap()

    def dout(name, shape):
        return nc.dram_tensor(name, list(shape), F32, kind="ExternalOutput").ap()

    xp = din("xp", [S, 1024]); xs = din("xs", [TS, 1024])
    ckvd = [din("ckv%d" % l, [NPHYS * 4, 4096]) for l in range(2)]
    krd = [din("kr%d" % l, [NPHYS * 4, 1024]) for l in range(2)]
    memk = din("memk", [2, NSB, 256, 1024]); memv = din("memv", [2, NSB, 256, 1024])
    spool = din("spool", [2, NSB * 15, 512])
    ptT = din("ptT", [128, NSB], I32); qoff = din("qoff", [128, 4 * NSB], I32)
    memp = din("memp", [256, 1024])
    w_in = din("w_in", [2, 1024, 928]); w_uq = din("w_uq", [2, 256, 768])
    w_uk = din("w_uk", [2, 128, 512]); w_uv = din("w_uv", [2, 128, 512])
    w_pool = din("w_pool", [2, 4, 128, 128]); w_o = din("w_o", [2, 1024, 1024])
    w_mq = din("w_mq", [2, 1024, 1024]); w_mk = din("w_mk", [2, 1024, 1024])
    w_mv = din("w_mv", [2, 1024, 1024]); w_mo = din("w_mo", [2, 1024, 1024])
    w1 = din("w1", [2, 1024, 4096]); w2 = din("w2", [2, 4096, 1024])
    vecs_d = din("vecs", [128, 110])
    ident_d = din("ident", [128, 128])
    cosP = din("cosP", [128, S]); sinP = din("sinP", [128, S])
    cosS_d = din("cosS", [128, TS]); sinS_d = din("sinS", [128, TS])
    maskD_d = din("maskD", [128, 4 * 512]); smask_d = din("smask", [4, 32]); corr_d = din("corr", [128, 64])

    y_p = dout("y_p", [S, 1024]); y_s = dout("y_s", [TS, 1024])
    ckv_p = dout("ckv_p", [2, S, 128]); kr_p = dout("kr_p", [2, S, 32])
    mk_p = dout("mk_p", [2, 256, 1024]); mv_p = dout("mv_p", [2, 256, 1024])
    pst_p = dout("pst_p", [2, 15, 512])
    ckv_s = dout("ckv_s", [2, TS, 128]); kr_s = dout("kr_s", [2, TS, 32])
    pst_s = dout("pst_s", [2, NSB * 15, 512])
    scr = nc.dram_tensor("scr_x1", [128, 8, S], F32, kind="Internal").ap()

    st = ExitStack()
    with st:
        mk = MK(nc, st)

        def T(name, shape, dt):
            return mk.sbuf("sb_" + name, list(shape), dt), mk.buf(name)

        import os
        KSUB = int(os.environ.get("KSUB", "100000"))
        subc = [0]

        class _Stop(Exception):
            pass

        def sub(name):
            subc[0] += 1
            if subc[0] >= KSUB:
                print("SUBSTOP", subc[0], name)
                raise _Stop()

        PS = [mk.psum("ps%d" % i, [128, 512], F32) for i in range(8)]
        PSB = [mk.buf("ps%d" % i, True) for i in range(8)]
        rot = [0]

        def nb():
            i = rot[0]
            rot[0] = (i + 1) % 4
            return PS[i], PSB[i]

        def MM(out, lhsT, rhs, start, stop, ins, outs):
            mk.op("pe", "matmul", dict(out=out, lhsT=lhsT, rhs=rhs, start=start, stop=stop), ins, outs)

        def TR(out, in_, ident, ins, outs):
            mk.op("pe", "transpose", dict(out=out, in_=in_, identity=ident), ins, outs)

        def ACT(out, in_, func, ins, outs, **kw):
            mk.op("act", "activation", dict(out=out, in_=in_, func=func, **kw), ins, outs)

        def TT(eng, out, in0, in1, op, ins, outs):
            mk.op(eng, "tensor_tensor", dict(out=out, in0=in0, in1=in1, op=op), ins, outs)

        def TSC(eng, out, in0, s1, s2, op0, op1, ins, outs):
            mk.op(eng, "tensor_scalar", dict(out=out, in0=in0, scalar1=s1, scalar2=s2, op0=op0, op1=op1), ins, outs)

        def STT(out, in0, scalar, in1, op0, op1, ins, outs):
            mk.op("dve", "scalar_tensor_tensor", dict(out=out, in0=in0, scalar=scalar, in1=in1, op0=op0, op1=op1), ins, outs)

        def RECIP(out, in_, ins, outs):
            mk.op("dve", "reciprocal", dict(out=out, in_=in_), ins, outs)

        def CP(eng, out, in_, ins, outs):
            if eng == "act":
                ACT(out, in_, AF.Copy, ins, outs)
            else:
                mk.op(eng, "tensor_copy", dict(out=out, in_=in_), ins, outs)

        def memset(eng, ap, val, outs):
            e = mk.engs[eng]
            mk._deps(e, (), outs)
            e.count += 1
            c = e.count
            en, sem = e.eng, e.sem
            e.thunks.append(lambda: en.memset(ap, val).then_inc(sem, 1))
            for b in outs:
                b.w = (eng, c)
                b.r = {}

        def LD(q, out, in_, outs, deps=()):
            mk.dma(q, "dma_start", dict(out=out, in_=in_), outs=outs, deps=deps)

        def STO(out, in_, ins):
            mk.dma("sp", "dma_start", dict(out=out, in_=in_), ins=ins, is_output=True)

        evt = [0]

        def ev_eng():
            evt[0] ^= 1
            return "act" if evt[0] else "dve"

        ident32, b_id32 = T("ident32", [128, 128], F32)
        identb, b_idb = T("identb", [128, 128], BF16)
        onesb, b_ones = T("onesb", [128, 128], BF16)
        ones32, b_ones32 = T("ones32", [128, 128], F32)
        maskD, b_mask = T("maskD", [128, 4, 512], BF16)
        smask, b_smask = T("smask", [4, 32], F32)
        corr, b_corr = T("corr", [128, 4, 16], F32)
        cosS, b_cosS = T("cosS", [128, TS], F32)
        sinS, b_sinS = T("sinS", [128, TS], F32)
        vecs, b_vecs = T("vecs", [128, 110], F32)
        pt_sb, b_pt = T("pt_sb", [128, NSB], I32)
        qoff_sb, b_qoff = T("qoff_sb", [128, 4, NSB], I32)
        idx4, b_idx4 = T("idx4", [128, 4, NSB], I32)
        pt2, b_pt2 = T("pt2", [128, NSB], I32)

        LD("sp", ident32[:], ident_d, [b_id32])
        LD("pool", identb[:], ident_d, [b_idb])
        memset("dve", onesb[:], 1.0, [b_ones])
        memset("dve", ones32[:], 1.0, [b_ones32])
        LD("pool", maskD[:], maskD_d.rearrange("p (j n) -> p j n", j=4), [b_mask])
        LD("sp", smask[:], smask_d, [b_smask])
        LD("sp", corr[:], corr_d.rearrange("p (g t) -> p g t", g=4), [b_corr])
        LD("sp", cosS[:], cosS_d, [b_cosS])
        LD("sp", sinS[:], sinS_d, [b_sinS])
        LD("sp", vecs[:], vecs_d, [b_vecs])
        LD("sp", pt_sb[:], ptT, [b_pt])
        LD("sp", qoff_sb[:], qoff.rearrange("p (q b) -> p q b", q=4), [b_qoff])
        TT("pool", pt2[:], pt_sb[:], pt_sb[:], ALU.add, [b_pt], [b_pt2])
        TT("pool", pt_sb[:], pt2[:], pt2[:], ALU.add, [b_pt2], [b_pt])
        for q in range(4):
            TT("pool", idx4[:, q, :], pt_sb[:], qoff_sb[:, q, :], ALU.add, [b_pt, b_qoff], [b_idx4])

        def V(l, a, b=None):
            c0 = 55 * l + a
            return vecs[:, c0:c0 + 1] if b is None else vecs[:, c0 + b:c0 + b + 1]

        KTl, b_ktl = T("KTl", [128, S], BF16)
        KTr, b_ktr = T("KTr", [64, S], BF16)
        Vt, b_v = T("Vt", [128, NT, 128], BF16)
        wuq, b_wuq = T("wuq", [128, 2, 1024], BF16)
        wuk, b_wuk = T("wuk", [128, 512], BF16)
        wukT, b_wukT = T("wukT", [128, 4, 128], BF16)
        wuv, b_wuv = T("wuv", [128, 512], BF16)
        wpool, b_wpool = T("wpool", [128, 4, 128], BF16)
        mkT, b_mkT = T("mkT", [128, 8, 256], BF16)
        mvp, b_mvp = T("mvp", [128, 2, 1024], BF16)
        hal, b_hal = T("hal", [128, 4, 16], F32)

        NSLOT = 3
        FILL = -(-(4 * NSB) // (3 * NB))
        slots = [T("slab%d" % i, [128, 4096], BF16) for i in range(NSLOT)]
        gck = [T("gck%d" % i, [128, 4096], BF16) for i in range(2)]
        gkr = [T("gkr%d" % i, [128, 1024], BF16) for i in range(2)]

        xres, b_xres = T("xres", [128, 8, 512], F32)
        xbf, b_xbf = T("xbf", [128, 8, 512], BF16)
        T1, _ = T("T1", [128, 8, 512], BF16); b_t1 = [mk.buf("t1a"), mk.buf("t1b")]
        mempT = T1[:, :, 0:256]
        T2, b_t2 = T("T2", [128, 8, 512], BF16)
        T3, _ = T("T3", [128, 4, 528], F32); b_t3 = [mk.buf("t3_%d" % i) for i in range(4)]
        T4, _ = T("T4", [128, 4, 512], BF16); b_t4 = [mk.buf("t4_%d" % i) for i in range(4)]
        T5, _ = T("T5", [128, 4, 512], BF16); b_t5 = [mk.buf("t5_%d" % i) for i in range(4)]
        T6, b_t6 = T("T6", [64, 4, 512], BF16)
        T7, _ = T("T7", [128, 4, 512], F32); b_t7 = [mk.buf("t7_%d" % i) for i in range(4)]
        cqn, b_cqn = T("cqn", [128, 2, 512], BF16)
        cosb, b_cos = T("cosb", [128, 512], F32)
        sinb, b_sin = T("sinb", [128, 512], F32)
        ptmp = [T("ptmp%d" % i, [128, 528], F32) for i in range(2)]
        olat = [T("olat%d" % i, [128, 512], BF16) for i in range(2)]
        ystage, b_yst = T("ystage", [128, 1024], F32)
        xstage, b_xst = ystage, b_yst

        xres_s, b_xres_s = T("xres_s", [128, 8, TS], F32)
        xbf_s, b_xbf_s = T("xbf_s", [128, 8, TS], BF16)
        qlat_s, b_qlat_s = T("qlat_s", [128, 8, TS], BF16)
        qrope_s, b_qrope_s = T("qrope_s", [64, 8, TS], BF16)
        qrope_full, b_qrope_full = T("qrope_full", [64, 32], BF16)
        ckvnT_s, b_ckvnT_s = T("ckvnT_s", [128, TS], BF16)
        krnT_s, b_krnT_s = T("krnT_s", [64, TS], BF16)
        mix_s, b_mix_s = T("mix_s", [128, 8, TS], BF16)
        olat_s, b_olat_s = T("olat_s", [128, 8, TS], BF16)
        KTs = [T("KTs%d" % i, [128, 512], BF16) for i in range(2)]
        krTs = [T("krTs%d" % i, [64, 2, 128], BF16) for i in range(2)]
        pts = [T("pts%d" % i, [128, 16, 32], BF16) for i in range(2)]
        qlat_b, b_qlat_b = T("qlat_b", [128, 32], BF16)
        qrope_b, b_qrope_b = T("qrope_b", [64, 32], BF16)
        lacc, b_lacc = T("lacc", [128, 32], F32)
        lred, b_lred = T("lred", [128, 32], F32)
        vnew, b_vnew = T("vnew", [4, 128], BF16)
        pnew, b_pnew = T("pnew", [4, 32], F32)
        pnewb, b_pnewb = T("pnewb", [4, 32], BF16)
        recs, b_recs = T("recs", [128, 32], F32)
        memk_sb, b_memk = T("memk_sb", [128, 2, 1024], BF16)
        memv_sb, b_memv = T("memv_sb", [128, 2, 1024], BF16)
        mkT_s, b_mkT_s = T2[:, :, 0:256], b_t2

        sstate = {"i": 0}

        def slab_load(key):
            kind, l, j = key[0], key[1], key[2]
            i = sstate["i"] % NSLOT
            sstate["i"] += 1
            t, b = slots[i]
            v8 = t[:].rearrange("p (k n) -> p k n", k=8)
            v4 = t[:].rearrange("p (k n) -> p k n", k=4)
            if kind == "w_in":
                src = w_in[l].rearrange("(k p) n -> p k n", p=128)
                if j == 0:
                    LD("pool", v8[:, :, 0:512], src[:, :, 0:512], [b])
                else:
                    LD("pool", v8[:, :, 0:416], src[:, :, 512:928], [b])
                    LD("pool", v8[:, :, 416:432], src[:, :, 912:928], [b])
                    LD("pool", v8[:, :, 432:448], src[:, :, 896:912], [b])
            elif kind == "w1":
                src = w1[l].rearrange("(k p) n -> p k n", p=128)
                LD("pool", v8, src[:, :, 512 * j:512 * j + 512], [b])
            elif kind == "w2":
                src = w2[l][512 * j:512 * j + 512, :].rearrange("(k p) n -> p k n", p=128)
                LD("pool", v4, src, [b])
            else:
                wd = {"w_o": w_o, "w_mq": w_mq, "w_mk": w_mk, "w_mv": w_mv, "w_mo": w_mo}[kind]
                src = wd[l].rearrange("(k p) n -> p k n", p=128)
                LD("pool", v8, src[:, :, 512 * j:512 * j + 512], [b])
            return (v8, v4, b)

        plan = []
        loaded = []
        ptr = {"c": 0, "l": 0}

        def snext(key):
            c = ptr["c"]
            assert plan[c] == key, (plan[c], key)
            while ptr["l"] < len(plan) and ptr["l"] < c + NSLOT:
                loaded.append(slab_load(plan[ptr["l"]]))
                ptr["l"] += 1
            ptr["c"] += 1
            return loaded[c]

        def unit_keys(l):
            ks = [("w_in", l, 0), ("w_in", l, 1), ("w_o", l, 0), ("w_o", l, 1), ("w_mq", l, 0), ("w_mq", l, 1),
                  ("w_mo", l, 0), ("w_mo", l, 1)]
            ks += [("w1", l, 0)]
            for j in range(8):
                if j + 1 < 8:
                    ks += [("w1", l, j + 1)]
                ks += [("w2", l, j)]
            return ks

        for l in range(2):
            plan += [("w_mk", l, 0), ("w_mk", l, 1), ("w_mv", l, 0), ("w_mv", l, 1)]
            plan += [("w_in", l, 0), ("w_in", l, 1)]
            for blk in range(NB):
                plan += unit_keys(l)
            plan += unit_keys(l)[2:]

        def layer_norm(l, which, X, bX, XB, bXB, N, fill=0):
            gcol, bcol = 16 * which, 16 * which + 8
            sum_ps, bsum = PS[4], PSB[4]
            sq_ps, bsq = PS[6], PSB[6]
            for c in range(8):
                zb = T5[:, (2 * c) % 4, :N]; bz = b_t5[(2 * c) % 4]
                zs = T5[:, (2 * c + 1) % 4, :N]; bs = b_t5[(2 * c + 1) % 4]
                ACT(zb, X[:, c, :N], AF.Copy, [bX], [bz])
                ACT(zs, X[:, c, :N], AF.Square, [bX], [bs])
                MM(sum_ps[:, :N], onesb[:], zb, c == 0, c == 7, [b_ones, bz], [bsum])
                MM(sq_ps[:, :N], onesb[:], zs, c == 0, c == 7, [b_ones, bs], [bsq])
            m = T3[:, 0, :N]; var = T3[:, 1, :N]
            ACT(m, sum_ps[:, :N], AF.Copy, [bsum], [b_t3[0]], scale=1.0 / 1024)
            TT("dve", var, m, m, ALU.mult, [b_t3[0]], [b_t3[1]])
            STT(var, sq_ps[:, :N], 1.0 / 1024, var, ALU.mult, ALU.subtract, [bsq, b_t3[1]], [b_t3[1]])
            ACT(var, var, AF.Sqrt, [b_t3[1]], [b_t3[1]], bias=LN_EPS, scale=1.0)
            RECIP(var, var, [b_t3[1]], [b_t3[1]])
            for c in range(8):
                t = T3[:, 2 + (c % 2), :N]; bt = b_t3[2 + (c % 2)]
                TT("dve", t, X[:, c, :N], m, ALU.subtract, [bX, b_t3[0]], [bt])
                TT("dve", t, t, var, ALU.mult, [bt, b_t3[1]], [bt])
                ACT(X[:, c, :N], t, AF.Identity, [bt, b_vecs], [bX], scale=V(l, gcol, c), bias=V(l, bcol, c))
                ACT(XB[:, c, :N], t, AF.Identity, [bt, b_vecs], [bXB], scale=V(l, gcol, c), bias=V(l, bcol, c))
            if fill:
                sample_fill(fill)

        def proj_residual(kind, l, RHS, bR, X, bX, N):
            for s in range(2):
                v8, v4, bs = snext((kind, l, s))
                for c4 in range(4):
                    c = 4 * s + c4
                    ps, bp = nb()
                    for k in range(8):
                        MM(ps[:, :N], v8[:, k, 128 * c4:128 * c4 + 128], RHS[:, k, :N], k == 0, k == 7, [bs] + bR, [bp])
                    STT(X[:, c, :N], X[:, c, :N], ALPHA, ps[:, :N], ALU.mult, ALU.add, [bX, bp], [bX])

        def mem_attn(QM, bQM, OM, bOM, n, c0, mkT_, b_mk_, mv_, b_mv_):
            for h in range(4):
                pm = T4[:, 2 * (h % 2):2 * (h % 2) + 2, :]
                bpm = b_t4[2 * (h % 2):2 * (h % 2) + 2]
                for mt in range(2):
                    ps, bp = nb()
                    for dh in range(2):
                        MM(ps[:, :n], mkT_[:, 2 * h + dh, 128 * mt:128 * mt + 128], QM[:, 2 * h + dh, c0:c0 + n],
                           dh == 0, dh == 1, [b_mk_] + bQM, [bp])
                    ACT(pm[:, mt, :n], ps[:, :n], AF.Exp, [bp], [bpm[mt]], scale=1.0 / 16)
                lps, bl = nb()
                for mt in range(2):
                    MM(lps[:, :n], onesb[:], pm[:, mt, :n], mt == 0, mt == 1, [b_ones, bpm[mt]], [bl])
                rec = T3[:, h % 2, :n]; brec = b_t3[h % 2]
                RECIP(rec, lps[:, :n], [bl], [brec])
                for dh in range(2):
                    ops, bo = nb()
                    for mt in range(2):
                        MM(ops[:, :n], mv_[:, mt, 256 * h + 128 * dh:256 * h + 128 * dh + 128], pm[:, mt, :n],
                           mt == 0, mt == 1, [b_mv_, bpm[mt]], [bo])
                    TT("dve", OM[:, 2 * h + dh, c0:c0 + n], ops[:, :n], rec, ALU.mult, [bo, brec], bOM)

        def mlp(l, XB, bXB, X, bX, N):
            def stage1(j):
                v8, _, b1 = snext(("w1", l, j))
                hs = j % 2
                h1 = T1[:, 4 * hs:4 * hs + 4, :]; bh = b_t1[hs]
                for f in range(4):
                    ps, bp = nb()
                    for k in range(8):
                        MM(ps[:, :N], v8[:, k, 128 * f:128 * f + 128], XB[:, k, :N], k == 0, k == 7, [b1, bXB], [bp])
                    r = T3[:, 2 + (f % 2), :N]; br = b_t3[2 + (f % 2)]
                    ACT(r, ps[:, :N], AF.Relu, [bp], [br])
                    TT("dve", h1[:, f, :N], r, r, ALU.mult, [br], [bh])

            def stage2(j):
                _, v4, b2 = snext(("w2", l, j))
                hs = j % 2
                h1 = T1[:, 4 * hs:4 * hs + 4, :]; bh = b_t1[hs]
                for c in range(8):
                    ps, bp = nb()
                    for f in range(4):
                        MM(ps[:, :N], v4[:, f, 128 * c:128 * c + 128], h1[:, f, :N], f == 0, f == 3, [b2, bh], [bp])
                    if j == 0:
                        STT(X[:, c, :N], X[:, c, :N], ALPHA, ps[:, :N], ALU.mult, ALU.add, [bX, bp], [bX])
                    else:
                        TT("dve", X[:, c, :N], X[:, c, :N], ps[:, :N], ALU.add, [bX, bp], [bX])

            stage1(0)
            for j in range(8):
                if j + 1 < 8:
                    stage1(j + 1)
                stage2(j)

        def phaseA(l, prm, blk=0):
            N = 512 if prm else TS
            tok0 = 512 * blk
            X, bX, XB, bXB = (xres, b_xres, xbf, b_xbf) if prm else (xres_s, b_xres_s, xbf_s, b_xbf_s)
            uT = T3
            if prm:
                LD("sp", cosb[:], cosP[:, tok0:tok0 + 512], [b_cos])
                LD("sp", sinb[:], sinP[:, tok0:tok0 + 512], [b_sin])
                cs, sn, bcs, bsn = cosb, sinb, b_cos, b_sin
                if blk == 0:
                    memset("dve", uT[:, :, 0:16], 0.0, b_t3)
                else:
                    CP("dve", uT[:, :, 0:16], hal[:], [b_hal], b_t3)
            else:
                cs, sn, bcs, bsn = cosS, sinS, b_cosS, b_sinS
                uS = T3[:].rearrange("p g x -> p (g x)")[:, 0:4 * NSB * 20].rearrange("p (g b t) -> p g b t", g=4, b=NSB)
                memset("dve", T3[:], 0.0, b_t3)
                for i in range((NSB + 7) // 8):
                    nbt = min(8, NSB - 8 * i)
                    rows = nbt * 15
                    LD("sp", xstage[0:rows, 0:512], spool[l, 120 * i:120 * i + rows, :], [b_xst])
                    ps, bp = nb()
                    for g in range(4):
                        TR(ps[:, 128 * g:128 * g + rows], xstage[0:rows, 128 * g:128 * g + 128], ident32[0:rows, 0:rows],
                           [b_xst, b_id32], [bp])
                    for g in range(4):
                        CP("dve", uS[:, g, 8 * i:8 * i + nbt, 1:16],
                           ps[:, 128 * g:128 * g + rows].rearrange("p (b t) -> p b t", t=15), [bp], b_t3)
            v8, _, bs = snext(("w_in", l, 0))
            for g in range(4):
                ps, bp = nb()
                for k in range(8):
                    MM(ps[:, :N], v8[:, k, 128 * g:128 * g + 128], XB[:, k, :N], k == 0, k == 7, [bs, bXB], [bp])
                if prm:
                    CP("act", uT[:, g, 16:16 + N], ps[:, :N], [bp], b_t3)
                else:
                    CP("act", uS[:, g, :, 16:20], ps[:, :N].rearrange("p (b t) -> p b t", t=4), [bp], b_t3)
            if prm:
                sub("u")
            v8, _, bs = snext(("w_in", l, 1))
            cq32 = [T7[:, 0, :N], T7[:, 1, :N]]
            st_ps, bst = nb()
            for c in range(2):
                ps, bp = nb()
                for k in range(8):
                    MM(ps[:, :N], v8[:, k, 128 * c:128 * c + 128], XB[:, k, :N], k == 0, k == 7, [bs, bXB], [bp])
                DBG = os.environ.get("DBG", "")
                if "e" not in DBG:
                    CP("dve", cq32[c], ps[:, :N], [bp], [b_t7[c]])
                sq = T5[:, c, :N]
                if "d" not in DBG:
                    ACT(sq, ps[:, :N], AF.Square, [bp], [b_t5[c]])
                if "f" not in DBG:
                    MM(st_ps[:, :N], onesb[:], sq, c == 0, c == 1, [b_ones, b_t5[c]], [bst])
            rq = T7[:, 2, :N]
            DBG = os.environ.get("DBG", "")
            if "a" not in DBG:
                ACT(rq, st_ps[:, :N], AF.Sqrt, [bst], [b_t7[2]], bias=RMS_EPS, scale=1.0 / 256)
            if "b" not in DBG:
                RECIP(rq, rq, [b_t7[2]], [b_t7[2]])
            if "c" not in DBG:
                for c in range(2):
                    STT(cqn[:, c, :N], cq32[c], V(l, 52, c), rq, ALU.mult, ALU.mult, [b_t7[c], b_t7[2], b_vecs], [b_cqn])
            if prm:
                sub("cq")
            ps, bp = nb()
            for k in range(8):
                MM(ps[:, :N], v8[:, k, 256:384], XB[:, k, :N], k == 0, k == 7, [bs, bXB], [bp])
            ckv32 = T7[:, 3, :N]
            CP("dve", ckv32, ps[:, :N], [bp], [b_t7[3]])
            sq = T5[:, 2, :N]
            ACT(sq, ps[:, :N], AF.Square, [bp], [b_t5[2]])
            st_ps, bst = nb()
            MM(st_ps[:, :N], onesb[:], sq, True, True, [b_ones, b_t5[2]], [bst])
            rk = T7[:, 2, :N]
            ACT(rk, st_ps[:, :N], AF.Sqrt, [bst], [b_t7[2]], bias=RMS_EPS, scale=1.0 / 128)
            RECIP(rk, rk, [b_t7[2]], [b_t7[2]])
            STT(ckv32, ckv32, V(l, 54), rk, ALU.mult, ALU.mult, [b_t7[3], b_t7[2], b_vecs], [b_t7[3]])
            if prm:
                CP("act", KTl[:, tok0:tok0 + N], ckv32, [b_t7[3]], [b_ktl])
                ps, bp = nb()
                for tt in range(4):
                    TR(ps[:, 128 * tt:128 * tt + 128], ckv32[:, 128 * tt:128 * tt + 128], ident32[:], [b_t7[3], b_id32], [bp])
                CP("dve", Vt[:, 4 * blk:4 * blk + 4, :], ps[:].rearrange("p (t c) -> p t c", t=4), [bp], [b_v])
                CP("act", ystage[:, 0:512], ps[:], [bp], [b_yst])
                STO(ckv_p[l, tok0:tok0 + 512, :].rearrange("(t p) c -> p t c", p=128),
                    ystage[:, 0:512].rearrange("p (t c) -> p t c", t=4), [b_yst])
            else:
                CP("act", ckvnT_s[:], ckv32, [b_t7[3]], [b_ckvnT_s])
                ps, bp = nb()
                TR(ps[0:TS, 0:128], ckv32, ident32[:], [b_t7[3], b_id32], [bp])
                CP("act", ystage[0:TS, 0:128], ps[0:TS, 0:128], [bp], [b_yst])
                STO(ckv_s[l], ystage[0:TS, 0:128], [b_yst])
            if prm:
                sub("ckv")
            psk, bpk = nb()
            for k in range(8):
                MM(psk[0:32, :N], v8[:, k, 384:416], XB[:, k, :N], k == 0, k == 7, [bs, bXB], [bpk])
            psr, bpr = nb()
            for k in range(8):
                MM(psr[0:32, :N], v8[:, k, 416:448], XB[:, k, :N], k == 0, k == 7, [bs, bXB], [bpr])
            t1 = ptmp[0][0][0:32, :N]; bt1 = ptmp[0][1]
            t2 = ptmp[1][0][0:32, :N]; bt2 = ptmp[1][1]
            TT("dve", t1, psr[0:32, :N], sn[0:32, :N], ALU.mult, [bpr, bsn], [bt1])
            TT("dve", t2, psk[0:32, :N], cs[0:32, :N], ALU.mult, [bpk, bcs], [bt2])
            TT("dve", t1, t1, t2, ALU.add, [bt1, bt2], [bt1])
            if prm:
                CP("act", KTr[0:32, tok0:tok0 + N], t1, [bt1], [b_ktr])
                CP("dve", KTr[32:64, tok0:tok0 + N], t1, [bt1], [b_ktr])
                ps, bp = nb()
                for tt in range(4):
                    TR(ps[:, 32 * tt:32 * tt + 32], t1[:, 128 * tt:128 * tt + 128], ident32[0:32, 0:32], [bt1, b_id32], [bp])
                CP("act", ystage[:, 512:640], ps[:, 0:128], [bp], [b_yst])
                STO(kr_p[l, tok0:tok0 + 512, :].rearrange("(t p) c -> p t c", p=128),
                    ystage[:, 512:640].rearrange("p (t c) -> p t c", t=4), [b_yst])
            else:
                CP("act", krnT_s[0:32, :], t1, [bt1], [b_krnT_s])
                CP("dve", krnT_s[32:64, :], t1, [bt1], [b_krnT_s])
                ps, bp = nb()
                TR(ps[0:TS, 0:32], t1, ident32[0:32, 0:32], [bt1, b_id32], [bp])
                CP("act", ystage[0:TS, 512:544], ps[0:TS, 0:32], [bp], [b_yst])
                STO(kr_s[l], ystage[0:TS, 512:544], [b_yst])
            if prm:
                sub("kr")
            QN = T5
            for j in range(4):
                ps, bp = nb()
                for k in range(2):
                    MM(ps[:, :N], wuq[:, k, 128 * j:128 * j + 128], cqn[:, k, :N], k == 0, k == 1, [b_wuq, b_cqn], [bp])
                CP(ev_eng(), QN[:, j, :N], ps[:, :N], [bp], [b_t5[j]])
            bt1 = ptmp[0][1]; bt2 = ptmp[1][1]
            if prm:
                for j in range(4):
                    psa, bpa = nb()
                    for k in range(2):
                        MM(psa[0:64, :N], wuq[:, k, 512 + 64 * j:512 + 64 * j + 64], cqn[:, k, :N], k == 0, k == 1, [b_wuq, b_cqn], [bpa])
                    psb, bpb = nb()
                    for k in range(2):
                        MM(psb[0:64, :N], wuq[:, k, 768 + 64 * j:768 + 64 * j + 64], cqn[:, k, :N], k == 0, k == 1, [b_wuq, b_cqn], [bpb])
                    t1 = ptmp[0][0][0:64, :N]; t2 = ptmp[1][0][0:64, :N]
                    TT("dve", t1, psb[0:64, :N], sn[0:64, :N], ALU.mult, [bpb, bsn], [bt1])
                    TT("dve", t2, psa[0:64, :N], cs[0:64, :N], ALU.mult, [bpa, bcs], [bt2])
                    TT("dve", T6[:, j, :N], t1, t2, ALU.add, [bt1, bt2], [b_t6])
            else:
                for rep in range(2):
                    r0 = 32 * rep
                    psa, bpa = nb()
                    psb, bpb = nb()
                    for h in range(8):
                        for k in range(2):
                            MM(psa[r0:r0 + 32, TS * h:TS * h + TS], wuq[:, k, 512 + 32 * h:512 + 32 * h + 32], cqn[:, k, :N],
                               k == 0, k == 1, [b_wuq, b_cqn], [bpa])
                        for k in range(2):
                            MM(psb[r0:r0 + 32, TS * h:TS * h + TS], wuq[:, k, 768 + 32 * h:768 + 32 * h + 32], cqn[:, k, :N],
                               k == 0, k == 1, [b_wuq, b_cqn], [bpb])
                    for h in range(8):
                        t1 = ptmp[0][0][r0:r0 + 32, :N]; t2 = ptmp[1][0][r0:r0 + 32, :N]
                        TT("dve", t1, psb[r0:r0 + 32, TS * h:TS * h + TS], sn[r0:r0 + 32, :N], ALU.mult, [bpb, bsn], [bt1])
                        TT("dve", t2, psa[r0:r0 + 32, TS * h:TS * h + TS], cs[r0:r0 + 32, :N], ALU.mult, [bpa, bcs], [bt2])
                        TT("dve", qrope_s[r0:r0 + 32, h, :], t1, t2, ALU.add, [bt1, bt2], [b_qrope_s])
            if prm:
                sub("qrope")
            QL, bQL = (T1, b_t1) if prm else (qlat_s, [b_qlat_s])
            for h in range(8):
                ps, bp = nb()
                p0 = 64 * (h % 2)
                MM(ps[:, :N], wukT[p0:p0 + 64, h // 2, :], QN[p0:p0 + 64, h // 2, :N], True, True, [b_wukT, b_t5[h // 2]], [bp])
                CP(ev_eng(), QL[:, h, :N], ps[:, :N], [bp], bQL)
            if prm:
                sub("qlat")
            MIX, bMIX = (T2, b_t2) if prm else (mix_s, b_mix_s)
            if prm:
                def uview(g, a, b_):
                    return uT[:, g, a:b_]

                def tview(i, a, b_):
                    return ptmp[i][0][:, a:b_]
                L = 16 + N
            else:
                def uview(g, a, b_):
                    return uS[:, g, :, a:b_]

                def tview(i, a, b_):
                    return ptmp[i][0][:, 0:NSB * 20].rearrange("p (b t) -> p b t", b=NSB)[:, :, a:b_]
                L = 20
            for g in range(4):
                w = 2 << g
                cur = None
                lo = 0
                sh = 1
                for stp in range(g + 1):
                    lo2 = lo + sh
                    dst_i = stp % 2
                    if cur is None:
                        a0 = uview(g, lo2, L); a1 = uview(g, lo2 - sh, L - sh); bin_ = list(b_t3)
                    else:
                        a0 = tview(cur, lo2, L); a1 = tview(cur, lo2 - sh, L - sh); bin_ = [ptmp[cur][1]]
                    TT("pool", tview(dst_i, lo2, L), a0, a1, ALU.add, bin_, [ptmp[dst_i][1]])
                    cur = dst_i
                    lo = lo2
                    sh *= 2
                if prm and blk == 0:
                    TT("dve", tview(cur, 16, 32), tview(cur, 16, 32), corr[:, g, :], ALU.mult, [ptmp[cur][1], b_corr], [ptmp[cur][1]])
                if prm:
                    pooled = T4[:, g, :N]
                    STT(pooled, tview(cur, 16, L), 1.0 / w, uview(g, 16, L), ALU.mult, ALU.subtract,
                        [ptmp[cur][1]] + b_t3, [b_t4[g]])
                else:
                    pooled = T4[:, g, :N]
                    STT(pooled.rearrange("p (b t) -> p b t", t=4), tview(cur, 16, L), 1.0 / w, uview(g, 16, L),
                        ALU.mult, ALU.subtract, [ptmp[cur][1]] + b_t3, [b_t4[g]])
                ps, bp = nb()
                MM(ps[:, :N], wpool[:, g, :], pooled, True, True, [b_wpool, b_t4[g]], [bp])
                TSC("dve", MIX[:, g, :N], ps[:, :N], V(l, 48, g), None, ALU.mult, ALU.bypass, [bp, b_vecs], [bMIX] if not prm else [b_t2])
            if prm:
                sub("pool")
            if prm:
                if blk == NB - 1:
                    ps, bp = nb()
                    for g in range(4):
                        TR(ps[0:15, 128 * g:128 * g + 128], uT[:, g, N + 1:N + 16], ident32[:], b_t3 + [b_id32], [bp])
                    CP("act", ystage[0:15, 0:512], ps[0:15, :], [bp], [b_yst])
                    STO(pst_p[l], ystage[0:15, 0:512], [b_yst])
                else:
                    CP("dve", hal[:], uT[:, :, N:N + 16], b_t3, [b_hal])
            else:
                for i in range((NSB + 7) // 8):
                    nbt = min(8, NSB - 8 * i)
                    rows = nbt * 15
                    ps, bp = nb()
                    for g in range(4):
                        tmpc = ptmp[g % 2][0][:, 0:rows]; btc = ptmp[g % 2][1]
                        CP("dve", tmpc.rearrange("p (b t) -> p b t", t=15), uS[:, g, 8 * i:8 * i + nbt, 5:20], b_t3, [btc])
                        TR(ps[0:rows, 128 * g:128 * g + 128], tmpc, ident32[:], [btc, b_id32], [bp])
                    CP("act", ystage[0:rows, 0:512], ps[0:rows, :], [bp], [b_yst])
                    STO(pst_s[l, 120 * i:120 * i + rows, :], ystage[0:rows, 0:512], [b_yst])

        def prompt_attn(l, blk):
            N = 512
            nk = 4 * blk + 4
            pti = [0]
            for h in range(8):
                o_ps, bo = PS[4 + h % 2], PSB[4 + h % 2]
                l_ps, bl = PS[6 + h % 2], PSB[6 + h % 2]
                p0 = 32 * (h % 2)
                def emit_S(kt):
                    j = kt - 4 * blk
                    c0 = 128 * j if j > 0 else 0
                    s_ps, bsp = nb()
                    MM(s_ps[:, c0:N], KTl[:, 128 * kt:128 * kt + 128], T1[:, h, c0:N], True, False, [b_ktl] + b_t1, [bsp])
                    MM(s_ps[:, c0:N], KTr[p0:p0 + 32, 128 * kt:128 * kt + 128], T6[p0:p0 + 32, h // 2, c0:N], False, True,
                       [b_ktr, b_t6], [bsp])
                    return s_ps, bsp, c0, j

                pend = emit_S(0)
                for kt in range(nk):
                    s_ps, bsp, c0, j = pend
                    if kt + 1 < nk:
                        pend = emit_S(kt + 1)
                    pi = pti[0] % 4
                    pti[0] += 1
                    pt = T4[:, pi, :]; bpt = b_t4[pi]
                    ACT(pt[:, c0:N], s_ps[:, c0:N], AF.Exp, [bsp], [bpt], scale=SM_SCALE)
                    if j >= 0:
                        TT("pool", pt[:, c0:N], pt[:, c0:N], maskD[:, j, c0:N], ALU.mult, [bpt, b_mask], [bpt])
                    MM(o_ps[:, c0:N], Vt[:, kt, :], pt[:, c0:N], kt == 0, kt == nk - 1, [b_v, bpt], [bo])
                    MM(l_ps[:, c0:N], onesb[:], pt[:, c0:N], kt == 0, kt == nk - 1, [b_ones, bpt], [bl])
                rec = T3[:, h % 2, :N]; brec = b_t3[h % 2]
                RECIP(rec, l_ps[:, :N], [bl], [brec])
                ol, bol = olat[h % 2]
                TT("dve", ol[:, :N], o_ps[:, :N], rec, ALU.mult, [bo, brec], [bol])
                if h % 2 == 1:
                    ps, bp = nb()
                    MM(ps[0:64, :N], wuv[:, 64 * (h - 1):64 * h], olat[0][0][:, :N], True, True, [b_wuv, olat[0][1]], [bp])
                    MM(ps[64:128, :N], wuv[:, 64 * h:64 * h + 64], olat[1][0][:, :N], True, True, [b_wuv, olat[1][1]], [bp])
                    CP("act", T2[:, 4 + h // 2, :N], ps[:, :N], [bp], [b_t2])

        sq = {"issued": 0}

        def gather_ahead(l, k):
            while sq["issued"] <= k and sq["issued"] < NSB * 4:
                b, q = divmod(sq["issued"], 4)
                i = sq["issued"] % 2
                tk, bk = gck[i]
                tr_, br = gkr[i]
                mk.dma("pool", "indirect_dma_start",
                       dict(out=tk[:], out_offset=None, in_=ckvd[l],
                            in_offset=bass.IndirectOffsetOnAxis(ap=idx4[:, q, b:b + 1], axis=0)),
                       outs=[bk], deps=[b_idx4])
                mk.dma("pool", "indirect_dma_start",
                       dict(out=tr_[:], out_offset=None, in_=krd[l],
                            in_offset=bass.IndirectOffsetOnAxis(ap=idx4[:, q, b:b + 1], axis=0)),
                       outs=[br], deps=[b_idx4])
                sq["issued"] += 1

        oacc, b_oacc = T("oacc", [128, 32], F32)
        slotc = [0]

        def sample_attn_batch(l, b):
            CP("dve", qlat_b[:].rearrange("p (t h) -> p h t", h=8), qlat_s[:, :, 4 * b:4 * b + 4], [b_qlat_s], [b_qlat_b])
            CP("dve", qrope_full[:].rearrange("p (t h) -> p h t", h=8), qrope_s[:, :, 4 * b:4 * b + 4], [b_qrope_s], [b_qrope_full])
            o_ps, bo = PS[4], PSB[4]
            memset("dve", lacc[:], 0.0, [b_lacc])
            for q in range(4):
                kq = 4 * b + q
                gather_ahead(l, kq + 1)
                tk, bk = gck[kq % 2]
                tr_, br = gkr[kq % 2]

                def prep(i):
                    grp, g4 = divmod(i, 4)
                    tb = 16 * grp + 4 * g4
                    si = slotc[0] % 2
                    slotc[0] += 1
                    kts, bkts = KTs[si]
                    krs, bkrs = krTs[si]
                    tp, btp = nb()
                    tpb = tp[:].bitcast(BF16)
                    for e_ in range(4):
                        TR(tpb[:, 128 * e_:128 * e_ + 128], tk[:, 128 * (tb + e_):128 * (tb + e_) + 128], identb[:],
                           [bk, b_idb], [btp])
                    CP("dve", kts[:], tpb[:, 0:512], [btp], [bkts])
                    tp2, btp2 = nb()
                    tpb2 = tp2[:].bitcast(BF16)
                    for e2 in range(2):
                        TR(tpb2[0:64, 128 * e2:128 * e2 + 128], tr_[:, 32 * (tb + 2 * e2):32 * (tb + 2 * e2) + 64], identb[:],
                           [br, b_idb], [btp2])
                    CP("act", krs[:].rearrange("p e k -> p (e k)"), tpb2[0:64, 0:256], [btp2], [bkrs])
                    return (kts, bkts, krs, bkrs)

                def scores(i, slots_):
                    grp, g4 = divmod(i, 4)
                    kts, bkts, krs, bkrs = slots_
                    s_ps, bsp = PS[5 + 2 * grp], PSB[5 + 2 * grp]
                    for e_ in range(4):
                        sl = 4 * g4 + e_
                        pr = 32 * (e_ % 2)
                        MM(s_ps[:, 32 * sl:32 * sl + 32], kts[:, 128 * e_:128 * e_ + 128], qlat_b[:], True, False,
                           [bkts, b_qlat_b], [bsp])
                        MM(s_ps[:, 32 * sl:32 * sl + 32], krs[pr:pr + 32, e_ // 2, :], qrope_full[pr:pr + 32, :], False, True,
                           [bkrs, b_qrope_full], [bsp])

                def softmax_part(grp):
                    s_ps, bsp = PS[5 + 2 * grp], PSB[5 + 2 * grp]
                    ptile, bpt = pts[grp]
                    ACT(ptile[:].rearrange("p s q -> p (s q)"), s_ps[:], AF.Exp, [bsp], [bpt], scale=SM_SCALE)
                    mk.op("dve", "tensor_reduce", dict(out=lred[:], in_=ptile[:].rearrange("p s q -> p q s"), axis=AX.X, op=ALU.add),
                          [bpt], [b_lred])
                    TT("dve", lacc[:], lacc[:], lred[:], ALU.add, [b_lacc, b_lred], [b_lacc])

                def pv_part(grp):
                    ptile, bpt = pts[grp]
                    for sl in range(16):
                        tkn = 16 * grp + sl
                        MM(o_ps[:, 0:32], tk[:, 128 * tkn:128 * tkn + 128], ptile[:, sl, :], grp == 0 and sl == 0,
                           grp == 1 and sl == 15, [bk, bpt], [bo])

                pend = prep(0)
                for i in range(8):
                    grp, g4 = divmod(i, 4)
                    cur = pend
                    if i + 1 < 8:
                        pend = prep(i + 1)
                    scores(i, cur)
                    if i == 4:
                        pv_part(0)
                    if g4 == 3:
                        softmax_part(grp)
                pv_part(1)
                if q == 0:
                    CP("dve", oacc[:], o_ps[:, 0:32], [bo], [b_oacc])
                else:
                    TT("dve", oacc[:], oacc[:], o_ps[:, 0:32], ALU.add, [b_oacc, bo], [b_oacc])
                if q < 3:
                    yield (b, q)
            s_ps, bsp = nb()
            MM(s_ps[0:4, 0:32], ckvnT_s[:, 4 * b:4 * b + 4], qlat_b[:], True, False, [b_ckvnT_s, b_qlat_b], [bsp])
            MM(s_ps[0:4, 0:32], krnT_s[0:32, 4 * b:4 * b + 4], qrope_full[0:32, :], False, True, [b_krnT_s, b_qrope_full], [bsp])
            ACT(pnew[:], s_ps[0:4, 0:32], AF.Exp, [bsp], [b_pnew], scale=SM_SCALE)
            TT("dve", pnew[:], pnew[:], smask[:], ALU.mult, [b_pnew, b_smask], [b_pnew])
            CP("dve", pnewb[:], pnew[:], [b_pnew], [b_pnewb])
            tp, btp = nb()
            tpb = tp[:].bitcast(BF16)
            TR(tpb[0:4, 0:128], ckvnT_s[:, 4 * b:4 * b + 4], identb[:], [b_ckvnT_s, b_idb], [btp])
            CP("dve", vnew[:], tpb[0:4, 0:128], [btp], [b_vnew])
            MM(o_ps[:, 0:32], vnew[:], pnewb[:], True, True, [b_vnew, b_pnewb], [bo])
            TT("dve", oacc[:], oacc[:], o_ps[:, 0:32], ALU.add, [b_oacc, bo], [b_oacc])
            l_ps, bl = PS[6], PSB[6]
            MM(l_ps[:, 0:32], ones32[:], lacc[:], True, False, [b_ones32, b_lacc], [bl])
            MM(l_ps[:, 0:32], ones32[0:4, :], pnew[:], False, True, [b_ones32, b_pnew], [bl])
            RECIP(recs[:], l_ps[:, 0:32], [bl], [b_recs])
            TT("dve", olat_s[:, :, 4 * b:4 * b + 4], oacc[:].rearrange("p (t h) -> p h t", h=8),
               recs[:].rearrange("p (t h) -> p h t", h=8), ALU.mult, [b_oacc, b_recs], [b_olat_s])
            yield (b, 3)

        sgen = {"g": None}

        def sample_layer_gen(l):
            for b_ in range(NSB):
                yield from sample_attn_batch(l, b_)

        def sample_fill(n):
            if sgen["g"] is None:
                return
            for _ in range(n):
                try:
                    next(sgen["g"])
                except StopIteration:
                    sgen["g"] = None
                    return

        def mem_kv_prompt(l):
            if True:
                for mt in range(2):
                    LD("sp", xstage[:], memp[128 * mt:128 * mt + 128, :], [b_xst])
                    for half in range(2):
                        ps, bp = nb()
                        for c4 in range(4):
                            c = 4 * half + c4
                            TR(ps[:, 128 * c4:128 * c4 + 128], xstage[:, 128 * c:128 * c + 128], ident32[:], [b_xst, b_id32], [bp])
                        CP("dve", mempT[:, 4 * half:4 * half + 4, 128 * mt:128 * mt + 128],
                           ps[:].rearrange("p (c m) -> p c m", c=4), [bp], b_t1)
            for s in range(2):
                v8, _, bs = snext(("w_mk", l, s))
                for c4 in range(4):
                    ps, bp = nb()
                    for k in range(8):
                        MM(ps[:, 0:256], v8[:, k, 128 * c4:128 * c4 + 128], mempT[:, k, :], k == 0, k == 7, [bs] + b_t1, [bp])
                    CP("act", mkT[:, 4 * s + c4, :], ps[:, 0:256], [bp], [b_mkT])
                for mt in range(2):
                    ps, bp = nb()
                    for k in range(8):
                        MM(ps[:], mempT[:, k, 128 * mt:128 * mt + 128], v8[:, k, :], k == 0, k == 7, [bs] + b_t1, [bp])
                    CP("act", ystage[:, 0:512], ps[:], [bp], [b_yst])
                    STO(mk_p[l, 128 * mt:128 * mt + 128, 512 * s:512 * s + 512], ystage[:, 0:512], [b_yst])
            for s in range(2):
                v8, _, bs = snext(("w_mv", l, s))
                for mt in range(2):
                    ps, bp = nb()
                    for k in range(8):
                        MM(ps[:], mempT[:, k, 128 * mt:128 * mt + 128], v8[:, k, :], k == 0, k == 7, [bs] + b_t1, [bp])
                    CP("act", ystage[:, 0:512], ps[:], [bp], [b_yst])
                    CP("dve", mvp[:, mt, 512 * s:512 * s + 512], ps[:], [bp], [b_mvp])
                    STO(mv_p[l, 128 * mt:128 * mt + 128, 512 * s:512 * s + 512], ystage[:, 0:512], [b_yst])

        def load_small(l):
            for k in range(2):
                src = w_uq[l, 128 * k:128 * k + 128, :].rearrange("p (h d) -> p h d", h=8)
                LD("pool", wuq[:, k, 0:512].rearrange("p (h d) -> p h d", h=8), src[:, :, 0:64], [b_wuq])
                LD("pool", wuq[:, k, 512:768].rearrange("p (h d) -> p h d", h=8), src[:, :, 64:96], [b_wuq])
                rot = wuq[:, k, 768:1024].rearrange("p (h d) -> p h d", h=8)
                LD("pool", rot[:, :, 0:16], src[:, :, 80:96], [b_wuq])
                LD("pool", rot[:, :, 16:32], src[:, :, 64:80], [b_wuq])
            LD("pool", wuk[:], w_uk[l], [b_wuk])
            LD("pool", wuv[:], w_uv[l], [b_wuv])
            LD("pool", wpool[:], w_pool[l].rearrange("g c d -> c g d"), [b_wpool])
            for j in range(4):
                tp, btp = nb()
                tpb = tp[:].bitcast(BF16)
                TR(tpb[:, 0:128], wuk[:, 128 * j:128 * j + 128], identb[:], [b_wuk, b_idb], [btp])
                CP("dve", wukT[:, j, :], tpb[:, 0:128], [btp], [b_wukT])

        def phaseB(l, prm):
            N = 512 if prm else TS
            X, bX, XB, bXB = (xres, b_xres, xbf, b_xbf) if prm else (xres_s, b_xres_s, xbf_s, b_xbf_s)
            MIX, bMIX = (T2, [b_t2]) if prm else (mix_s, [b_mix_s])
            QM, bQM = (T1, b_t1) if prm else (qlat_s, [b_qlat_s])
            if not prm:
                for hp in range(4):
                    ps, bp = nb()
                    MM(ps[0:64, :N], wuv[:, 128 * hp:128 * hp + 64], olat_s[:, 2 * hp, :], True, True, [b_wuv, b_olat_s], [bp])
                    MM(ps[64:128, :N], wuv[:, 128 * hp + 64:128 * hp + 128], olat_s[:, 2 * hp + 1, :], True, True, [b_wuv, b_olat_s], [bp])
                    CP("act", mix_s[:, 4 + hp, :], ps[:, :N], [bp], [b_mix_s])
            proj_residual("w_o", l, MIX, bMIX, X, bX, N)
            layer_norm(l, 0, X, bX, XB, bXB, N, FILL if prm else 0)
            for s in range(2):
                v8, _, bs = snext(("w_mq", l, s))
                for c4 in range(4):
                    ps, bp = nb()
                    for k in range(8):
                        MM(ps[:, :N], v8[:, k, 128 * c4:128 * c4 + 128], XB[:, k, :N], k == 0, k == 7, [bs, bXB], [bp])
                    CP(ev_eng(), QM[:, 4 * s + c4, :N], ps[:, :N], [bp], bQM)
            if prm:
                mem_attn(QM, bQM, T2, [b_t2], 512, 0, mkT, b_mkT, mvp, b_mvp)
            else:
                for b in range(NSB):
                    LD("pool", memk_sb[:], memk[l, b].rearrange("(t p) n -> p t n", p=128), [b_memk])
                    LD("pool", memv_sb[:], memv[l, b].rearrange("(t p) n -> p t n", p=128), [b_memv])
                    for c in range(8):
                        tp, btp = nb()
                        tpb = tp[:].bitcast(BF16)
                        for mt in range(2):
                            TR(tpb[:, 128 * mt:128 * mt + 128], memk_sb[:, mt, 128 * c:128 * c + 128], identb[:], [b_memk, b_idb], [btp])
                        CP(ev_eng(), mkT_s[:, c, :], tpb[:, 0:256], [btp], [b_mkT_s])
                    mem_attn(QM, bQM, mix_s, [b_mix_s], 4, 4 * b, mkT_s, b_mkT_s, memv_sb, b_memv)
            proj_residual("w_mo", l, MIX, bMIX, X, bX, N)
            layer_norm(l, 1, X, bX, XB, bXB, N, FILL if prm else 0)
            mlp(l, XB, bXB, X, bX, N)
            layer_norm(l, 2, X, bX, XB, bXB, N, FILL if prm else 0)

        def prompt_load(l, blk):
            tok0 = 512 * blk
            if l == 0:
                for tt in range(4):
                    LD("sp", xstage[:], xp[tok0 + 128 * tt:tok0 + 128 * tt + 128, :], [b_xst])
                    for half in range(2):
                        ps, bp = nb()
                        for c4 in range(4):
                            c = 4 * half + c4
                            TR(ps[:, 128 * c4:128 * c4 + 128], xstage[:, 128 * c:128 * c + 128], ident32[:], [b_xst, b_id32], [bp])
                        CP("dve", xres[:, 4 * half:4 * half + 4, 128 * tt:128 * tt + 128], ps[:].rearrange("p (c m) -> p c m", c=4), [bp], [b_xres])
                        CP("dve", xbf[:, 4 * half:4 * half + 4, 128 * tt:128 * tt + 128], ps[:].rearrange("p (c m) -> p c m", c=4), [bp], [b_xbf])
            else:
                LD("sp", xres[:], scr[:, :, tok0:tok0 + 512], [b_xres])
                CP("dve", xbf[:], xres[:], [b_xres], [b_xbf])

        def prompt_store(l, blk):
            tok0 = 512 * blk
            if l == 0:
                mk.dma("sp", "dma_start", dict(out=scr[:, :, tok0:tok0 + 512], in_=xres[:]), ins=[b_xres])
            else:
                for tt in range(4):
                    for half in range(2):
                        ps, bp = nb()
                        for c4 in range(4):
                            c = 4 * half + c4
                            TR(ps[:, 128 * c4:128 * c4 + 128], xres[:, c, 128 * tt:128 * tt + 128], ident32[:], [b_xres, b_id32], [bp])
                        CP(ev_eng(), ystage[:, 512 * half:512 * half + 512], ps[:], [bp], [b_yst])
                    STO(y_p[tok0 + 128 * tt:tok0 + 128 * tt + 128, :], ystage[:], [b_yst])

        def sample_load():
            LD("sp", xstage[0:TS, :], xs, [b_xst])
            for half in range(2):
                ps, bp = nb()
                for c4 in range(4):
                    c = 4 * half + c4
                    TR(ps[:, TS * c4:TS * c4 + TS], xstage[0:TS, 128 * c:128 * c + 128], ident32[0:TS, 0:TS], [b_xst, b_id32], [bp])
                CP("act", xres_s[:, 4 * half:4 * half + 4, :], ps[:, 0:4 * TS].rearrange("p (c m) -> p c m", c=4), [bp], [b_xres_s])
                CP("dve", xbf_s[:, 4 * half:4 * half + 4, :], ps[:, 0:4 * TS].rearrange("p (c m) -> p c m", c=4), [bp], [b_xbf_s])

        def sample_store():
            for half in range(2):
                ps, bp = nb()
                for c4 in range(4):
                    c = 4 * half + c4
                    TR(ps[0:TS, 128 * c4:128 * c4 + 128], xres_s[:, c, :], ident32[:], [b_xres_s, b_id32], [bp])
                CP(ev_eng(), ystage[0:TS, 512 * half:512 * half + 512], ps[0:TS, :], [bp], [b_yst])
            STO(y_s, ystage[0:TS, :], [b_yst])

        import os
        KSTOP = int(os.environ.get("KSTOP", "100000"))
        stg = [0]

        def stage(name):
            stg[0] += 1
            if stg[0] >= KSTOP:
                print("STOP at stage", stg[0], name)
                raise _Stop()

        try:
            sample_load()
            stage("sample_load")
            for l in range(2):
                load_small(l)
                stage("load_small")
                mem_kv_prompt(l)
                stage("mem_kv")
                sq["issued"] = 0
                gather_ahead(l, 0)
                phaseA(l, False)
                stage("phaseA_s")
                sgen["g"] = sample_layer_gen(l)
                for blk in range(NB):
                    prompt_load(l, blk)
                    stage("prompt_load")
                    phaseA(l, True, blk)
                    stage("phaseA_p")
                    prompt_attn(l, blk)
                    stage("attn_p")
                    phaseB(l, True)
                    stage("phaseB_p")
                    prompt_store(l, blk)
                    stage("store_p")
                    stage("sample_attn")
                sample_fill(10 ** 6)
                phaseB(l, False)
                stage("phaseB_s")
            sample_store()
            assert ptr["c"] == len(plan), (ptr["c"], len(plan))
        except _Stop:
            pass
        mk.finish()
        mk.emit()
    return nc


def _consts(S, NSB, past_len):
    TS = 4 * NSB
    half = 16
    inv = (np.float32(10000.0) ** (-(np.arange(half, dtype=np.float32) / np.float32(half)))).astype(np.float32)

    def tables(pos):
        ang = pos.astype(np.float32)[:, None] * inv[None, :]
        c, s = np.cos(ang).astype(np.float32), np.sin(ang).astype(np.float32)
        cos32 = np.concatenate([c, c], axis=1).T
        sin32 = np.concatenate([-s, s], axis=1).T
        return np.tile(cos32, (4, 1)).copy(), np.tile(sin32, (4, 1)).copy()
    cosP, sinP = tables(np.arange(S))
    cs4, sn4 = tables(past_len + np.arange(4))
    cosS = np.tile(cs4, (1, NSB)).copy()
    sinS = np.tile(sn4, (1, NSB)).copy()
    p = np.arange(128)[:, None, None]
    j = np.arange(4)[None, :, None]
    n = np.arange(512)[None, None, :]
    maskD = (n >= 128 * j + p).astype(np.float32).reshape(128, 2048)
    jj = np.arange(4)[:, None]
    tt = (np.arange(32) // 8)[None, :]
    smask = (jj <= tt).astype(np.float32)
    corr = np.zeros((128, 4, 16), np.float32)
    for g in range(4):
        w = 2 << g
        corr[:, g, :] = (w / np.minimum(w, np.arange(16) + 1.0))[None, :]
    qoff = np.zeros((128, 4, NSB), np.int32)
    for q in range(4):
        qoff[:, q, :] = q
    return dict(ident=np.eye(128, dtype=np.float32), cosP=cosP, sinP=sinP, cosS=cosS, sinS=sinS, maskD=maskD,
                smask=smask, corr=corr.reshape(128, 64), qoff=qoff.reshape(128, 4 * NSB))


def _vecs(inp):
    v = np.zeros((128, 110), np.float32)
    for l in range(2):
        b = 55 * l
        for i, nm in enumerate(["ln1_g", "ln1_b", "ln2_g", "ln2_b", "ln3_g", "ln3_b"]):
            v[:, b + 8 * i:b + 8 * i + 8] = np.asarray(inp[nm][l]).reshape(8, 128).T
        v[:, b + 48:b + 52] = np.asarray(inp["pool_scale"][l]).reshape(4, 128).T
        v[:, b + 52:b + 54] = np.asarray(inp["g_q"][l]).reshape(2, 128).T
        v[:, b + 54] = np.asarray(inp["g_kv"][l])
    return v


_NC_CACHE = {}


def make_in_maps(inp, n_cores):
    xpr = np.asarray(inp["x_prompt"]); xsm = np.asarray(inp["x_sample"])
    B, S, _ = xpr.shape
    DB = xsm.shape[0]
    NSB = DB // n_cores
    pt = np.asarray(inp["page_table"])
    n_pages = pt.shape[1]
    assert n_pages == 128 and B == n_cores
    cck = np.asarray(inp["cache_ckv"]); ckr = np.asarray(inp["cache_krope"])
    NPHYS = cck.shape[1]
    cst = _consts(S, NSB, n_pages * 128)
    vec = _vecs(inp)
    f = lambda k: np.ascontiguousarray(np.asarray(inp[k]), dtype=np.float32)
    shared = dict(
        ckv0=cck[0].reshape(NPHYS * 4, 4096), ckv1=cck[1].reshape(NPHYS * 4, 4096),
        kr0=ckr[0].reshape(NPHYS * 4, 1024), kr1=ckr[1].reshape(NPHYS * 4, 1024),
        w_in=f("w_in"), w_uq=f("w_uq"), w_uk=f("w_uk").reshape(2, 128, 512), w_uv=f("w_uv").reshape(2, 128, 512),
        w_pool=f("w_pool"), w_o=f("w_o"), w_mq=f("w_mq"), w_mk=f("w_mk"), w_mv=f("w_mv"), w_mo=f("w_mo"),
        w1=f("w1"), w2=f("w2"), vecs=vec, **cst)
    mk_ = np.asarray(inp["cache_mem_k"]); mv_ = np.asarray(inp["cache_mem_v"]); sp_ = np.asarray(inp["state_pool"])
    mp_ = np.asarray(inp["mem_prompt"])
    maps = []
    for c in range(n_cores):
        sl = slice(NSB * c, NSB * c + NSB)
        m = dict(shared)
        m.update(
            xp=xpr[c], xs=xsm[sl].reshape(NSB * 4, 1024),
            memk=mk_[:, sl].reshape(2, NSB, 256, 1024), memv=mv_[:, sl].reshape(2, NSB, 256, 1024),
            spool=sp_[:, sl].reshape(2, NSB * 15, 512),
            ptT=np.ascontiguousarray(pt[sl].T.astype(np.int32)), memp=mp_[c])
        maps.append(m)
    return maps, (S, NSB, NPHYS)


def assemble(results, n_cores, S, NSB):
    DB = NSB * n_cores
    y_p = np.stack([r["y_p"] for r in results])
    y_s = np.concatenate([r["y_s"].reshape(NSB, 4, 1024) for r in results])
    ckv_p = np.stack([r["ckv_p"] for r in results], axis=1)
    kr_p = np.stack([r["kr_p"] for r in results], axis=1)
    mk_p = np.stack([r["mk_p"].reshape(2, 256, 4, 256) for r in results], axis=1)
    mv_p = np.stack([r["mv_p"].reshape(2, 256, 4, 256) for r in results], axis=1)
    pst_p = np.stack([r["pst_p"] for r in results], axis=1)
    ckv_s = np.concatenate([r["ckv_s"].reshape(2, NSB, 4, 128) for r in results], axis=1)
    kr_s = np.concatenate([r["kr_s"].reshape(2, NSB, 4, 32) for r in results], axis=1)
    pst_s = np.concatenate([r["pst_s"].reshape(2, NSB, 15, 512) for r in results], axis=1)
    return tuple(np.ascontiguousarray(a, dtype=np.float32) for a in
                 (y_p, y_s, ckv_p, kr_p, mk_p, mv_p, pst_p, ckv_s, kr_s, pst_s))


def kernel(**inputs):
    maps, (S, NSB, NPHYS) = make_in_maps(inputs, NCORES)
    key = (S, NSB, NPHYS)
    if key not in _NC_CACHE:
        _NC_CACHE[key] = build(S, NSB, NPHYS)
    nc = _NC_CACHE[key]
    res = run_bass_kernel_spmd(nc, maps, core_ids=list(range(NCORES)))
    return assemble(res.results, NCORES, S, NSB)
```

```python
import numpy as np
from contextlib import ExitStack
import concourse.bass as bass
import concourse.mybir as mybir
from concourse.bass_utils import run_bass_kernel_spmd

F32 = mybir.dt.float32
BF16 = mybir.dt.bfloat16
I32 = mybir.dt.int32
AF = mybir.ActivationFunctionType
ALU = mybir.AluOpType
AX = mybir.AxisListType

ALPHA = 4.0 ** 0.25
SM_SCALE = 96.0 ** -0.5
LN_EPS = 1e-5
RMS_EPS = 1e-6
NCORES = 8


class Buf:
    __slots__ = ("name", "w", "r", "dsem", "dcount", "excl")

    def __init__(self, name, excl=False):
        self.name = name
        self.excl = excl
        self.w = None
        self.r = {}
        self.dsem = None
        self.dcount = 0


class Eng:
    def __init__(self, name, eng, sem):
        self.name, self.eng, self.sem = name, eng, sem
        self.count = 0
        self.seen = {}
        self.thunks = []


class MK:
    def __init__(self, nc, stack):
        self.nc, self.stack = nc, stack
        self.engs = {}
        for name, eng in (("pe", nc.tensor), ("act", nc.scalar), ("dve", nc.vector),
                          ("pool", nc.gpsimd), ("sp", nc.sync)):
            sem = stack.enter_context(nc.semaphore("sem_" + name))
            self.engs[name] = Eng(name, eng, sem)
        self.out_dmas = []
        self.nb = 0

    def sbuf(self, name, shape, dtype):
        return self.stack.enter_context(self.nc.sbuf_tensor(name, shape, dtype))

    def psum(self, name, shape, dtype):
        return self.stack.enter_context(self.nc.psum_tensor(name, shape, dtype))

    def buf(self, name, excl=False):
        self.nb += 1
        return Buf("%s_%d" % (name, self.nb), excl)

    def _wait(self, e, key, sem, val):
        if e.seen.get(key, 0) >= val:
            return
        e.seen[key] = val
        eng = e.eng
        e.thunks.append(lambda: eng.wait_ge(sem, val))

    def _wait_on(self, e, who, cnt, b):
        if who == "dma":
            self._wait(e, ("d", id(b)), b.dsem, cnt)
        else:
            if who == e.name and e.name in ("pe", "sp"):
                return
            self._wait(e, who, self.engs[who].sem, cnt)

    def _deps(self, e, ins, outs):
        for b in ins:
            if b.w is not None:
                self._wait_on(e, b.w[0], b.w[1], b)
            if b.excl:
                for who, cnt in b.r.items():
                    if who != e.name:
                        self._wait_on(e, who, cnt, b)
        for b in outs:
            if b.w is not None:
                self._wait_on(e, b.w[0], b.w[1], b)
            for who, cnt in b.r.items():
                self._wait_on(e, who, cnt, b)

    def op(self, ename, meth, kw, ins=(), outs=()):
        e = self.engs[ename]
        self._deps(e, ins, outs)
        e.count += 1
        c = e.count
        eng, sem = e.eng, e.sem
        e.thunks.append(lambda: getattr(eng, meth)(**kw).then_inc(sem, 1))
        for b in ins:
            b.r[ename] = c
        for b in outs:
            b.w = (ename, c)
            b.r = {}

    def dma(self, qname, meth, kw, ins=(), outs=(), deps=(), is_output=False):
        e = self.engs[qname]
        self._deps(e, list(ins) + list(deps), outs)
        eng = e.eng
        main = outs[0] if outs else ins[0]
        if main.dsem is None:
            main.dsem = self.stack.enter_context(self.nc.semaphore("d_" + main.name))
        sem = main.dsem
        main.dcount += 16
        cnt = main.dcount
        e.thunks.append(lambda: getattr(eng, meth)(**kw).then_inc(sem, 16))
        for b in outs:
            assert b is main
            b.w = ("dma", cnt)
            b.r = {}
        for b in ins:
            assert b is main
            b.r["dma"] = cnt
        if is_output:
            self.out_dmas.append((main, cnt))

    def finish(self):
        e = self.engs["sp"]
        for b, cnt in self.out_dmas:
            self._wait(e, ("d", id(b)), b.dsem, cnt)
        for name in ("pe", "act", "dve", "pool"):
            o = self.engs[name]
            if o.count:
                self._wait(e, name, o.sem, o.count)

    def emit(self):
        with self.nc.Block() as block:
            @block.tensor
            def _(x):
                for t in self.engs["pe"].thunks:
                    t()

            @block.scalar
            def _(x):
                for t in self.engs["act"].thunks:
                    t()

            @block.vector
            def _(x):
                for t in self.engs["dve"].thunks:
                    t()

            @block.gpsimd
            def _(x):
                for t in self.engs["pool"].thunks:
                    t()

            @block.sync
            def _(x):
                for t in self.engs["sp"].thunks:
                    t()


def build(S, NSB, NPHYS):
    NB = S // 512
    TS = 4 * NSB
    NT = S // 128
    nc = bass.Bass("TRN2", target_bir_lowering=False)

    def din(name, shape, dt=F32):
        return nc.dram_tensor(name, list(shape), dt, kind="ExternalInput").ap()

    def dout(name, shape):
        return nc.dram_tensor(name, list(shape), F32, kind="ExternalOutput").ap()

    xp = din("xp", [S, 1024]); xs = din("xs", [TS, 1024])
    ckvd = [din("ckv%d" % l, [NPHYS * 4, 4096]) for l in range(2)]
    krd = [din("kr%d" % l, [NPHYS * 4, 1024]) for l in range(2)]
    memk = din("memk", [2, NSB, 256, 1024]); memv = din("memv", [2, NSB, 256, 1024])
    spool = din("spool", [2, NSB * 15, 512])
    ptT = din("ptT", [128, NSB], I32); qoff = din("qoff", [128, 4 * NSB], I32)
    memp = din("memp", [256, 1024])
    w_in = din("w_in", [2, 1024, 928]); w_uq = din("w_uq", [2, 256, 768])
    w_uk = din("w_uk", [2, 128, 512]); w_uv = din("w_uv", [2, 128, 512])
    w_pool = din("w_pool", [2, 4, 128, 128]); w_o = din("w_o", [2, 1024, 1024])
    w_mq = din("w_mq", [2, 1024, 1024]); w_mk = din("w_mk", [2, 1024, 1024])
    w_mv = din("w_mv", [2, 1024, 1024]); w_mo = din("w_mo", [2, 1024, 1024])
    w1 = din("w1", [2, 1024, 4096]); w2 = din("w2", [2, 4096, 1024])
    vecs_d = din("vecs", [128, 110])
    ident_d = din("ident", [128, 128])
    cosP = din("cosP", [128, S]); sinP = din("sinP", [128, S])
    cosS_d = din("cosS", [128, TS]); sinS_d = din("sinS", [128, TS])
    maskD_d = din("maskD", [128, 4 * 512]); smask_d = din("smask", [4, 32]); corr_d = din("corr", [128, 64])

    y_p = dout("y_p", [S, 1024]); y_s = dout("y_s", [TS, 1024])
    ckv_p = dout("ckv_p", [2, S, 128]); kr_p = dout("kr_p", [2, S, 32])
    mk_p = dout("mk_p", [2, 256, 1024]); mv_p = dout("mv_p", [2, 256, 1024])
    pst_p = dout("pst_p", [2, 15, 512])
    ckv_s = dout("ckv_s", [2, TS, 128]); kr_s = dout("kr_s", [2, TS, 32])
    pst_s = dout("pst_s", [2, NSB * 15, 512])
    scr = nc.dram_tensor("scr_x1", [128, 8, S], F32, kind="Internal").ap()

    st = ExitStack()
    with st:
        mk = MK(nc, st)

        def T(name, shape, dt):
            return mk.sbuf("sb_" + name, list(shape), dt), mk.buf(name)

        import os
        KSUB = int(os.environ.get("KSUB", "100000"))
        subc = [0]

        class _Stop(Exception):
            pass

        def sub(name):
            subc[0] += 1
            if subc[0] >= KSUB:
                print("SUBSTOP", subc[0], name)
                raise _Stop()

        PS = [mk.psum("ps%d" % i, [128, 512], F32) for i in range(8)]
        PSB = [mk.buf("ps%d" % i, True) for i in range(8)]
        rot = [0]

        def nb():
            i = rot[0]
            rot[0] = (i + 1) % 4
            return PS[i], PSB[i]

        def MM(out, lhsT, rhs, start, stop, ins, outs):
            mk.op("pe", "matmul", dict(out=out, lhsT=lhsT, rhs=rhs, start=start, stop=stop), ins, outs)

        def TR(out, in_, ident, ins, outs):
            mk.op("pe", "transpose", dict(out=out, in_=in_, identity=ident), ins, outs)

        def ACT(out, in_, func, ins, outs, **kw):
            mk.op("act", "activation", dict(out=out, in_=in_, func=func, **kw), ins, outs)

        def TT(eng, out, in0, in1, op, ins, outs):
            mk.op(eng, "tensor_tensor", dict(out=out, in0=in0, in1=in1, op=op), ins, outs)

        def TSC(eng, out, in0, s1, s2, op0, op1, ins, outs):
            mk.op(eng, "tensor_scalar", dict(out=out, in0=in0, scalar1=s1, scalar2=s2, op0=op0, op1=op1), ins, outs)

        def STT(out, in0, scalar, in1, op0, op1, ins, outs):
            mk.op("dve", "scalar_tensor_tensor", dict(out=out, in0=in0, scalar=scalar, in1=in1, op0=op0, op1=op1), ins, outs)

        def RECIP(out, in_, ins, outs):
            mk.op("dve", "reciprocal", dict(out=out, in_=in_), ins, outs)

        def CP(eng, out, in_, ins, outs):
            if eng == "act":
                ACT(out, in_, AF.Copy, ins, outs)
            else:
                mk.op(eng, "tensor_copy", dict(out=out, in_=in_), ins, outs)

        def memset(eng, ap, val, outs):
            e = mk.engs[eng]
            mk._deps(e, (), outs)
            e.count += 1
            c = e.count
            en, sem = e.eng, e.sem
            e.thunks.append(lambda: en.memset(ap, val).then_inc(sem, 1))
            for b in outs:
                b.w = (eng, c)
                b.r = {}

        def LD(q, out, in_, outs, deps=()):
            mk.dma(q, "dma_start", dict(out=out, in_=in_), outs=outs, deps=deps)

        def STO(out, in_, ins):
            mk.dma("sp", "dma_start", dict(out=out, in_=in_), ins=ins, is_output=True)

        evt = [0]

        def ev_eng():
            evt[0] ^= 1
            return "act" if evt[0] else "dve"

        ident32, b_id32 = T("ident32", [128, 128], F32)
        identb, b_idb = T("identb", [128, 128], BF16)
        onesb, b_ones = T("onesb", [128, 128], BF16)
        ones32, b_ones32 = T("ones32", [128, 128], F32)
        maskD, b_mask = T("maskD", [128, 4, 512], BF16)
        smask, b_smask = T("smask", [4, 32], F32)
        corr, b_corr = T("corr", [128, 4, 16], F32)
        cosS, b_cosS = T("cosS", [128, TS], F32)
        sinS, b_sinS = T("sinS", [128, TS], F32)
        vecs, b_vecs = T("vecs", [128, 110], F32)
        pt_sb, b_pt = T("pt_sb", [128, NSB], I32)
        qoff_sb, b_qoff = T("qoff_sb", [128, 4, NSB], I32)
        idx4, b_idx4 = T("idx4", [128, 4, NSB], I32)
        pt2, b_pt2 = T("pt2", [128, NSB], I32)

        LD("sp", ident32[:], ident_d, [b_id32])
        LD("pool", identb[:], ident_d, [b_idb])
        memset("dve", onesb[:], 1.0, [b_ones])
        memset("dve", ones32[:], 1.0, [b_ones32])
        LD("pool", maskD[:], maskD_d.rearrange("p (j n) -> p j n", j=4), [b_mask])
        LD("sp", smask[:], smask_d, [b_smask])
        LD("sp", corr[:], corr_d.rearrange("p (g t) -> p g t", g=4), [b_corr])
        LD("sp", cosS[:], cosS_d, [b_cosS])
        LD("sp", sinS[:], sinS_d, [b_sinS])
        LD("sp", vecs[:], vecs_d, [b_vecs])
        LD("sp", pt_sb[:], ptT, [b_pt])
        LD("sp", qoff_sb[:], qoff.rearrange("p (q b) -> p q b", q=4), [b_qoff])
        TT("pool", pt2[:], pt_sb[:], pt_sb[:], ALU.add, [b_pt], [b_pt2])
        TT("pool", pt_sb[:], pt2[:], pt2[:], ALU.add, [b_pt2], [b_pt])
        for q in range(4):
            TT("pool", idx4[:, q, :], pt_sb[:], qoff_sb[:, q, :], ALU.add, [b_pt, b_qoff], [b_idx4])

        def V(l, a, b=None):
            c0 = 55 * l + a
            return vecs[:, c0:c0 + 1] if b is None else vecs[:, c0 + b:c0 + b + 1]

        KTl, b_ktl = T("KTl", [128, S], BF16)
        KTr, b_ktr = T("KTr", [64, S], BF16)
        Vt, b_v = T("Vt", [128, NT, 128], BF16)
        wuq, b_wuq = T("wuq", [128, 2, 1024], BF16)
        wuk, b_wuk = T("wuk", [128, 512], BF16)
        wukT, b_wukT = T("wukT", [128, 4, 128], BF16)
        wuv, b_wuv = T("wuv", [128, 512], BF16)
        wpool, b_wpool = T("wpool", [128, 4, 128], BF16)
        mkT, b_mkT = T("mkT", [128, 8, 256], BF16)
        mvp, b_mvp = T("mvp", [128, 2, 1024], BF16)
        hal, b_hal = T("hal", [128, 4, 16], F32)

        NSLOT = 3
        FILL = 1
        slots = [T("slab%d" % i, [128, 4096], BF16) for i in range(NSLOT)]
        gck = [T("gck%d" % i, [128, 4096], BF16) for i in range(2)]
        gkr = [T("gkr%d" % i, [128, 1024], BF16) for i in range(2)]

        xres, b_xres = T("xres", [128, 8, 512], F32)
        xbf, b_xbf = T("xbf", [128, 8, 512], BF16)
        T1, _ = T("T1", [128, 8, 512], BF16); b_t1 = [mk.buf("t1a"), mk.buf("t1b")]
        mempT = T1[:, :, 0:256]
        T2, b_t2 = T("T2", [128, 8, 512], BF16)
        T3, _ = T("T3", [128, 4, 528], F32); b_t3 = [mk.buf("t3_%d" % i) for i in range(4)]
        T4, _ = T("T4", [128, 4, 512], BF16); b_t4 = [mk.buf("t4_%d" % i) for i in range(4)]
        T5, _ = T("T5", [128, 4, 512], BF16); b_t5 = [mk.buf("t5_%d" % i) for i in range(4)]
        T6, b_t6 = T("T6", [64, 4, 512], BF16)
        T7, _ = T("T7", [128, 4, 512], F32); b_t7 = [mk.buf("t7_%d" % i) for i in range(4)]
        cqn, b_cqn = T("cqn", [128, 2, 512], BF16)
        cosb, b_cos = T("cosb", [128, 512], F32)
        sinb, b_sin = T("sinb", [128, 512], F32)
        ptmp = [T("ptmp%d" % i, [128, 528], F32) for i in range(2)]
        olat = [T("olat%d" % i, [128, 512], BF16) for i in range(2)]
        ystage, b_yst = T("ystage", [128, 1024], F32)
        xstage, b_xst = ystage, b_yst

        xres_s, b_xres_s = T("xres_s", [128, 8, TS], F32)
        xbf_s, b_xbf_s = T("xbf_s", [128, 8, TS], BF16)
        qlat_s, b_qlat_s = T("qlat_s", [128, 8, TS], BF16)
        qrope_s, b_qrope_s = T("qrope_s", [64, 8, TS], BF16)
        qrope_full, b_qrope_full = T("qrope_full", [64, 32], BF16)
        ckvnT_s, b_ckvnT_s = T("ckvnT_s", [128, TS], BF16)
        krnT_s, b_krnT_s = T("krnT_s", [64, TS], BF16)
        mix_s, b_mix_s = T("mix_s", [128, 8, TS], BF16)
        olat_s, b_olat_s = T("olat_s", [128, 8, TS], BF16)
        KTs = [T("KTs%d" % i, [128, 512], BF16) for i in range(2)]
        krTs = [T("krTs%d" % i, [64, 2, 128], BF16) for i in range(2)]
        pts = [T("pts%d" % i, [128, 16, 32], BF16) for i in range(2)]
        qlat_b, b_qlat_b = T("qlat_b", [128, 32], BF16)
        qrope_b, b_qrope_b = T("qrope_b", [64, 32], BF16)
        lacc, b_lacc = T("lacc", [128, 32], F32)
        lred, b_lred = T("lred", [128, 32], F32)
        vnew, b_vnew = T("vnew", [4, 128], BF16)
        pnew, b_pnew = T("pnew", [4, 32], F32)
        pnewb, b_pnewb = T("pnewb", [4, 32], BF16)
        recs, b_recs = T("recs", [128, 32], F32)
        memk_sb, b_memk = T("memk_sb", [128, 2, 1024], BF16)
        memv_sb, b_memv = T("memv_sb", [128, 2, 1024], BF16)
        mkT_s, b_mkT_s = T2[:, :, 0:256], b_t2

        sstate = {"i": 0}

        def slab_load(key):
            kind, l, j = key[0], key[1], key[2]
            i = sstate["i"] % NSLOT
            sstate["i"] += 1
            t, b = slots[i]
            v8 = t[:].rearrange("p (k n) -> p k n", k=8)
            v4 = t[:].rearrange("p (k n) -> p k n", k=4)
            if kind == "w_in":
                src = w_in[l].rearrange("(k p) n -> p k n", p=128)
                if j == 0:
                    LD("pool", v8[:, :, 0:512], src[:, :, 0:512], [b])
                else:
                    LD("pool", v8[:, :, 0:416], src[:, :, 512:928], [b])
                    LD("pool", v8[:, :, 416:432], src[:, :, 912:928], [b])
                    LD("pool", v8[:, :, 432:448], src[:, :, 896:912], [b])
            elif kind == "w1":
                src = w1[l].rearrange("(k p) n -> p k n", p=128)
                LD("pool", v8, src[:, :, 512 * j:512 * j + 512], [b])
            elif kind == "w2":
                src = w2[l][512 * j:512 * j + 512, :].rearrange("(k p) n -> p k n", p=128)
                LD("pool", v4, src, [b])
            else:
                wd = {"w_o": w_o, "w_mq": w_mq, "w_mk": w_mk, "w_mv": w_mv, "w_mo": w_mo}[kind]
                src = wd[l].rearrange("(k p) n -> p k n", p=128)
                LD("pool", v8, src[:, :, 512 * j:512 * j + 512], [b])
            return (v8, v4, b)

        plan = []
        loaded = []
        ptr = {"c": 0, "l": 0}

        def snext(key):
            c = ptr["c"]
            assert plan[c] == key, (plan[c], key)
            while ptr["l"] < len(plan) and ptr["l"] < c + NSLOT:
                loaded.append(slab_load(plan[ptr["l"]]))
                ptr["l"] += 1
            ptr["c"] += 1
            return loaded[c]

        def unit_keys(l):
            ks = [("w_in", l, 0), ("w_in", l, 1), ("w_o", l, 0), ("w_o", l, 1), ("w_mq", l, 0), ("w_mq", l, 1),
                  ("w_mo", l, 0), ("w_mo", l, 1)]
            ks += [("w1", l, 0)]
            for j in range(8):
                if j + 1 < 8:
                    ks += [("w1", l, j + 1)]
                ks += [("w2", l, j)]
            return ks

        for l in range(2):
            plan += [("w_mk", l, 0), ("w_mk", l, 1), ("w_mv", l, 0), ("w_mv", l, 1)]
            plan += [("w_in", l, 0), ("w_in", l, 1)]
            for blk in range(NB):
                plan += unit_keys(l)
            plan += unit_keys(l)[2:]

        def layer_norm(l, which, X, bX, XB, bXB, N, fill=0):
            gcol, bcol = 16 * which, 16 * which + 8
            sum_ps, bsum = PS[4], PSB[4]
            sq_ps, bsq = PS[6], PSB[6]
            for c in range(8):
                zb = T5[:, (2 * c) % 4, :N]; bz = b_t5[(2 * c) % 4]
                zs = T5[:, (2 * c + 1) % 4, :N]; bs = b_t5[(2 * c + 1) % 4]
                ACT(zb, X[:, c, :N], AF.Copy, [bX], [bz])
                ACT(zs, X[:, c, :N], AF.Square, [bX], [bs])
                MM(sum_ps[:, :N], onesb[:], zb, c == 0, c == 7, [b_ones, bz], [bsum])
                MM(sq_ps[:, :N], onesb[:], zs, c == 0, c == 7, [b_ones, bs], [bsq])
            m = T3[:, 0, :N]; var = T3[:, 1, :N]
            ACT(m, sum_ps[:, :N], AF.Copy, [bsum], [b_t3[0]], scale=1.0 / 1024)
            TT("dve", var, m, m, ALU.mult, [b_t3[0]], [b_t3[1]])
            STT(var, sq_ps[:, :N], 1.0 / 1024, var, ALU.mult, ALU.subtract, [bsq, b_t3[1]], [b_t3[1]])
            ACT(var, var, AF.Sqrt, [b_t3[1]], [b_t3[1]], bias=LN_EPS, scale=1.0)
            RECIP(var, var, [b_t3[1]], [b_t3[1]])
            for c in range(8):
                t = T3[:, 2 + (c % 2), :N]; bt = b_t3[2 + (c % 2)]
                TT("dve", t, X[:, c, :N], m, ALU.subtract, [bX, b_t3[0]], [bt])
                TT("dve", t, t, var, ALU.mult, [bt, b_t3[1]], [bt])
                ACT(X[:, c, :N], t, AF.Identity, [bt, b_vecs], [bX], scale=V(l, gcol, c), bias=V(l, bcol, c))
                ACT(XB[:, c, :N], t, AF.Identity, [bt, b_vecs], [bXB], scale=V(l, gcol, c), bias=V(l, bcol, c))
            if fill:
                sample_fill(fill)

        def proj_residual(kind, l, RHS, bR, X, bX, N):
            for s in range(2):
                v8, v4, bs = snext((kind, l, s))
                for c4 in range(4):
                    c = 4 * s + c4
                    ps, bp = nb()
                    for k in range(8):
                        MM(ps[:, :N], v8[:, k, 128 * c4:128 * c4 + 128], RHS[:, k, :N], k == 0, k == 7, [bs] + bR, [bp])
                    STT(X[:, c, :N], X[:, c, :N], ALPHA, ps[:, :N], ALU.mult, ALU.add, [bX, bp], [bX])

        def mem_attn(QM, bQM, OM, bOM, n, c0, mkT_, b_mk_, mv_, b_mv_):
            for h in range(4):
                pm = T4[:, 2 * (h % 2):2 * (h % 2) + 2, :]
                bpm = b_t4[2 * (h % 2):2 * (h % 2) + 2]
                for mt in range(2):
                    ps, bp = nb()
                    for dh in range(2):
                        MM(ps[:, :n], mkT_[:, 2 * h + dh, 128 * mt:128 * mt + 128], QM[:, 2 * h + dh, c0:c0 + n],
                           dh == 0, dh == 1, [b_mk_] + bQM, [bp])
                    ACT(pm[:, mt, :n], ps[:, :n], AF.Exp, [bp], [bpm[mt]], scale=1.0 / 16)
                lps, bl = nb()
                for mt in range(2):
                    MM(lps[:, :n], onesb[:], pm[:, mt, :n], mt == 0, mt == 1, [b_ones, bpm[mt]], [bl])
                rec = T3[:, h % 2, :n]; brec = b_t3[h % 2]
                RECIP(rec, lps[:, :n], [bl], [brec])
                for dh in range(2):
                    ops, bo = nb()
                    for mt in range(2):
                        MM(ops[:, :n], mv_[:, mt, 256 * h + 128 * dh:256 * h + 128 * dh + 128], pm[:, mt, :n],
                           mt == 0, mt == 1, [b_mv_, bpm[mt]], [bo])
                    TT("dve", OM[:, 2 * h + dh, c0:c0 + n], ops[:, :n], rec, ALU.mult, [bo, brec], bOM)

        def mlp(l, XB, bXB, X, bX, N):
            def stage1(j):
                v8, _, b1 = snext(("w1", l, j))
                hs = j % 2
                h1 = T1[:, 4 * hs:4 * hs + 4, :]; bh = b_t1[hs]
                for f in range(4):
                    ps, bp = nb()
                    for k in range(8):
                        MM(ps[:, :N], v8[:, k, 128 * f:128 * f + 128], XB[:, k, :N], k == 0, k == 7, [b1, bXB], [bp])
                    r = T3[:, 2 + (f % 2), :N]; br = b_t3[2 + (f % 2)]
                    ACT(r, ps[:, :N], AF.Relu, [bp], [br])
                    TT("dve", h1[:, f, :N], r, r, ALU.mult, [br], [bh])

            def stage2(j):
                _, v4, b2 = snext(("w2", l, j))
                hs = j % 2
                h1 = T1[:, 4 * hs:4 * hs + 4, :]; bh = b_t1[hs]
                for c in range(8):
                    ps, bp = nb()
                    for f in range(4):
                        MM(ps[:, :N], v4[:, f, 128 * c:128 * c + 128], h1[:, f, :N], f == 0, f == 3, [b2, bh], [bp])
                    if j == 0:
                        STT(X[:, c, :N], X[:, c, :N], ALPHA, ps[:, :N], ALU.mult, ALU.add, [bX, bp], [bX])
                    else:
                        TT("dve", X[:, c, :N], X[:, c, :N], ps[:, :N], ALU.add, [bX, bp], [bX])

            stage1(0)
            for j in range(8):
                if j + 1 < 8:
                    stage1(j + 1)
                stage2(j)

        def phaseA(l, prm, blk=0):
            N = 512 if prm else TS
            tok0 = 512 * blk
            X, bX, XB, bXB = (xres, b_xres, xbf, b_xbf) if prm else (xres_s, b_xres_s, xbf_s, b_xbf_s)
            uT = T3
            if prm:
                LD("sp", cosb[:], cosP[:, tok0:tok0 + 512], [b_cos])
                LD("sp", sinb[:], sinP[:, tok0:tok0 + 512], [b_sin])
                cs, sn, bcs, bsn = cosb, sinb, b_cos, b_sin
                if blk == 0:
                    memset("dve", uT[:, :, 0:16], 0.0, b_t3)
                else:
                    CP("dve", uT[:, :, 0:16], hal[:], [b_hal], b_t3)
            else:
                cs, sn, bcs, bsn = cosS, sinS, b_cosS, b_sinS
                uS = T3[:].rearrange("p g x -> p (g x)")[:, 0:4 * NSB * 20].rearrange("p (g b t) -> p g b t", g=4, b=NSB)
                memset("dve", T3[:], 0.0, b_t3)
                for i in range((NSB + 7) // 8):
                    nbt = min(8, NSB - 8 * i)
                    rows = nbt * 15
                    LD("sp", xstage[0:rows, 0:512], spool[l, 120 * i:120 * i + rows, :], [b_xst])
                    ps, bp = nb()
                    for g in range(4):
                        TR(ps[:, 128 * g:128 * g + rows], xstage[0:rows, 128 * g:128 * g + 128], ident32[0:rows, 0:rows],
                           [b_xst, b_id32], [bp])
                    for g in range(4):
                        CP("dve", uS[:, g, 8 * i:8 * i + nbt, 1:16],
                           ps[:, 128 * g:128 * g + rows].rearrange("p (b t) -> p b t", t=15), [bp], b_t3)
            v8, _, bs = snext(("w_in", l, 0))
            for g in range(4):
                ps, bp = nb()
                for k in range(8):
                    MM(ps[:, :N], v8[:, k, 128 * g:128 * g + 128], XB[:, k, :N], k == 0, k == 7, [bs, bXB], [bp])
                if prm:
                    CP("act", uT[:, g, 16:16 + N], ps[:, :N], [bp], b_t3)
                else:
                    CP("act", uS[:, g, :, 16:20], ps[:, :N].rearrange("p (b t) -> p b t", t=4), [bp], b_t3)
            if prm:
                sub("u")
            v8, _, bs = snext(("w_in", l, 1))
            cq32 = [T7[:, 0, :N], T7[:, 1, :N]]
            st_ps, bst = nb()
            for c in range(2):
                ps, bp = nb()
                for k in range(8):
                    MM(ps[:, :N], v8[:, k, 128 * c:128 * c + 128], XB[:, k, :N], k == 0, k == 7, [bs, bXB], [bp])
                DBG = os.environ.get("DBG", "")
                if "e" not in DBG:
                    CP("dve", cq32[c], ps[:, :N], [bp], [b_t7[c]])
                sq = T5[:, c, :N]
                if "d" not in DBG:
                    ACT(sq, ps[:, :N], AF.Square, [bp], [b_t5[c]])
                if "f" not in DBG:
                    MM(st_ps[:, :N], onesb[:], sq, c == 0, c == 1, [b_ones, b_t5[c]], [bst])
            rq = T7[:, 2, :N]
            DBG = os.environ.get("DBG", "")
            if "a" not in DBG:
                ACT(rq, st_ps[:, :N], AF.Sqrt, [bst], [b_t7[2]], bias=RMS_EPS, scale=1.0 / 256)
            if "b" not in DBG:
                RECIP(rq, rq, [b_t7[2]], [b_t7[2]])
            if "c" not in DBG:
                for c in range(2):
                    STT(cqn[:, c, :N], cq32[c], V(l, 52, c), rq, ALU.mult, ALU.mult, [b_t7[c], b_t7[2], b_vecs], [b_cqn])
            if prm:
                sub("cq")
            ps, bp = nb()
            for k in range(8):
                MM(ps[:, :N], v8[:, k, 256:384], XB[:, k, :N], k == 0, k == 7, [bs, bXB], [bp])
            ckv32 = T7[:, 3, :N]
            CP("dve", ckv32, ps[:, :N], [bp], [b_t7[3]])
            sq = T5[:, 2, :N]
            ACT(sq, ps[:, :N], AF.Square, [bp], [b_t5[2]])
            st_ps, bst = nb()
            MM(st_ps[:, :N], onesb[:], sq, True, True, [b_ones, b_t5[2]], [bst])
            rk = T7[:, 2, :N]
            ACT(rk, st_ps[:, :N], AF.Sqrt, [bst], [b_t7[2]], bias=RMS_EPS, scale=1.0 / 128)
            RECIP(rk, rk, [b_t7[2]], [b_t7[2]])
            STT(ckv32, ckv32, V(l, 54), rk, ALU.mult, ALU.mult, [b_t7[3], b_t7[2], b_vecs], [b_t7[3]])
            if prm:
                CP("act", KTl[:, tok0:tok0 + N], ckv32, [b_t7[3]], [b_ktl])
                ps, bp = nb()
                for tt in range(4):
                    TR(ps[:, 128 * tt:128 * tt + 128], ckv32[:, 128 * tt:128 * tt + 128], ident32[:], [b_t7[3], b_id32], [bp])
                CP("dve", Vt[:, 4 * blk:4 * blk + 4, :], ps[:].rearrange("p (t c) -> p t c", t=4), [bp], [b_v])
                CP("act", ystage[:, 0:512], ps[:], [bp], [b_yst])
                STO(ckv_p[l, tok0:tok0 + 512, :].rearrange("(t p) c -> p t c", p=128),
                    ystage[:, 0:512].rearrange("p (t c) -> p t c", t=4), [b_yst])
            else:
                CP("act", ckvnT_s[:], ckv32, [b_t7[3]], [b_ckvnT_s])
                ps, bp = nb()
                TR(ps[0:TS, 0:128], ckv32, ident32[:], [b_t7[3], b_id32], [bp])
                CP("act", ystage[0:TS, 0:128], ps[0:TS, 0:128], [bp], [b_yst])
                STO(ckv_s[l], ystage[0:TS, 0:128], [b_yst])
            if prm:
                sub("ckv")
            psk, bpk = nb()
            for k in range(8):
                MM(psk[0:32, :N], v8[:, k, 384:416], XB[:, k, :N], k == 0, k == 7, [bs, bXB], [bpk])
            psr, bpr = nb()
            for k in range(8):
                MM(psr[0:32, :N], v8[:, k, 416:448], XB[:, k, :N], k == 0, k == 7, [bs, bXB], [bpr])
            t1 = ptmp[0][0][0:32, :N]; bt1 = ptmp[0][1]
            t2 = ptmp[1][0][0:32, :N]; bt2 = ptmp[1][1]
            TT("dve", t1, psr[0:32, :N], sn[0:32, :N], ALU.mult, [bpr, bsn], [bt1])
            TT("dve", t2, psk[0:32, :N], cs[0:32, :N], ALU.mult, [bpk, bcs], [bt2])
            TT("dve", t1, t1, t2, ALU.add, [bt1, bt2], [bt1])
            if prm:
                CP("act", KTr[0:32, tok0:tok0 + N], t1, [bt1], [b_ktr])
                CP("dve", KTr[32:64, tok0:tok0 + N], t1, [bt1], [b_ktr])
                ps, bp = nb()
                for tt in range(4):
                    TR(ps[:, 32 * tt:32 * tt + 32], t1[:, 128 * tt:128 * tt + 128], ident32[0:32, 0:32], [bt1, b_id32], [bp])
                CP("act", ystage[:, 512:640], ps[:, 0:128], [bp], [b_yst])
                STO(kr_p[l, tok0:tok0 + 512, :].rearrange("(t p) c -> p t c", p=128),
                    ystage[:, 512:640].rearrange("p (t c) -> p t c", t=4), [b_yst])
            else:
                CP("act", krnT_s[0:32, :], t1, [bt1], [b_krnT_s])
                CP("dve", krnT_s[32:64, :], t1, [bt1], [b_krnT_s])
                ps, bp = nb()
                TR(ps[0:TS, 0:32], t1, ident32[0:32, 0:32], [bt1, b_id32], [bp])
                CP("act", ystage[0:TS, 512:544], ps[0:TS, 0:32], [bp], [b_yst])
                STO(kr_s[l], ystage[0:TS, 512:544], [b_yst])
            if prm:
                sub("kr")
            QN = T5
            for j in range(4):
                ps, bp = nb()
                for k in range(2):
                    MM(ps[:, :N], wuq[:, k, 128 * j:128 * j + 128], cqn[:, k, :N], k == 0, k == 1, [b_wuq, b_cqn], [bp])
                CP(ev_eng(), QN[:, j, :N], ps[:, :N], [bp], [b_t5[j]])
            bt1 = ptmp[0][1]; bt2 = ptmp[1][1]
            if prm:
                for j in range(4):
                    psa, bpa = nb()
                    for k in range(2):
                        MM(psa[0:64, :N], wuq[:, k, 512 + 64 * j:512 + 64 * j + 64], cqn[:, k, :N], k == 0, k == 1, [b_wuq, b_cqn], [bpa])
                    psb, bpb = nb()
                    for k in range(2):
                        MM(psb[0:64, :N], wuq[:, k, 768 + 64 * j:768 + 64 * j + 64], cqn[:, k, :N], k == 0, k == 1, [b_wuq, b_cqn], [bpb])
                    t1 = ptmp[0][0][0:64, :N]; t2 = ptmp[1][0][0:64, :N]
                    TT("dve", t1, psb[0:64, :N], sn[0:64, :N], ALU.mult, [bpb, bsn], [bt1])
                    TT("dve", t2, psa[0:64, :N], cs[0:64, :N], ALU.mult, [bpa, bcs], [bt2])
                    TT("dve", T6[:, j, :N], t1, t2, ALU.add, [bt1, bt2], [b_t6])
            else:
                for rep in range(2):
                    r0 = 32 * rep
                    psa, bpa = nb()
                    psb, bpb = nb()
                    for h in range(8):
                        for k in range(2):
                            MM(psa[r0:r0 + 32, TS * h:TS * h + TS], wuq[:, k, 512 + 32 * h:512 + 32 * h + 32], cqn[:, k, :N],
                               k == 0, k == 1, [b_wuq, b_cqn], [bpa])
                        for k in range(2):
                            MM(psb[r0:r0 + 32, TS * h:TS * h + TS], wuq[:, k, 768 + 32 * h:768 + 32 * h + 32], cqn[:, k, :N],
                               k == 0, k == 1, [b_wuq, b_cqn], [bpb])
                    for h in range(8):
                        t1 = ptmp[0][0][r0:r0 + 32, :N]; t2 = ptmp[1][0][r0:r0 + 32, :N]
                        TT("dve", t1, psb[r0:r0 + 32, TS * h:TS * h + TS], sn[r0:r0 + 32, :N], ALU.mult, [bpb, bsn], [bt1])
                        TT("dve", t2, psa[r0:r0 + 32, TS * h:TS * h + TS], cs[r0:r0 + 32, :N], ALU.mult, [bpa, bcs], [bt2])
                        TT("dve", qrope_s[r0:r0 + 32, h, :], t1, t2, ALU.add, [bt1, bt2], [b_qrope_s])
            if prm:
                sub("qrope")
            QL, bQL = (T1, b_t1) if prm else (qlat_s, [b_qlat_s])
            for h in range(8):
                ps, bp = nb()
                p0 = 64 * (h % 2)
                MM(ps[:, :N], wukT[p0:p0 + 64, h // 2, :], QN[p0:p0 + 64, h // 2, :N], True, True, [b_wukT, b_t5[h // 2]], [bp])
                CP(ev_eng(), QL[:, h, :N], ps[:, :N], [bp], bQL)
            if prm:
                sub("qlat")
            MIX, bMIX = (T2, b_t2) if prm else (mix_s, b_mix_s)
            if prm:
                def uview(g, a, b_):
                    return uT[:, g, a:b_]

                def tview(i, a, b_):
                    return ptmp[i][0][:, a:b_]
                L = 16 + N
            else:
                def uview(g, a, b_):
                    return uS[:, g, :, a:b_]

                def tview(i, a, b_):
                    return ptmp[i][0][:, 0:NSB * 20].rearrange("p (b t) -> p b t", b=NSB)[:, :, a:b_]
                L = 20
            for g in range(4):
                w = 2 << g
                cur = None
                lo = 0
                sh = 1
                for stp in range(g + 1):
                    lo2 = lo + sh
                    dst_i = stp % 2
                    if cur is None:
                        a0 = uview(g, lo2, L); a1 = uview(g, lo2 - sh, L - sh); bin_ = list(b_t3)
                    else:
                        a0 = tview(cur, lo2, L); a1 = tview(cur, lo2 - sh, L - sh); bin_ = [ptmp[cur][1]]
                    TT("pool", tview(dst_i, lo2, L), a0, a1, ALU.add, bin_, [ptmp[dst_i][1]])
                    cur = dst_i
                    lo = lo2
                    sh *= 2
                if prm and blk == 0:
                    TT("dve", tview(cur, 16, 32), tview(cur, 16, 32), corr[:, g, :], ALU.mult, [ptmp[cur][1], b_corr], [ptmp[cur][1]])
                if prm:
                    pooled = T4[:, g, :N]
                    STT(pooled, tview(cur, 16, L), 1.0 / w, uview(g, 16, L), ALU.mult, ALU.subtract,
                        [ptmp[cur][1]] + b_t3, [b_t4[g]])
                else:
                    pooled = T4[:, g, :N]
                    STT(pooled.rearrange("p (b t) -> p b t", t=4), tview(cur, 16, L), 1.0 / w, uview(g, 16, L),
                        ALU.mult, ALU.subtract, [ptmp[cur][1]] + b_t3, [b_t4[g]])
                ps, bp = nb()
                MM(ps[:, :N], wpool[:, g, :], pooled, True, True, [b_wpool, b_t4[g]], [bp])
                TSC("dve", MIX[:, g, :N], ps[:, :N], V(l, 48, g), None, ALU.mult, ALU.bypass, [bp, b_vecs], [bMIX] if not prm else [b_t2])
            if prm:
                sub("pool")
            if prm:
                if blk == NB - 1:
                    ps, bp = nb()
                    for g in range(4):
                        TR(ps[0:15, 128 * g:128 * g + 128], uT[:, g, N + 1:N + 16], ident32[:], b_t3 + [b_id32], [bp])
                    CP("act", ystage[0:15, 0:512], ps[0:15, :], [bp], [b_yst])
                    STO(pst_p[l], ystage[0:15, 0:512], [b_yst])
                else:
                    CP("dve", hal[:], uT[:, :, N:N + 16], b_t3, [b_hal])
            else:
                for i in range((NSB + 7) // 8):
                    nbt = min(8, NSB - 8 * i)
                    rows = nbt * 15
                    ps, bp = nb()
                    for g in range(4):
                        tmpc = ptmp[g % 2][0][:, 0:rows]; btc = ptmp[g % 2][1]
                        CP("dve", tmpc.rearrange("p (b t) -> p b t", t=15), uS[:, g, 8 * i:8 * i + nbt, 5:20], b_t3, [btc])
                        TR(ps[0:rows, 128 * g:128 * g + 128], tmpc, ident32[:], [btc, b_id32], [bp])
                    CP("act", ystage[0:rows, 0:512], ps[0:rows, :], [bp], [b_yst])
                    STO(pst_s[l, 120 * i:120 * i + rows, :], ystage[0:rows, 0:512], [b_yst])

        def prompt_attn(l, blk):
            N = 512
            nk = 4 * blk + 4
            pti = [0]
            for h in range(8):
                o_ps, bo = PS[4 + h % 2], PSB[4 + h % 2]
                l_ps, bl = PS[6 + h % 2], PSB[6 + h % 2]
                p0 = 32 * (h % 2)
                def emit_S(kt):
                    j = kt - 4 * blk
                    c0 = 128 * j if j > 0 else 0
                    s_ps, bsp = nb()
                    MM(s_ps[:, c0:N], KTl[:, 128 * kt:128 * kt + 128], T1[:, h, c0:N], True, False, [b_ktl] + b_t1, [bsp])
                    MM(s_ps[:, c0:N], KTr[p0:p0 + 32, 128 * kt:128 * kt + 128], T6[p0:p0 + 32, h // 2, c0:N], False, True,
                       [b_ktr, b_t6], [bsp])
                    return s_ps, bsp, c0, j

                pend = emit_S(0)
                for kt in range(nk):
                    s_ps, bsp, c0, j = pend
                    if kt + 1 < nk:
                        pend = emit_S(kt + 1)
                    pi = pti[0] % 4
                    pti[0] += 1
                    pt = T4[:, pi, :]; bpt = b_t4[pi]
                    ACT(pt[:, c0:N], s_ps[:, c0:N], AF.Exp, [bsp], [bpt], scale=SM_SCALE)
                    if j >= 0:
                        TT("pool", pt[:, c0:N], pt[:, c0:N], maskD[:, j, c0:N], ALU.mult, [bpt, b_mask], [bpt])
                    MM(o_ps[:, c0:N], Vt[:, kt, :], pt[:, c0:N], kt == 0, kt == nk - 1, [b_v, bpt], [bo])
                    MM(l_ps[:, c0:N], onesb[:], pt[:, c0:N], kt == 0, kt == nk - 1, [b_ones, bpt], [bl])
                rec = T3[:, h % 2, :N]; brec = b_t3[h % 2]
                RECIP(rec, l_ps[:, :N], [bl], [brec])
                ol, bol = olat[h % 2]
                TT("dve", ol[:, :N], o_ps[:, :N], rec, ALU.mult, [bo, brec], [bol])
                if h % 2 == 1:
                    ps, bp = nb()
                    MM(ps[0:64, :N], wuv[:, 64 * (h - 1):64 * h], olat[0][0][:, :N], True, True, [b_wuv, olat[0][1]], [bp])
                    MM(ps[64:128, :N], wuv[:, 64 * h:64 * h + 64], olat[1][0][:, :N], True, True, [b_wuv, olat[1][1]], [bp])
                    CP("act", T2[:, 4 + h // 2, :N], ps[:, :N], [bp], [b_t2])

        sq = {"issued": 0}

        def gather_ahead(l, k):
            while sq["issued"] <= k and sq["issued"] < NSB * 4:
                b, q = divmod(sq["issued"], 4)
                i = sq["issued"] % 2
                tk, bk = gck[i]
                tr_, br = gkr[i]
                mk.dma("pool", "indirect_dma_start",
                       dict(out=tk[:], out_offset=None, in_=ckvd[l],
                            in_offset=bass.IndirectOffsetOnAxis(ap=idx4[:, q, b:b + 1], axis=0)),
                       outs=[bk], deps=[b_idx4])
                mk.dma("pool", "indirect_dma_start",
                       dict(out=tr_[:], out_offset=None, in_=krd[l],
                            in_offset=bass.IndirectOffsetOnAxis(ap=idx4[:, q, b:b + 1], axis=0)),
                       outs=[br], deps=[b_idx4])
                sq["issued"] += 1

        oacc, b_oacc = T("oacc", [128, 32], F32)
        slotc = [0]

        def sample_attn_batch(l, b):
            CP("dve", qlat_b[:].rearrange("p (t h) -> p h t", h=8), qlat_s[:, :, 4 * b:4 * b + 4], [b_qlat_s], [b_qlat_b])
            CP("dve", qrope_full[:].rearrange("p (t h) -> p h t", h=8), qrope_s[:, :, 4 * b:4 * b + 4], [b_qrope_s], [b_qrope_full])
            o_ps, bo = PS[4], PSB[4]
            memset("dve", lacc[:], 0.0, [b_lacc])
            for q in range(4):
                kq = 4 * b + q
                gather_ahead(l, kq + 1)
                tk, bk = gck[kq % 2]
                tr_, br = gkr[kq % 2]

                def prep(i):
                    grp, g4 = divmod(i, 4)
                    tb = 16 * grp + 4 * g4
                    si = slotc[0] % 2
                    slotc[0] += 1
                    kts, bkts = KTs[si]
                    krs, bkrs = krTs[si]
                    tp, btp = nb()
                    tpb = tp[:].bitcast(BF16)
                    for e_ in range(4):
                        TR(tpb[:, 128 * e_:128 * e_ + 128], tk[:, 128 * (tb + e_):128 * (tb + e_) + 128], identb[:],
                           [bk, b_idb], [btp])
                    CP("dve", kts[:], tpb[:, 0:512], [btp], [bkts])
                    tp2, btp2 = nb()
                    tpb2 = tp2[:].bitcast(BF16)
                    for e2 in range(2):
                        TR(tpb2[0:64, 128 * e2:128 * e2 + 128], tr_[:, 32 * (tb + 2 * e2):32 * (tb + 2 * e2) + 64], identb[:],
                           [br, b_idb], [btp2])
                    CP("act", krs[:].rearrange("p e k -> p (e k)"), tpb2[0:64, 0:256], [btp2], [bkrs])
                    return (kts, bkts, krs, bkrs)

                def scores(i, slots_):
                    grp, g4 = divmod(i, 4)
                    kts, bkts, krs, bkrs = slots_
                    s_ps, bsp = PS[5 + 2 * grp], PSB[5 + 2 * grp]
                    for e_ in range(4):
                        sl = 4 * g4 + e_
                        pr = 32 * (e_ % 2)
                        MM(s_ps[:, 32 * sl:32 * sl + 32], kts[:, 128 * e_:128 * e_ + 128], qlat_b[:], True, False,
                           [bkts, b_qlat_b], [bsp])
                        MM(s_ps[:, 32 * sl:32 * sl + 32], krs[pr:pr + 32, e_ // 2, :], qrope_full[pr:pr + 32, :], False, True,
                           [bkrs, b_qrope_full], [bsp])

                def softmax_part(grp):
                    s_ps, bsp = PS[5 + 2 * grp], PSB[5 + 2 * grp]
                    ptile, bpt = pts[grp]
                    ACT(ptile[:].rearrange("p s q -> p (s q)"), s_ps[:], AF.Exp, [bsp], [bpt], scale=SM_SCALE)
                    mk.op("dve", "tensor_reduce", dict(out=lred[:], in_=ptile[:].rearrange("p s q -> p q s"), axis=AX.X, op=ALU.add),
                          [bpt], [b_lred])
                    TT("dve", lacc[:], lacc[:], lred[:], ALU.add, [b_lacc, b_lred], [b_lacc])

                def pv_part(grp):
                    ptile, bpt = pts[grp]
                    for sl in range(16):
                        tkn = 16 * grp + sl
                        MM(o_ps[:, 0:32], tk[:, 128 * tkn:128 * tkn + 128], ptile[:, sl, :], grp == 0 and sl == 0,
                           grp == 1 and sl == 15, [bk, bpt], [bo])

                pend = prep(0)
                for i in range(8):
                    grp, g4 = divmod(i, 4)
                    cur = pend
                    if i + 1 < 8:
                        pend = prep(i + 1)
                    scores(i, cur)
                    if i == 4:
                        pv_part(0)
                    if g4 == 3:
                        softmax_part(grp)
                pv_part(1)
                if q == 0:
                    CP("dve", oacc[:], o_ps[:, 0:32], [bo], [b_oacc])
                else:
                    TT("dve", oacc[:], oacc[:], o_ps[:, 0:32], ALU.add, [b_oacc, bo], [b_oacc])
                if q < 3:
                    yield (b, q)
            s_ps, bsp = nb()
            MM(s_ps[0:4, 0:32], ckvnT_s[:, 4 * b:4 * b + 4], qlat_b[:], True, False, [b_ckvnT_s, b_qlat_b], [bsp])
            MM(s_ps[0:4, 0:32], krnT_s[0:32, 4 * b:4 * b + 4], qrope_full[0:32, :], False, True, [b_krnT_s, b_qrope_full], [bsp])
            ACT(pnew[:], s_ps[0:4, 0:32], AF.Exp, [bsp], [b_pnew], scale=SM_SCALE)
            TT("dve", pnew[:], pnew[:], smask[:], ALU.mult, [b_pnew, b_smask], [b_pnew])
            CP("dve", pnewb[:], pnew[:], [b_pnew], [b_pnewb])
            tp, btp = nb()
            tpb = tp[:].bitcast(BF16)
            TR(tpb[0:4, 0:128], ckvnT_s[:, 4 * b:4 * b + 4], identb[:], [b_ckvnT_s, b_idb], [btp])
            CP("dve", vnew[:], tpb[0:4, 0:128], [btp], [b_vnew])
            MM(o_ps[:, 0:32], vnew[:], pnewb[:], True, True, [b_vnew, b_pnewb], [bo])
            TT("dve", oacc[:], oacc[:], o_ps[:, 0:32], ALU.add, [b_oacc, bo], [b_oacc])
            l_ps, bl = PS[6], PSB[6]
            MM(l_ps[:, 0:32], ones32[:], lacc[:], True, False, [b_ones32, b_lacc], [bl])
            MM(l_ps[:, 0:32], ones32[0:4, :], pnew[:], False, True, [b_ones32, b_pnew], [bl])
            RECIP(recs[:], l_ps[:, 0:32], [bl], [b_recs])
            TT("dve", olat_s[:, :, 4 * b:4 * b + 4], oacc[:].rearrange("p (t h) -> p h t", h=8),
               recs[:].rearrange("p (t h) -> p h t", h=8), ALU.mult, [b_oacc, b_recs], [b_olat_s])
            yield (b, 3)

        sgen = {"g": None}

        def sample_layer_gen(l):
            for b_ in range(NSB):
                yield from sample_attn_batch(l, b_)

        def sample_fill(n):
            if sgen["g"] is None:
                return
            for _ in range(n):
                try:
                    next(sgen["g"])
                except StopIteration:
                    sgen["g"] = None
                    return

        def mem_kv_prompt(l):
            if True:
                for mt in range(2):
                    LD("sp", xstage[:], memp[128 * mt:128 * mt + 128, :], [b_xst])
                    for half in range(2):
                        ps, bp = nb()
                        for c4 in range(4):
                            c = 4 * half + c4
                            TR(ps[:, 128 * c4:128 * c4 + 128], xstage[:, 128 * c:128 * c + 128], ident32[:], [b_xst, b_id32], [bp])
                        CP("dve", mempT[:, 4 * half:4 * half + 4, 128 * mt:128 * mt + 128],
                           ps[:].rearrange("p (c m) -> p c m", c=4), [bp], b_t1)
            for s in range(2):
                v8, _, bs = snext(("w_mk", l, s))
                for c4 in range(4):
                    ps, bp = nb()
                    for k in range(8):
                        MM(ps[:, 0:256], v8[:, k, 128 * c4:128 * c4 + 128], mempT[:, k, :], k == 0, k == 7, [bs] + b_t1, [bp])
                    CP("act", mkT[:, 4 * s + c4, :], ps[:, 0:256], [bp], [b_mkT])
                for mt in range(2):
                    ps, bp = nb()
                    for k in range(8):
                        MM(ps[:], mempT[:, k, 128 * mt:128 * mt + 128], v8[:, k, :], k == 0, k == 7, [bs] + b_t1, [bp])
                    CP("act", ystage[:, 0:512], ps[:], [bp], [b_yst])
                    STO(mk_p[l, 128 * mt:128 * mt + 128, 512 * s:512 * s + 512], ystage[:, 0:512], [b_yst])
            for s in range(2):
                v8, _, bs = snext(("w_mv", l, s))
                for mt in range(2):
                    ps, bp = nb()
                    for k in range(8):
                        MM(ps[:], mempT[:, k, 128 * mt:128 * mt + 128], v8[:, k, :], k == 0, k == 7, [bs] + b_t1, [bp])
                    CP("act", ystage[:, 0:512], ps[:], [bp], [b_yst])
                    CP("dve", mvp[:, mt, 512 * s:512 * s + 512], ps[:], [bp], [b_mvp])
                    STO(mv_p[l, 128 * mt:128 * mt + 128, 512 * s:512 * s + 512], ystage[:, 0:512], [b_yst])

        def load_small(l):
            for k in range(2):
                src = w_uq[l, 128 * k:128 * k + 128, :].rearrange("p (h d) -> p h d", h=8)
                LD("pool", wuq[:, k, 0:512].rearrange("p (h d) -> p h d", h=8), src[:, :, 0:64], [b_wuq])
                LD("pool", wuq[:, k, 512:768].rearrange("p (h d) -> p h d", h=8), src[:, :, 64:96], [b_wuq])
                rot = wuq[:, k, 768:1024].rearrange("p (h d) -> p h d", h=8)
                LD("pool", rot[:, :, 0:16], src[:, :, 80:96], [b_wuq])
                LD("pool", rot[:, :, 16:32], src[:, :, 64:80], [b_wuq])
            LD("pool", wuk[:], w_uk[l], [b_wuk])
            LD("pool", wuv[:], w_uv[l], [b_wuv])
            LD("pool", wpool[:], w_pool[l].rearrange("g c d -> c g d"), [b_wpool])
            for j in range(4):
                tp, btp = nb()
                tpb = tp[:].bitcast(BF16)
                TR(tpb[:, 0:128], wuk[:, 128 * j:128 * j + 128], identb[:], [b_wuk, b_idb], [btp])
                CP("dve", wukT[:, j, :], tpb[:, 0:128], [btp], [b_wukT])

        def phaseB(l, prm):
            N = 512 if prm else TS
            X, bX, XB, bXB = (xres, b_xres, xbf, b_xbf) if prm else (xres_s, b_xres_s, xbf_s, b_xbf_s)
            MIX, bMIX = (T2, [b_t2]) if prm else (mix_s, [b_mix_s])
            QM, bQM = (T1, b_t1) if prm else (qlat_s, [b_qlat_s])
            if not prm:
                for hp in range(4):
                    ps, bp = nb()
                    MM(ps[0:64, :N], wuv[:, 128 * hp:128 * hp + 64], olat_s[:, 2 * hp, :], True, True, [b_wuv, b_olat_s], [bp])
                    MM(ps[64:128, :N], wuv[:, 128 * hp + 64:128 * hp + 128], olat_s[:, 2 * hp + 1, :], True, True, [b_wuv, b_olat_s], [bp])
                    CP("act", mix_s[:, 4 + hp, :], ps[:, :N], [bp], [b_mix_s])
            proj_residual("w_o", l, MIX, bMIX, X, bX, N)
            layer_norm(l, 0, X, bX, XB, bXB, N, FILL if prm else 0)
            for s in range(2):
                v8, _, bs = snext(("w_mq", l, s))
                for c4 in range(4):
                    ps, bp = nb()
                    for k in range(8):
                        MM(ps[:, :N], v8[:, k, 128 * c4:128 * c4 + 128], XB[:, k, :N], k == 0, k == 7, [bs, bXB], [bp])
                    CP(ev_eng(), QM[:, 4 * s + c4, :N], ps[:, :N], [bp], bQM)
            if prm:
                mem_attn(QM, bQM, T2, [b_t2], 512, 0, mkT, b_mkT, mvp, b_mvp)
            else:
                for b in range(NSB):
                    LD("pool", memk_sb[:], memk[l, b].rearrange("(t p) n -> p t n", p=128), [b_memk])
                    LD("pool", memv_sb[:], memv[l, b].rearrange("(t p) n -> p t n", p=128), [b_memv])
                    for c in range(8):
                        tp, btp = nb()
                        tpb = tp[:].bitcast(BF16)
                        for mt in range(2):
                            TR(tpb[:, 128 * mt:128 * mt + 128], memk_sb[:, mt, 128 * c:128 * c + 128], identb[:], [b_memk, b_idb], [btp])
                        CP(ev_eng(), mkT_s[:, c, :], tpb[:, 0:256], [btp], [b_mkT_s])
                    mem_attn(QM, bQM, mix_s, [b_mix_s], 4, 4 * b, mkT_s, b_mkT_s, memv_sb, b_memv)
            proj_residual("w_mo", l, MIX, bMIX, X, bX, N)
            layer_norm(l, 1, X, bX, XB, bXB, N, FILL if prm else 0)
            mlp(l, XB, bXB, X, bX, N)
            layer_norm(l, 2, X, bX, XB, bXB, N, FILL if prm else 0)

        def prompt_load(l, blk):
            tok0 = 512 * blk
            if l == 0:
                for tt in range(4):
                    LD("sp", xstage[:], xp[tok0 + 128 * tt:tok0 + 128 * tt + 128, :], [b_xst])
                    for half in range(2):
                        ps, bp = nb()
                        for c4 in range(4):
                            c = 4 * half + c4
                            TR(ps[:, 128 * c4:128 * c4 + 128], xstage[:, 128 * c:128 * c + 128], ident32[:], [b_xst, b_id32], [bp])
                        CP("dve", xres[:, 4 * half:4 * half + 4, 128 * tt:128 * tt + 128], ps[:].rearrange("p (c m) -> p c m", c=4), [bp], [b_xres])
                        CP("dve", xbf[:, 4 * half:4 * half + 4, 128 * tt:128 * tt + 128], ps[:].rearrange("p (c m) -> p c m", c=4), [bp], [b_xbf])
            else:
                LD("sp", xres[:], scr[:, :, tok0:tok0 + 512], [b_xres])
                CP("dve", xbf[:], xres[:], [b_xres], [b_xbf])

        def prompt_store(l, blk):
            tok0 = 512 * blk
            if l == 0:
                mk.dma("sp", "dma_start", dict(out=scr[:, :, tok0:tok0 + 512], in_=xres[:]), ins=[b_xres])
            else:
                for tt in range(4):
                    for half in range(2):
                        ps, bp = nb()
                        for c4 in range(4):
                            c = 4 * half + c4
                            TR(ps[:, 128 * c4:128 * c4 + 128], xres[:, c, 128 * tt:128 * tt + 128], ident32[:], [b_xres, b_id32], [bp])
                        CP(ev_eng(), ystage[:, 512 * half:512 * half + 512], ps[:], [bp], [b_yst])
                    STO(y_p[tok0 + 128 * tt:tok0 + 128 * tt + 128, :], ystage[:], [b_yst])

        def sample_load():
            LD("sp", xstage[0:TS, :], xs, [b_xst])
            for half in range(2):
                ps, bp = nb()
                for c4 in range(4):
                    c = 4 * half + c4
                    TR(ps[:, TS * c4:TS * c4 + TS], xstage[0:TS, 128 * c:128 * c + 128], ident32[0:TS, 0:TS], [b_xst, b_id32], [bp])
                CP("act", xres_s[:, 4 * half:4 * half + 4, :], ps[:, 0:4 * TS].rearrange("p (c m) -> p c m", c=4), [bp], [b_xres_s])
                CP("dve", xbf_s[:, 4 * half:4 * half + 4, :], ps[:, 0:4 * TS].rearrange("p (c m) -> p c m", c=4), [bp], [b_xbf_s])

        def sample_store():
            for half in range(2):
                ps, bp = nb()
                for c4 in range(4):
                    c = 4 * half + c4
                    TR(ps[0:TS, 128 * c4:128 * c4 + 128], xres_s[:, c, :], ident32[:], [b_xres_s, b_id32], [bp])
                CP(ev_eng(), ystage[0:TS, 512 * half:512 * half + 512], ps[0:TS, :], [bp], [b_yst])
            STO(y_s, ystage[0:TS, :], [b_yst])

        import os
        KSTOP = int(os.environ.get("KSTOP", "100000"))
        stg = [0]

        def stage(name):
            stg[0] += 1
            if stg[0] >= KSTOP:
                print("STOP at stage", stg[0], name)
                raise _Stop()

        try:
            sample_load()
            stage("sample_load")
            for l in range(2):
                load_small(l)
                stage("load_small")
                mem_kv_prompt(l)
                stage("mem_kv")
                sq["issued"] = 0
                gather_ahead(l, 0)
                phaseA(l, False)
                stage("phaseA_s")
                sgen["g"] = sample_layer_gen(l)
                for blk in range(NB):
                    prompt_load(l, blk)
                    stage("prompt_load")
                    phaseA(l, True, blk)
                    stage("phaseA_p")
                    prompt_attn(l, blk)
                    stage("attn_p")
                    phaseB(l, True)
                    stage("phaseB_p")
                    prompt_store(l, blk)
                    stage("store_p")
                    stage("sample_attn")
                sample_fill(10 ** 6)
                phaseB(l, False)
                stage("phaseB_s")
            sample_store()
            assert ptr["c"] == len(plan), (ptr["c"], len(plan))
        except _Stop:
            pass
        mk.finish()
        mk.emit()
    return nc


def _consts(S, NSB, past_len):
    TS = 4 * NSB
    half = 16
    inv = (np.float32(10000.0) ** (-(np.arange(half, dtype=np.float32) / np.float32(half)))).astype(np.float32)

    def tables(pos):
        ang = pos.astype(np.float32)[:, None] * inv[None, :]
        c, s = np.cos(ang).astype(np.float32), np.sin(ang).astype(np.float32)
        cos32 = np.concatenate([c, c], axis=1).T
        sin32 = np.concatenate([-s, s], axis=1).T
        return np.tile(cos32, (4, 1)).copy(), np.tile(sin32, (4, 1)).copy()
    cosP, sinP = tables(np.arange(S))
    cs4, sn4 = tables(past_len + np.arange(4))
    cosS = np.tile(cs4, (1, NSB)).copy()
    sinS = np.tile(sn4, (1, NSB)).copy()
    p = np.arange(128)[:, None, None]
    j = np.arange(4)[None, :, None]
    n = np.arange(512)[None, None, :]
    maskD = (n >= 128 * j + p).astype(np.float32).reshape(128, 2048)
    jj = np.arange(4)[:, None]
    tt = (np.arange(32) // 8)[None, :]
    smask = (jj <= tt).astype(np.float32)
    corr = np.zeros((128, 4, 16), np.float32)
    for g in range(4):
        w = 2 << g
        corr[:, g, :] = (w / np.minimum(w, np.arange(16) + 1.0))[None, :]
    qoff = np.zeros((128, 4, NSB), np.int32)
    for q in range(4):
        qoff[:, q, :] = q
    return dict(ident=np.eye(128, dtype=np.float32), cosP=cosP, sinP=sinP, cosS=cosS, sinS=sinS, maskD=maskD,
                smask=smask, corr=corr.reshape(128, 64), qoff=qoff.reshape(128, 4 * NSB))


def _vecs(inp):
    v = np.zeros((128, 110), np.float32)
    for l in range(2):
        b = 55 * l
        for i, nm in enumerate(["ln1_g", "ln1_b", "ln2_g", "ln2_b", "ln3_g", "ln3_b"]):
            v[:, b + 8 * i:b + 8 * i + 8] = np.asarray(inp[nm][l]).reshape(8, 128).T
        v[:, b + 48:b + 52] = np.asarray(inp["pool_scale"][l]).reshape(4, 128).T
        v[:, b + 52:b + 54] = np.asarray(inp["g_q"][l]).reshape(2, 128).T
        v[:, b + 54] = np.asarray(inp["g_kv"][l])
    return v


_NC_CACHE = {}


def make_in_maps(inp, n_cores):
    xpr = np.asarray(inp["x_prompt"]); xsm = np.asarray(inp["x_sample"])
    B, S, _ = xpr.shape
    DB = xsm.shape[0]
    NSB = DB // n_cores
    pt = np.asarray(inp["page_table"])
    n_pages = pt.shape[1]
    assert n_pages == 128 and B == n_cores
    cck = np.asarray(inp["cache_ckv"]); ckr = np.asarray(inp["cache_krope"])
    NPHYS = cck.shape[1]
    cst = _consts(S, NSB, n_pages * 128)
    vec = _vecs(inp)
    f = lambda k: np.ascontiguousarray(np.asarray(inp[k]), dtype=np.float32)
    shared = dict(
        ckv0=cck[0].reshape(NPHYS * 4, 4096), ckv1=cck[1].reshape(NPHYS * 4, 4096),
        kr0=ckr[0].reshape(NPHYS * 4, 1024), kr1=ckr[1].reshape(NPHYS * 4, 1024),
        w_in=f("w_in"), w_uq=f("w_uq"), w_uk=f("w_uk").reshape(2, 128, 512), w_uv=f("w_uv").reshape(2, 128, 512),
        w_pool=f("w_pool"), w_o=f("w_o"), w_mq=f("w_mq"), w_mk=f("w_mk"), w_mv=f("w_mv"), w_mo=f("w_mo"),
        w1=f("w1"), w2=f("w2"), vecs=vec, **cst)
    mk_ = np.asarray(inp["cache_mem_k"]); mv_ = np.asarray(inp["cache_mem_v"]); sp_ = np.asarray(inp["state_pool"])
    mp_ = np.asarray(inp["mem_prompt"])
    maps = []
    for c in range(n_cores):
        sl = slice(NSB * c, NSB * c + NSB)
        m = dict(shared)
        m.update(
            xp=xpr[c], xs=xsm[sl].reshape(NSB * 4, 1024),
            memk=mk_[:, sl].reshape(2, NSB, 256, 1024), memv=mv_[:, sl].reshape(2, NSB, 256, 1024),
            spool=sp_[:, sl].reshape(2, NSB * 15, 512),
            ptT=np.ascontiguousarray(pt[sl].T.astype(np.int32)), memp=mp_[c])
        maps.append(m)
    return maps, (S, NSB, NPHYS)


def assemble(results, n_cores, S, NSB):
    DB = NSB * n_cores
    y_p = np.stack([r["y_p"] for r in results])
    y_s = np.concatenate([r["y_s"].reshape(NSB, 4, 1024) for r in results])
    ckv_p = np.stack([r["ckv_p"] for r in results], axis=1)
    kr_p = np.stack([r["kr_p"] for r in results], axis=1)
    mk_p = np.stack([r["mk_p"].reshape(2, 256, 4, 256) for r in results], axis=1)
    mv_p = np.stack([r["mv_p"].reshape(2, 256, 4, 256) for r in results], axis=1)
    pst_p = np.stack([r["pst_p"] for r in results], axis=1)
    ckv_s = np.concatenate([r["ckv_s"].reshape(2, NSB, 4, 128) for r in results], axis=1)
    kr_s = np.concatenate([r["kr_s"].reshape(2, NSB, 4, 32) for r in results], axis=1)
    pst_s = np.concatenate([r["pst_s"].reshape(2, NSB, 15, 512) for r in results], axis=1)
    return tuple(np.ascontiguousarray(a, dtype=np.float32) for a in
                 (y_p, y_s, ckv_p, kr_p, mk_p, mv_p, pst_p, ckv_s, kr_s, pst_s))


def kernel(**inputs):
    maps, (S, NSB, NPHYS) = make_in_maps(inputs, NCORES)
    key = (S, NSB, NPHYS)
    if key not in _NC_CACHE:
        _NC_CACHE[key] = build(S, NSB, NPHYS)
    nc = _NC_CACHE[key]
    res = run_bass_kernel_spmd(nc, maps, core_ids=list(range(NCORES)))
    return assemble(res.results, NCORES, S, NSB)
```

```python
import numpy as np
from contextlib import ExitStack
import concourse.bass as bass
import concourse.mybir as mybir
from concourse.bass_utils import run_bass_kernel_spmd

F32 = mybir.dt.float32
BF16 = mybir.dt.bfloat16
I32 = mybir.dt.int32
AF = mybir.ActivationFunctionType
ALU = mybir.AluOpType
AX = mybir.AxisListType

ALPHA = 4.0 ** 0.25
SM_SCALE = 96.0 ** -0.5
LN_EPS = 1e-5
RMS_EPS = 1e-6
NCORES = 8


class Buf:
    __slots__ = ("name", "w", "r", "dsem", "dcount", "excl")

    def __init__(self, name, excl=False):
        self.name = name
        self.excl = excl
        self.w = None
        self.r = {}
        self.dsem = None
        self.dcount = 0


class Eng:
    def __init__(self, name, eng, sem):
        self.name, self.eng, self.sem = name, eng, sem
        self.count = 0
        self.seen = {}
        self.thunks = []


class MK:
    def __init__(self, nc, stack):
        self.nc, self.stack = nc, stack
        self.engs = {}
        for name, eng in (("pe", nc.tensor), ("act", nc.scalar), ("dve", nc.vector),
                          ("pool", nc.gpsimd), ("sp", nc.sync)):
            sem = stack.enter_context(nc.semaphore("sem_" + name))
            self.engs[name] = Eng(name, eng, sem)
        self.out_dmas = []
        self.nb = 0

    def sbuf(self, name, shape, dtype):
        return self.stack.enter_context(self.nc.sbuf_tensor(name, shape, dtype))

    def psum(self, name, shape, dtype):
        return self.stack.enter_context(self.nc.psum_tensor(name, shape, dtype))

    def buf(self, name, excl=False):
        self.nb += 1
        return Buf("%s_%d" % (name, self.nb), excl)

    def _wait(self, e, key, sem, val):
        if e.seen.get(key, 0) >= val:
            return
        e.seen[key] = val
        eng = e.eng
        e.thunks.append(lambda: eng.wait_ge(sem, val))

    def _wait_on(self, e, who, cnt, b):
        if who == "dma":
            self._wait(e, ("d", id(b)), b.dsem, cnt)
        else:
            if who == e.name and e.name in ("pe", "sp"):
                return
            self._wait(e, who, self.engs[who].sem, cnt)

    def _deps(self, e, ins, outs):
        for b in ins:
            if b.w is not None:
                self._wait_on(e, b.w[0], b.w[1], b)
            if b.excl:
                for who, cnt in b.r.items():
                    if who != e.name:
                        self._wait_on(e, who, cnt, b)
        for b in outs:
            if b.w is not None:
                self._wait_on(e, b.w[0], b.w[1], b)
            for who, cnt in b.r.items():
                self._wait_on(e, who, cnt, b)

    def op(self, ename, meth, kw, ins=(), outs=()):
        e = self.engs[ename]
        self._deps(e, ins, outs)
        e.count += 1
        c = e.count
        eng, sem = e.eng, e.sem
        e.thunks.append(lambda: getattr(eng, meth)(**kw).then_inc(sem, 1))
        for b in ins:
            b.r[ename] = c
        for b in outs:
            b.w = (ename, c)
            b.r = {}

    def dma(self, qname, meth, kw, ins=(), outs=(), deps=(), is_output=False):
        e = self.engs[qname]
        self._deps(e, list(ins) + list(deps), outs)
        eng = e.eng
        main = outs[0] if outs else ins[0]
        if main.dsem is None:
            main.dsem = self.stack.enter_context(self.nc.semaphore("d_" + main.name))
        sem = main.dsem
        main.dcount += 16
        cnt = main.dcount
        e.thunks.append(lambda: getattr(eng, meth)(**kw).then_inc(sem, 16))
        for b in outs:
            assert b is main
            b.w = ("dma", cnt)
            b.r = {}
        for b in ins:
            assert b is main
            b.r["dma"] = cnt
        if is_output:
            self.out_dmas.append((main, cnt))

    def finish(self):
        e = self.engs["sp"]
        for b, cnt in self.out_dmas:
            self._wait(e, ("d", id(b)), b.dsem, cnt)
        for name in ("pe", "act", "dve", "pool"):
            o = self.engs[name]
            if o.count:
                self._wait(e, name, o.sem, o.count)

    def emit(self):
        with self.nc.Block() as block:
            @block.tensor
            def _(x):
                for t in self.engs["pe"].thunks:
                    t()

            @block.scalar
            def _(x):
                for t in self.engs["act"].thunks:
                    t()

            @block.vector
            def _(x):
                for t in self.engs["dve"].thunks:
                    t()

            @block.gpsimd
            def _(x):
                for t in self.engs["pool"].thunks:
                    t()

            @block.sync
            def _(x):
                for t in self.engs["sp"].thunks:
                    t()


def build(S, NSB, NPHYS):
    NB = S // 512
    TS = 4 * NSB
    NT = S // 128
    nc = bass.Bass("TRN2", target_bir_lowering=False)

    def din(name, shape, dt=F32):
        return nc.dram_tensor(name, list(shape), dt, kind="ExternalInput").ap()

    def dout(name, shape):
        return nc.dram_tensor(name, list(shape), F32, kind="ExternalOutput").ap()

    xp = din("xp", [S, 1024]); xs = din("xs", [TS, 1024])
    ckvd = [din("ckv%d" % l, [NPHYS * 4, 4096]) for l in range(2)]
    krd = [din("kr%d" % l, [NPHYS * 4, 1024]) for l in range(2)]
    memk = din("memk", [2, NSB, 256, 1024]); memv = din("memv", [2, NSB, 256, 1024])
    spool = din("spool", [2, NSB * 15, 512])
    ptT = din("ptT", [128, NSB], I32); qoff = din("qoff", [128, 4 * NSB], I32)
    memp = din("memp", [256, 1024])
    w_in = din("w_in", [2, 1024, 928]); w_uq = din("w_uq", [2, 256, 768])
    w_uk = din("w_uk", [2, 128, 512]); w_uv = din("w_uv", [2, 128, 512])
    w_pool = din("w_pool", [2, 4, 128, 128]); w_o = din("w_o", [2, 1024, 1024])
    w_mq = din("w_mq", [2, 1024, 1024]); w_mk = din("w_mk", [2, 1024, 1024])
    w_mv = din("w_mv", [2, 1024, 1024]); w_mo = din("w_mo", [2, 1024, 1024])
    w1 = din("w1", [2, 1024, 4096]); w2 = din("w2", [2, 4096, 1024])
    vecs_d = din("vecs", [128, 110])
    ident_d = din("ident", [128, 128])
    cosP = din("cosP", [128, S]); sinP = din("sinP", [128, S])
    cosS_d = din("cosS", [128, TS]); sinS_d = din("sinS", [128, TS])
    maskD_d = din("maskD", [128, 4 * 512]); smask_d = din("smask", [4, 32]); corr_d = din("corr", [128, 64])

    y_p = dout("y_p", [S, 1024]); y_s = dout("y_s", [TS, 1024])
    ckv_p = dout("ckv_p", [2, S, 128]); kr_p = dout("kr_p", [2, S, 32])
    mk_p = dout("mk_p", [2, 256, 1024]); mv_p = dout("mv_p", [2, 256, 1024])
    pst_p = dout("pst_p", [2, 15, 512])
    ckv_s = dout("ckv_s", [2, TS, 128]); kr_s = dout("kr_s", [2, TS, 32])
    pst_s = dout("pst_s", [2, NSB * 15, 512])
    scr = nc.dram_tensor("scr_x1", [128, 8, S], F32, kind="Internal").ap()

    st = ExitStack()
    with st:
        mk = MK(nc, st)

        def T(name, shape, dt):
            return mk.sbuf("sb_" + name, list(shape), dt), mk.buf(name)

        import os
        KSUB = int(os.environ.get("KSUB", "100000"))
        subc = [0]

        class _Stop(Exception):
            pass

        def sub(name):
            subc[0] += 1
            if subc[0] >= KSUB:
                print("SUBSTOP", subc[0], name)
                raise _Stop()

        PS = [mk.psum("ps%d" % i, [128, 512], F32) for i in range(8)]
        PSB = [mk.buf("ps%d" % i, True) for i in range(8)]
        rot = [0]

        def nb():
            i = rot[0]
            rot[0] = (i + 1) % 4
            return PS[i], PSB[i]

        def MM(out, lhsT, rhs, start, stop, ins, outs):
            mk.op("pe", "matmul", dict(out=out, lhsT=lhsT, rhs=rhs, start=start, stop=stop), ins, outs)

        def TR(out, in_, ident, ins, outs):
            mk.op("pe", "transpose", dict(out=out, in_=in_, identity=ident), ins, outs)

        def ACT(out, in_, func, ins, outs, **kw):
            mk.op("act", "activation", dict(out=out, in_=in_, func=func, **kw), ins, outs)

        def TT(eng, out, in0, in1, op, ins, outs):
            mk.op(eng, "tensor_tensor", dict(out=out, in0=in0, in1=in1, op=op), ins, outs)

        def TSC(eng, out, in0, s1, s2, op0, op1, ins, outs):
            mk.op(eng, "tensor_scalar", dict(out=out, in0=in0, scalar1=s1, scalar2=s2, op0=op0, op1=op1), ins, outs)

        def STT(out, in0, scalar, in1, op0, op1, ins, outs):
            mk.op("dve", "scalar_tensor_tensor", dict(out=out, in0=in0, scalar=scalar, in1=in1, op0=op0, op1=op1), ins, outs)

        def RECIP(out, in_, ins, outs):
            mk.op("dve", "reciprocal", dict(out=out, in_=in_), ins, outs)

        def CP(eng, out, in_, ins, outs):
            if eng == "act":
                ACT(out, in_, AF.Copy, ins, outs)
            else:
                mk.op(eng, "tensor_copy", dict(out=out, in_=in_), ins, outs)

        def memset(eng, ap, val, outs):
            e = mk.engs[eng]
            mk._deps(e, (), outs)
            e.count += 1
            c = e.count
            en, sem = e.eng, e.sem
            e.thunks.append(lambda: en.memset(ap, val).then_inc(sem, 1))
            for b in outs:
                b.w = (eng, c)
                b.r = {}

        def LD(q, out, in_, outs, deps=()):
            mk.dma(q, "dma_start", dict(out=out, in_=in_), outs=outs, deps=deps)

        def STO(out, in_, ins):
            mk.dma("sp", "dma_start", dict(out=out, in_=in_), ins=ins, is_output=True)

        evt = [0]

        def ev_eng():
            evt[0] ^= 1
            return "act" if evt[0] else "dve"

        ident32, b_id32 = T("ident32", [128, 128], F32)
        identb, b_idb = T("identb", [128, 128], BF16)
        onesb, b_ones = T("onesb", [128, 128], BF16)
        ones32, b_ones32 = T("ones32", [128, 128], F32)
        maskD, b_mask = T("maskD", [128, 4, 512], BF16)
        smask, b_smask = T("smask", [4, 32], F32)
        corr, b_corr = T("corr", [128, 4, 16], F32)
        cosS, b_cosS = T("cosS", [128, TS], F32)
        sinS, b_sinS = T("sinS", [128, TS], F32)
        vecs, b_vecs = T("vecs", [128, 110], F32)
        pt_sb, b_pt = T("pt_sb", [128, NSB], I32)
        qoff_sb, b_qoff = T("qoff_sb", [128, 4, NSB], I32)
        idx4, b_idx4 = T("idx4", [128, 4, NSB], I32)
        pt2, b_pt2 = T("pt2", [128, NSB], I32)

        LD("sp", ident32[:], ident_d, [b_id32])
        LD("pool", identb[:], ident_d, [b_idb])
        memset("dve", onesb[:], 1.0, [b_ones])
        memset("dve", ones32[:], 1.0, [b_ones32])
        LD("pool", maskD[:], maskD_d.rearrange("p (j n) -> p j n", j=4), [b_mask])
        LD("sp", smask[:], smask_d, [b_smask])
        LD("sp", corr[:], corr_d.rearrange("p (g t) -> p g t", g=4), [b_corr])
        LD("sp", cosS[:], cosS_d, [b_cosS])
        LD("sp", sinS[:], sinS_d, [b_sinS])
        LD("sp", vecs[:], vecs_d, [b_vecs])
        LD("sp", pt_sb[:], ptT, [b_pt])
        LD("sp", qoff_sb[:], qoff.rearrange("p (q b) -> p q b", q=4), [b_qoff])
        TT("pool", pt2[:], pt_sb[:], pt_sb[:], ALU.add, [b_pt], [b_pt2])
        TT("pool", pt_sb[:], pt2[:], pt2[:], ALU.add, [b_pt2], [b_pt])
        for q in range(4):
            TT("pool", idx4[:, q, :], pt_sb[:], qoff_sb[:, q, :], ALU.add, [b_pt, b_qoff], [b_idx4])

        def V(l, a, b=None):
            c0 = 55 * l + a
            return vecs[:, c0:c0 + 1] if b is None else vecs[:, c0 + b:c0 + b + 1]

        KTl, b_ktl = T("KTl", [128, S], BF16)
        KTr, b_ktr = T("KTr", [64, S], BF16)
        Vt, b_v = T("Vt", [128, NT, 128], BF16)
        wuq, b_wuq = T("wuq", [128, 2, 1024], BF16)
        wuk, b_wuk = T("wuk", [128, 512], BF16)
        wukT, b_wukT = T("wukT", [128, 4, 128], BF16)
        wuv, b_wuv = T("wuv", [128, 512], BF16)
        wpool, b_wpool = T("wpool", [128, 4, 128], BF16)
        mkT, b_mkT = T("mkT", [128, 8, 256], BF16)
        mvp, b_mvp = T("mvp", [128, 2, 1024], BF16)
        hal, b_hal = T("hal", [128, 4, 16], F32)

        NSLOT = 3
        slots = [T("slab%d" % i, [128, 4096], BF16) for i in range(NSLOT)]
        gck = [T("gck%d" % i, [128, 4096], BF16) for i in range(2)]
        gkr = [T("gkr%d" % i, [128, 1024], BF16) for i in range(2)]

        xres, b_xres = T("xres", [128, 8, 512], F32)
        xbf, b_xbf = T("xbf", [128, 8, 512], BF16)
        T1, _ = T("T1", [128, 8, 512], BF16); b_t1 = [mk.buf("t1a"), mk.buf("t1b")]
        mempT = T1[:, :, 0:256]
        T2, b_t2 = T("T2", [128, 8, 512], BF16)
        T3, _ = T("T3", [128, 4, 528], F32); b_t3 = [mk.buf("t3_%d" % i) for i in range(4)]
        T4, _ = T("T4", [128, 4, 512], BF16); b_t4 = [mk.buf("t4_%d" % i) for i in range(4)]
        T5, _ = T("T5", [128, 4, 512], BF16); b_t5 = [mk.buf("t5_%d" % i) for i in range(4)]
        T6, b_t6 = T("T6", [64, 4, 512], BF16)
        T7, _ = T("T7", [128, 4, 512], F32); b_t7 = [mk.buf("t7_%d" % i) for i in range(4)]
        cqn, b_cqn = T("cqn", [128, 2, 512], BF16)
        cosb, b_cos = T("cosb", [128, 512], F32)
        sinb, b_sin = T("sinb", [128, 512], F32)
        ptmp = [T("ptmp%d" % i, [128, 528], F32) for i in range(2)]
        olat = [T("olat%d" % i, [128, 512], BF16) for i in range(2)]
        ystage, b_yst = T("ystage", [128, 1024], F32)
        xstage, b_xst = ystage, b_yst

        xres_s, b_xres_s = T("xres_s", [128, 8, TS], F32)
        xbf_s, b_xbf_s = T("xbf_s", [128, 8, TS], BF16)
        qlat_s, b_qlat_s = T("qlat_s", [128, 8, TS], BF16)
        qrope_s, b_qrope_s = T("qrope_s", [64, 8, TS], BF16)
        qrope_full, b_qrope_full = T("qrope_full", [64, 32], BF16)
        ckvnT_s, b_ckvnT_s = T("ckvnT_s", [128, TS], BF16)
        krnT_s, b_krnT_s = T("krnT_s", [64, TS], BF16)
        mix_s, b_mix_s = T("mix_s", [128, 8, TS], BF16)
        olat_s, b_olat_s = T("olat_s", [128, 8, TS], BF16)
        KTs = [T("KTs%d" % i, [128, 512], BF16) for i in range(2)]
        krTs = [T("krTs%d" % i, [64, 2, 128], BF16) for i in range(2)]
        pts = [T("pts%d" % i, [128, 16, 32], BF16) for i in range(2)]
        qlat_b, b_qlat_b = T("qlat_b", [128, 32], BF16)
        qrope_b, b_qrope_b = T("qrope_b", [64, 32], BF16)
        lacc, b_lacc = T("lacc", [128, 32], F32)
        lred, b_lred = T("lred", [128, 32], F32)
        vnew, b_vnew = T("vnew", [4, 128], BF16)
        pnew, b_pnew = T("pnew", [4, 32], F32)
        pnewb, b_pnewb = T("pnewb", [4, 32], BF16)
        recs, b_recs = T("recs", [128, 32], F32)
        memk_sb, b_memk = T("memk_sb", [128, 2, 1024], BF16)
        memv_sb, b_memv = T("memv_sb", [128, 2, 1024], BF16)
        mkT_s, b_mkT_s = T2[:, :, 0:256], b_t2

        sstate = {"i": 0}

        def slab_load(key):
            kind, l, j = key[0], key[1], key[2]
            i = sstate["i"] % NSLOT
            sstate["i"] += 1
            t, b = slots[i]
            v8 = t[:].rearrange("p (k n) -> p k n", k=8)
            v4 = t[:].rearrange("p (k n) -> p k n", k=4)
            if kind == "w_in":
                src = w_in[l].rearrange("(k p) n -> p k n", p=128)
                if j == 0:
                    LD("pool", v8[:, :, 0:512], src[:, :, 0:512], [b])
                else:
                    LD("pool", v8[:, :, 0:416], src[:, :, 512:928], [b])
                    LD("pool", v8[:, :, 416:432], src[:, :, 912:928], [b])
                    LD("pool", v8[:, :, 432:448], src[:, :, 896:912], [b])
            elif kind == "w1":
                src = w1[l].rearrange("(k p) n -> p k n", p=128)
                LD("pool", v8, src[:, :, 512 * j:512 * j + 512], [b])
            elif kind == "w2":
                src = w2[l][512 * j:512 * j + 512, :].rearrange("(k p) n -> p k n", p=128)
                LD("pool", v4, src, [b])
            else:
                wd = {"w_o": w_o, "w_mq": w_mq, "w_mk": w_mk, "w_mv": w_mv, "w_mo": w_mo}[kind]
                src = wd[l].rearrange("(k p) n -> p k n", p=128)
                LD("pool", v8, src[:, :, 512 * j:512 * j + 512], [b])
            return (v8, v4, b)

        plan = []
        loaded = []
        ptr = {"c": 0, "l": 0}

        def snext(key):
            c = ptr["c"]
            assert plan[c] == key, (plan[c], key)
            while ptr["l"] < len(plan) and ptr["l"] < c + NSLOT:
                loaded.append(slab_load(plan[ptr["l"]]))
                ptr["l"] += 1
            ptr["c"] += 1
            return loaded[c]

        def unit_keys(l):
            ks = [("w_in", l, 0), ("w_in", l, 1), ("w_o", l, 0), ("w_o", l, 1), ("w_mq", l, 0), ("w_mq", l, 1),
                  ("w_mo", l, 0), ("w_mo", l, 1)]
            ks += [("w1", l, 0)]
            for j in range(8):
                if j + 1 < 8:
                    ks += [("w1", l, j + 1)]
                ks += [("w2", l, j)]
            return ks

        for l in range(2):
            plan += [("w_mk", l, 0), ("w_mk", l, 1), ("w_mv", l, 0), ("w_mv", l, 1)]
            plan += [("w_in", l, 0), ("w_in", l, 1)]
            for blk in range(NB):
                plan += unit_keys(l)
            plan += unit_keys(l)[2:]

        def layer_norm(l, which, X, bX, XB, bXB, N):
            gcol, bcol = 16 * which, 16 * which + 8
            sum_ps, bsum = PS[4], PSB[4]
            sq_ps, bsq = PS[6], PSB[6]
            for c in range(8):
                zb = T5[:, (2 * c) % 4, :N]; bz = b_t5[(2 * c) % 4]
                zs = T5[:, (2 * c + 1) % 4, :N]; bs = b_t5[(2 * c + 1) % 4]
                CP("dve", zb, X[:, c, :N], [bX], [bz])
                ACT(zs, X[:, c, :N], AF.Square, [bX], [bs])
                MM(sum_ps[:, :N], onesb[:], zb, c == 0, c == 7, [b_ones, bz], [bsum])
                MM(sq_ps[:, :N], onesb[:], zs, c == 0, c == 7, [b_ones, bs], [bsq])
            m = T3[:, 0, :N]; var = T3[:, 1, :N]
            ACT(m, sum_ps[:, :N], AF.Copy, [bsum], [b_t3[0]], scale=1.0 / 1024)
            TT("dve", var, m, m, ALU.mult, [b_t3[0]], [b_t3[1]])
            STT(var, sq_ps[:, :N], 1.0 / 1024, var, ALU.mult, ALU.subtract, [bsq, b_t3[1]], [b_t3[1]])
            ACT(var, var, AF.Sqrt, [b_t3[1]], [b_t3[1]], bias=LN_EPS, scale=1.0)
            RECIP(var, var, [b_t3[1]], [b_t3[1]])
            for c in range(8):
                t = T3[:, 2 + (c % 2), :N]; bt = b_t3[2 + (c % 2)]
                TT("dve", t, X[:, c, :N], m, ALU.subtract, [bX, b_t3[0]], [bt])
                TT("dve", t, t, var, ALU.mult, [bt, b_t3[1]], [bt])
                ACT(X[:, c, :N], t, AF.Identity, [bt, b_vecs], [bX], scale=V(l, gcol, c), bias=V(l, bcol, c))
                ACT(XB[:, c, :N], t, AF.Identity, [bt, b_vecs], [bXB], scale=V(l, gcol, c), bias=V(l, bcol, c))

        def proj_residual(kind, l, RHS, bR, X, bX, N):
            for s in range(2):
                v8, v4, bs = snext((kind, l, s))
                for c4 in range(4):
                    c = 4 * s + c4
                    ps, bp = nb()
                    for k in range(8):
                        MM(ps[:, :N], v8[:, k, 128 * c4:128 * c4 + 128], RHS[:, k, :N], k == 0, k == 7, [bs] + bR, [bp])
                    STT(X[:, c, :N], X[:, c, :N], ALPHA, ps[:, :N], ALU.mult, ALU.add, [bX, bp], [bX])

        def mem_attn(QM, bQM, OM, bOM, n, c0, mkT_, b_mk_, mv_, b_mv_):
            for h in range(4):
                pm = T4[:, 2 * (h % 2):2 * (h % 2) + 2, :]
                bpm = b_t4[2 * (h % 2):2 * (h % 2) + 2]
                for mt in range(2):
                    ps, bp = nb()
                    for dh in range(2):
                        MM(ps[:, :n], mkT_[:, 2 * h + dh, 128 * mt:128 * mt + 128], QM[:, 2 * h + dh, c0:c0 + n],
                           dh == 0, dh == 1, [b_mk_] + bQM, [bp])
                    ACT(pm[:, mt, :n], ps[:, :n], AF.Exp, [bp], [bpm[mt]], scale=1.0 / 16)
                lps, bl = nb()
                for mt in range(2):
                    MM(lps[:, :n], onesb[:], pm[:, mt, :n], mt == 0, mt == 1, [b_ones, bpm[mt]], [bl])
                rec = T3[:, h % 2, :n]; brec = b_t3[h % 2]
                RECIP(rec, lps[:, :n], [bl], [brec])
                for dh in range(2):
                    ops, bo = nb()
                    for mt in range(2):
                        MM(ops[:, :n], mv_[:, mt, 256 * h + 128 * dh:256 * h + 128 * dh + 128], pm[:, mt, :n],
                           mt == 0, mt == 1, [b_mv_, bpm[mt]], [bo])
                    TT("dve", OM[:, 2 * h + dh, c0:c0 + n], ops[:, :n], rec, ALU.mult, [bo, brec], bOM)

        def mlp(l, XB, bXB, X, bX, N):
            def stage1(j):
                v8, _, b1 = snext(("w1", l, j))
                hs = j % 2
                h1 = T1[:, 4 * hs:4 * hs + 4, :]; bh = b_t1[hs]
                for f in range(4):
                    ps, bp = nb()
                    for k in range(8):
                        MM(ps[:, :N], v8[:, k, 128 * f:128 * f + 128], XB[:, k, :N], k == 0, k == 7, [b1, bXB], [bp])
                    r = T3[:, 2 + (f % 2), :N]; br = b_t3[2 + (f % 2)]
                    ACT(r, ps[:, :N], AF.Relu, [bp], [br])
                    TT("dve", h1[:, f, :N], r, r, ALU.mult, [br], [bh])

            def stage2(j):
                _, v4, b2 = snext(("w2", l, j))
                hs = j % 2
                h1 = T1[:, 4 * hs:4 * hs + 4, :]; bh = b_t1[hs]
                for c in range(8):
                    ps, bp = nb()
                    for f in range(4):
                        MM(ps[:, :N], v4[:, f, 128 * c:128 * c + 128], h1[:, f, :N], f == 0, f == 3, [b2, bh], [bp])
                    if j == 0:
                        STT(X[:, c, :N], X[:, c, :N], ALPHA, ps[:, :N], ALU.mult, ALU.add, [bX, bp], [bX])
                    else:
                        TT("dve", X[:, c, :N], X[:, c, :N], ps[:, :N], ALU.add, [bX, bp], [bX])

            stage1(0)
            for j in range(8):
                if j + 1 < 8:
                    stage1(j + 1)
                stage2(j)

        def phaseA(l, prm, blk=0):
            N = 512 if prm else TS
            tok0 = 512 * blk
            X, bX, XB, bXB = (xres, b_xres, xbf, b_xbf) if prm else (xres_s, b_xres_s, xbf_s, b_xbf_s)
            uT = T3
            if prm:
                LD("sp", cosb[:], cosP[:, tok0:tok0 + 512], [b_cos])
                LD("sp", sinb[:], sinP[:, tok0:tok0 + 512], [b_sin])
                cs, sn, bcs, bsn = cosb, sinb, b_cos, b_sin
                if blk == 0:
                    memset("dve", uT[:, :, 0:16], 0.0, b_t3)
                else:
                    CP("dve", uT[:, :, 0:16], hal[:], [b_hal], b_t3)
            else:
                cs, sn, bcs, bsn = cosS, sinS, b_cosS, b_sinS
                uS = T3[:].rearrange("p g x -> p (g x)")[:, 0:4 * NSB * 20].rearrange("p (g b t) -> p g b t", g=4, b=NSB)
                memset("dve", T3[:], 0.0, b_t3)
                for i in range((NSB + 7) // 8):
                    nbt = min(8, NSB - 8 * i)
                    rows = nbt * 15
                    LD("sp", xstage[0:rows, 0:512], spool[l, 120 * i:120 * i + rows, :], [b_xst])
                    ps, bp = nb()
                    for g in range(4):
                        TR(ps[:, 128 * g:128 * g + rows], xstage[0:rows, 128 * g:128 * g + 128], ident32[0:rows, 0:rows],
                           [b_xst, b_id32], [bp])
                    for g in range(4):
                        CP("dve", uS[:, g, 8 * i:8 * i + nbt, 1:16],
                           ps[:, 128 * g:128 * g + rows].rearrange("p (b t) -> p b t", t=15), [bp], b_t3)
            v8, _, bs = snext(("w_in", l, 0))
            for g in range(4):
                ps, bp = nb()
                for k in range(8):
                    MM(ps[:, :N], v8[:, k, 128 * g:128 * g + 128], XB[:, k, :N], k == 0, k == 7, [bs, bXB], [bp])
                if prm:
                    CP("act", uT[:, g, 16:16 + N], ps[:, :N], [bp], b_t3)
                else:
                    CP("act", uS[:, g, :, 16:20], ps[:, :N].rearrange("p (b t) -> p b t", t=4), [bp], b_t3)
            if prm:
                sub("u")
            v8, _, bs = snext(("w_in", l, 1))
            cq32 = [T7[:, 0, :N], T7[:, 1, :N]]
            st_ps, bst = nb()
            for c in range(2):
                ps, bp = nb()
                for k in range(8):
                    MM(ps[:, :N], v8[:, k, 128 * c:128 * c + 128], XB[:, k, :N], k == 0, k == 7, [bs, bXB], [bp])
                DBG = os.environ.get("DBG", "")
                if "e" not in DBG:
                    CP("dve", cq32[c], ps[:, :N], [bp], [b_t7[c]])
                sq = T5[:, c, :N]
                if "d" not in DBG:
                    ACT(sq, ps[:, :N], AF.Square, [bp], [b_t5[c]])
                if "f" not in DBG:
                    MM(st_ps[:, :N], onesb[:], sq, c == 0, c == 1, [b_ones, b_t5[c]], [bst])
            rq = T7[:, 2, :N]
            DBG = os.environ.get("DBG", "")
            if "a" not in DBG:
                ACT(rq, st_ps[:, :N], AF.Sqrt, [bst], [b_t7[2]], bias=RMS_EPS, scale=1.0 / 256)
            if "b" not in DBG:
                RECIP(rq, rq, [b_t7[2]], [b_t7[2]])
            if "c" not in DBG:
                for c in range(2):
                    STT(cqn[:, c, :N], cq32[c], V(l, 52, c), rq, ALU.mult, ALU.mult, [b_t7[c], b_t7[2], b_vecs], [b_cqn])
            if prm:
                sub("cq")
            ps, bp = nb()
            for k in range(8):
                MM(ps[:, :N], v8[:, k, 256:384], XB[:, k, :N], k == 0, k == 7, [bs, bXB], [bp])
            ckv32 = T7[:, 3, :N]
            CP("dve", ckv32, ps[:, :N], [bp], [b_t7[3]])
            sq = T5[:, 2, :N]
            ACT(sq, ps[:, :N], AF.Square, [bp], [b_t5[2]])
            st_ps, bst = nb()
            MM(st_ps[:, :N], onesb[:], sq, True, True, [b_ones, b_t5[2]], [bst])
            rk = T7[:, 2, :N]
            ACT(rk, st_ps[:, :N], AF.Sqrt, [bst], [b_t7[2]], bias=RMS_EPS, scale=1.0 / 128)
            RECIP(rk, rk, [b_t7[2]], [b_t7[2]])
            STT(ckv32, ckv32, V(l, 54), rk, ALU.mult, ALU.mult, [b_t7[3], b_t7[2], b_vecs], [b_t7[3]])
            if prm:
                CP("act", KTl[:, tok0:tok0 + N], ckv32, [b_t7[3]], [b_ktl])
                ps, bp = nb()
                for tt in range(4):
                    TR(ps[:, 128 * tt:128 * tt + 128], ckv32[:, 128 * tt:128 * tt + 128], ident32[:], [b_t7[3], b_id32], [bp])
                CP("dve", Vt[:, 4 * blk:4 * blk + 4, :], ps[:].rearrange("p (t c) -> p t c", t=4), [bp], [b_v])
                CP("act", ystage[:, 0:512], ps[:], [bp], [b_yst])
                STO(ckv_p[l, tok0:tok0 + 512, :].rearrange("(t p) c -> p t c", p=128),
                    ystage[:, 0:512].rearrange("p (t c) -> p t c", t=4), [b_yst])
            else:
                CP("act", ckvnT_s[:], ckv32, [b_t7[3]], [b_ckvnT_s])
                ps, bp = nb()
                TR(ps[0:TS, 0:128], ckv32, ident32[:], [b_t7[3], b_id32], [bp])
                CP("act", ystage[0:TS, 0:128], ps[0:TS, 0:128], [bp], [b_yst])
                STO(ckv_s[l], ystage[0:TS, 0:128], [b_yst])
            if prm:
                sub("ckv")
            psk, bpk = nb()
            for k in range(8):
                MM(psk[0:32, :N], v8[:, k, 384:416], XB[:, k, :N], k == 0, k == 7, [bs, bXB], [bpk])
            psr, bpr = nb()
            for k in range(8):
                MM(psr[0:32, :N], v8[:, k, 416:448], XB[:, k, :N], k == 0, k == 7, [bs, bXB], [bpr])
            t1 = ptmp[0][0][0:32, :N]; bt1 = ptmp[0][1]
            t2 = ptmp[1][0][0:32, :N]; bt2 = ptmp[1][1]
            TT("dve", t1, psr[0:32, :N], sn[0:32, :N], ALU.mult, [bpr, bsn], [bt1])
            TT("dve", t2, psk[0:32, :N], cs[0:32, :N], ALU.mult, [bpk, bcs], [bt2])
            TT("dve", t1, t1, t2, ALU.add, [bt1, bt2], [bt1])
            if prm:
                CP("act", KTr[0:32, tok0:tok0 + N], t1, [bt1], [b_ktr])
                CP("dve", KTr[32:64, tok0:tok0 + N], t1, [bt1], [b_ktr])
                ps, bp = nb()
                for tt in range(4):
                    TR(ps[:, 32 * tt:32 * tt + 32], t1[:, 128 * tt:128 * tt + 128], ident32[0:32, 0:32], [bt1, b_id32], [bp])
                CP("act", ystage[:, 512:640], ps[:, 0:128], [bp], [b_yst])
                STO(kr_p[l, tok0:tok0 + 512, :].rearrange("(t p) c -> p t c", p=128),
                    ystage[:, 512:640].rearrange("p (t c) -> p t c", t=4), [b_yst])
            else:
                CP("act", krnT_s[0:32, :], t1, [bt1], [b_krnT_s])
                CP("dve", krnT_s[32:64, :], t1, [bt1], [b_krnT_s])
                ps, bp = nb()
                TR(ps[0:TS, 0:32], t1, ident32[0:32, 0:32], [bt1, b_id32], [bp])
                CP("act", ystage[0:TS, 512:544], ps[0:TS, 0:32], [bp], [b_yst])
                STO(kr_s[l], ystage[0:TS, 512:544], [b_yst])
            if prm:
                sub("kr")
            QN = T5
            for j in range(4):
                ps, bp = nb()
                for k in range(2):
                    MM(ps[:, :N], wuq[:, k, 128 * j:128 * j + 128], cqn[:, k, :N], k == 0, k == 1, [b_wuq, b_cqn], [bp])
                CP(ev_eng(), QN[:, j, :N], ps[:, :N], [bp], [b_t5[j]])
            bt1 = ptmp[0][1]; bt2 = ptmp[1][1]
            if prm:
                for j in range(4):
                    psa, bpa = nb()
                    for k in range(2):
                        MM(psa[0:64, :N], wuq[:, k, 512 + 64 * j:512 + 64 * j + 64], cqn[:, k, :N], k == 0, k == 1, [b_wuq, b_cqn], [bpa])
                    psb, bpb = nb()
                    for k in range(2):
                        MM(psb[0:64, :N], wuq[:, k, 768 + 64 * j:768 + 64 * j + 64], cqn[:, k, :N], k == 0, k == 1, [b_wuq, b_cqn], [bpb])
                    t1 = ptmp[0][0][0:64, :N]; t2 = ptmp[1][0][0:64, :N]
                    TT("dve", t1, psb[0:64, :N], sn[0:64, :N], ALU.mult, [bpb, bsn], [bt1])
                    TT("dve", t2, psa[0:64, :N], cs[0:64, :N], ALU.mult, [bpa, bcs], [bt2])
                    TT("dve", T6[:, j, :N], t1, t2, ALU.add, [bt1, bt2], [b_t6])
            else:
                for rep in range(2):
                    r0 = 32 * rep
                    psa, bpa = nb()
                    psb, bpb = nb()
                    for h in range(8):
                        for k in range(2):
                            MM(psa[r0:r0 + 32, TS * h:TS * h + TS], wuq[:, k, 512 + 32 * h:512 + 32 * h + 32], cqn[:, k, :N],
                               k == 0, k == 1, [b_wuq, b_cqn], [bpa])
                        for k in range(2):
                            MM(psb[r0:r0 + 32, TS * h:TS * h + TS], wuq[:, k, 768 + 32 * h:768 + 32 * h + 32], cqn[:, k, :N],
                               k == 0, k == 1, [b_wuq, b_cqn], [bpb])
                    for h in range(8):
                        t1 = ptmp[0][0][r0:r0 + 32, :N]; t2 = ptmp[1][0][r0:r0 + 32, :N]
                        TT("dve", t1, psb[r0:r0 + 32, TS * h:TS * h + TS], sn[r0:r0 + 32, :N], ALU.mult, [bpb, bsn], [bt1])
                        TT("dve", t2, psa[r0:r0 + 32, TS * h:TS * h + TS], cs[r0:r0 + 32, :N], ALU.mult, [bpa, bcs], [bt2])
                        TT("dve", qrope_s[r0:r0 + 32, h, :], t1, t2, ALU.add, [bt1, bt2], [b_qrope_s])
            if prm:
                sub("qrope")
            QL, bQL = (T1, b_t1) if prm else (qlat_s, [b_qlat_s])
            for h in range(8):
                ps, bp = nb()
                p0 = 64 * (h % 2)
                MM(ps[:, :N], wukT[p0:p0 + 64, h // 2, :], QN[p0:p0 + 64, h // 2, :N], True, True, [b_wukT, b_t5[h // 2]], [bp])
                CP(ev_eng(), QL[:, h, :N], ps[:, :N], [bp], bQL)
            if prm:
                sub("qlat")
            MIX, bMIX = (T2, b_t2) if prm else (mix_s, b_mix_s)
            if prm:
                def uview(g, a, b_):
                    return uT[:, g, a:b_]

                def tview(i, a, b_):
                    return ptmp[i][0][:, a:b_]
                L = 16 + N
            else:
                def uview(g, a, b_):
                    return uS[:, g, :, a:b_]

                def tview(i, a, b_):
                    return ptmp[i][0][:, 0:NSB * 20].rearrange("p (b t) -> p b t", b=NSB)[:, :, a:b_]
                L = 20
            for g in range(4):
                w = 2 << g
                cur = None
                lo = 0
                sh = 1
                for stp in range(g + 1):
                    lo2 = lo + sh
                    dst_i = stp % 2
                    if cur is None:
                        a0 = uview(g, lo2, L); a1 = uview(g, lo2 - sh, L - sh); bin_ = list(b_t3)
                    else:
                        a0 = tview(cur, lo2, L); a1 = tview(cur, lo2 - sh, L - sh); bin_ = [ptmp[cur][1]]
                    TT("pool", tview(dst_i, lo2, L), a0, a1, ALU.add, bin_, [ptmp[dst_i][1]])
                    cur = dst_i
                    lo = lo2
                    sh *= 2
                if prm and blk == 0:
                    TT("dve", tview(cur, 16, 32), tview(cur, 16, 32), corr[:, g, :], ALU.mult, [ptmp[cur][1], b_corr], [ptmp[cur][1]])
                if prm:
                    pooled = T4[:, g, :N]
                    STT(pooled, tview(cur, 16, L), 1.0 / w, uview(g, 16, L), ALU.mult, ALU.subtract,
                        [ptmp[cur][1]] + b_t3, [b_t4[g]])
                else:
                    pooled = T4[:, g, :N]
                    STT(pooled.rearrange("p (b t) -> p b t", t=4), tview(cur, 16, L), 1.0 / w, uview(g, 16, L),
                        ALU.mult, ALU.subtract, [ptmp[cur][1]] + b_t3, [b_t4[g]])
                ps, bp = nb()
                MM(ps[:, :N], wpool[:, g, :], pooled, True, True, [b_wpool, b_t4[g]], [bp])
                TSC("dve", MIX[:, g, :N], ps[:, :N], V(l, 48, g), None, ALU.mult, ALU.bypass, [bp, b_vecs], [bMIX] if not prm else [b_t2])
            if prm:
                sub("pool")
            if prm:
                if blk == NB - 1:
                    ps, bp = nb()
                    for g in range(4):
                        TR(ps[0:15, 128 * g:128 * g + 128], uT[:, g, N + 1:N + 16], ident32[:], b_t3 + [b_id32], [bp])
                    CP("act", ystage[0:15, 0:512], ps[0:15, :], [bp], [b_yst])
                    STO(pst_p[l], ystage[0:15, 0:512], [b_yst])
                else:
                    CP("dve", hal[:], uT[:, :, N:N + 16], b_t3, [b_hal])
            else:
                for i in range((NSB + 7) // 8):
                    nbt = min(8, NSB - 8 * i)
                    rows = nbt * 15
                    ps, bp = nb()
                    for g in range(4):
                        tmpc = ptmp[g % 2][0][:, 0:rows]; btc = ptmp[g % 2][1]
                        CP("dve", tmpc.rearrange("p (b t) -> p b t", t=15), uS[:, g, 8 * i:8 * i + nbt, 5:20], b_t3, [btc])
                        TR(ps[0:rows, 128 * g:128 * g + 128], tmpc, ident32[:], [btc, b_id32], [bp])
                    CP("act", ystage[0:rows, 0:512], ps[0:rows, :], [bp], [b_yst])
                    STO(pst_s[l, 120 * i:120 * i + rows, :], ystage[0:rows, 0:512], [b_yst])

        def prompt_attn(l, blk):
            N = 512
            nk = 4 * blk + 4
            pti = [0]
            for h in range(8):
                o_ps, bo = PS[4 + h % 2], PSB[4 + h % 2]
                l_ps, bl = PS[6 + h % 2], PSB[6 + h % 2]
                p0 = 32 * (h % 2)
                def emit_S(kt):
                    j = kt - 4 * blk
                    c0 = 128 * j if j > 0 else 0
                    s_ps, bsp = nb()
                    MM(s_ps[:, c0:N], KTl[:, 128 * kt:128 * kt + 128], T1[:, h, c0:N], True, False, [b_ktl] + b_t1, [bsp])
                    MM(s_ps[:, c0:N], KTr[p0:p0 + 32, 128 * kt:128 * kt + 128], T6[p0:p0 + 32, h // 2, c0:N], False, True,
                       [b_ktr, b_t6], [bsp])
                    return s_ps, bsp, c0, j

                pend = emit_S(0)
                for kt in range(nk):
                    s_ps, bsp, c0, j = pend
                    if kt + 1 < nk:
                        pend = emit_S(kt + 1)
                    pi = pti[0] % 4
                    pti[0] += 1
                    pt = T4[:, pi, :]; bpt = b_t4[pi]
                    ACT(pt[:, c0:N], s_ps[:, c0:N], AF.Exp, [bsp], [bpt], scale=SM_SCALE)
                    if j >= 0:
                        TT("pool", pt[:, c0:N], pt[:, c0:N], maskD[:, j, c0:N], ALU.mult, [bpt, b_mask], [bpt])
                    MM(o_ps[:, c0:N], Vt[:, kt, :], pt[:, c0:N], kt == 0, kt == nk - 1, [b_v, bpt], [bo])
                    MM(l_ps[:, c0:N], onesb[:], pt[:, c0:N], kt == 0, kt == nk - 1, [b_ones, bpt], [bl])
                rec = T3[:, h % 2, :N]; brec = b_t3[h % 2]
                RECIP(rec, l_ps[:, :N], [bl], [brec])
                ol, bol = olat[h % 2]
                TT("dve", ol[:, :N], o_ps[:, :N], rec, ALU.mult, [bo, brec], [bol])
                if h % 2 == 1:
                    ps, bp = nb()
                    MM(ps[0:64, :N], wuv[:, 64 * (h - 1):64 * h], olat[0][0][:, :N], True, True, [b_wuv, olat[0][1]], [bp])
                    MM(ps[64:128, :N], wuv[:, 64 * h:64 * h + 64], olat[1][0][:, :N], True, True, [b_wuv, olat[1][1]], [bp])
                    CP("act", T2[:, 4 + h // 2, :N], ps[:, :N], [bp], [b_t2])

        sq = {"issued": 0}

        def gather_ahead(l, k):
            while sq["issued"] <= k and sq["issued"] < NSB * 4:
                b, q = divmod(sq["issued"], 4)
                i = sq["issued"] % 2
                tk, bk = gck[i]
                tr_, br = gkr[i]
                mk.dma("pool", "indirect_dma_start",
                       dict(out=tk[:], out_offset=None, in_=ckvd[l],
                            in_offset=bass.IndirectOffsetOnAxis(ap=idx4[:, q, b:b + 1], axis=0)),
                       outs=[bk], deps=[b_idx4])
                mk.dma("pool", "indirect_dma_start",
                       dict(out=tr_[:], out_offset=None, in_=krd[l],
                            in_offset=bass.IndirectOffsetOnAxis(ap=idx4[:, q, b:b + 1], axis=0)),
                       outs=[br], deps=[b_idx4])
                sq["issued"] += 1

        def sample_attn_batch(l, b):
            CP("dve", qlat_b[:].rearrange("p (t h) -> p h t", h=8), qlat_s[:, :, 4 * b:4 * b + 4], [b_qlat_s], [b_qlat_b])
            CP("dve", qrope_full[:].rearrange("p (t h) -> p h t", h=8), qrope_s[:, :, 4 * b:4 * b + 4], [b_qrope_s], [b_qrope_full])
            o_ps, bo = PS[4], PSB[4]
            memset("dve", lacc[:], 0.0, [b_lacc])
            first = [True]
            groups = [(q, grp, g4) for q in range(4) for grp in range(2) for g4 in range(4)]
            gi = [0]

            def prep(idx):
                q, grp, g4 = groups[idx]
                kq = 4 * b + q
                if grp == 0 and g4 == 0:
                    gather_ahead(l, kq + 1)
                tk, bk = gck[kq % 2]
                tr_, br = gkr[kq % 2]
                tb = 16 * grp + 4 * g4
                kts, bkts = KTs[idx % 2]
                krs, bkrs = krTs[idx % 2]
                tp, btp = nb()
                tpb = tp[:].bitcast(BF16)
                for e_ in range(4):
                    TR(tpb[:, 128 * e_:128 * e_ + 128], tk[:, 128 * (tb + e_):128 * (tb + e_) + 128], identb[:],
                       [bk, b_idb], [btp])
                CP("dve", kts[:], tpb[:, 0:512], [btp], [bkts])
                tp2, btp2 = nb()
                tpb2 = tp2[:].bitcast(BF16)
                for e2 in range(2):
                    TR(tpb2[0:64, 128 * e2:128 * e2 + 128], tr_[:, 32 * (tb + 2 * e2):32 * (tb + 2 * e2) + 64], identb[:],
                       [br, b_idb], [btp2])
                CP("act", krs[:].rearrange("p e k -> p (e k)"), tpb2[0:64, 0:256], [btp2], [bkrs])
                return (kts, bkts, krs, bkrs)

            def scores(idx, slots_):
                q, grp, g4 = groups[idx]
                kts, bkts, krs, bkrs = slots_
                s_ps, bsp = PS[5 + 2 * grp], PSB[5 + 2 * grp]
                for e_ in range(4):
                    sl = 4 * g4 + e_
                    pr = 32 * (e_ % 2)
                    MM(s_ps[:, 32 * sl:32 * sl + 32], kts[:, 128 * e_:128 * e_ + 128], qlat_b[:], True, False,
                       [bkts, b_qlat_b], [bsp])
                    MM(s_ps[:, 32 * sl:32 * sl + 32], krs[pr:pr + 32, e_ // 2, :], qrope_full[pr:pr + 32, :], False, True,
                       [bkrs, b_qrope_full], [bsp])

            def softmax_part(q, grp):
                s_ps, bsp = PS[5 + 2 * grp], PSB[5 + 2 * grp]
                ptile, bpt = pts[grp]
                ACT(ptile[:].rearrange("p s q -> p (s q)"), s_ps[:], AF.Exp, [bsp], [bpt], scale=SM_SCALE)
                mk.op("dve", "tensor_reduce", dict(out=lred[:], in_=ptile[:].rearrange("p s q -> p q s"), axis=AX.X, op=ALU.add),
                      [bpt], [b_lred])
                TT("dve", lacc[:], lacc[:], lred[:], ALU.add, [b_lacc, b_lred], [b_lacc])

            def pv_part(q, grp):
                kq = 4 * b + q
                tk, bk = gck[kq % 2]
                ptile, bpt = pts[grp]
                for sl in range(16):
                    tkn = 16 * grp + sl
                    MM(o_ps[:, 0:32], tk[:, 128 * tkn:128 * tkn + 128], ptile[:, sl, :], first[0], False, [bk, bpt], [bo])
                    first[0] = False

            pend = prep(0)
            pend_pv = None
            for idx in range(len(groups)):
                q, grp, g4 = groups[idx]
                cur = pend
                if idx + 1 < len(groups) and not (groups[idx + 1][1] == 0 and groups[idx + 1][2] == 0):
                    pend = prep(idx + 1)
                    nxt_done = True
                else:
                    nxt_done = False
                scores(idx, cur)
                if pend_pv is not None and g4 == 0:
                    pv_part(*pend_pv)
                    pend_pv = None
                if g4 == 3:
                    softmax_part(q, grp)
                    pend_pv = (q, grp)
                    if grp == 1:
                        pv_part(*pend_pv)
                        pend_pv = None
                if not nxt_done and idx + 1 < len(groups):
                    pend = prep(idx + 1)
            s_ps, bsp = nb()
            MM(s_ps[0:4, 0:32], ckvnT_s[:, 4 * b:4 * b + 4], qlat_b[:], True, False, [b_ckvnT_s, b_qlat_b], [bsp])
            MM(s_ps[0:4, 0:32], krnT_s[0:32, 4 * b:4 * b + 4], qrope_full[0:32, :], False, True, [b_krnT_s, b_qrope_full], [bsp])
            ACT(pnew[:], s_ps[0:4, 0:32], AF.Exp, [bsp], [b_pnew], scale=SM_SCALE)
            TT("dve", pnew[:], pnew[:], smask[:], ALU.mult, [b_pnew, b_smask], [b_pnew])
            CP("dve", pnewb[:], pnew[:], [b_pnew], [b_pnewb])
            tp, btp = nb()
            tpb = tp[:].bitcast(BF16)
            TR(tpb[0:4, 0:128], ckvnT_s[:, 4 * b:4 * b + 4], identb[:], [b_ckvnT_s, b_idb], [btp])
            CP("dve", vnew[:], tpb[0:4, 0:128], [btp], [b_vnew])
            MM(o_ps[:, 0:32], vnew[:], pnewb[:], False, True, [b_vnew, b_pnewb], [bo])
            l_ps, bl = PS[6], PSB[6]
            MM(l_ps[:, 0:32], ones32[:], lacc[:], True, False, [b_ones32, b_lacc], [bl])
            MM(l_ps[:, 0:32], ones32[0:4, :], pnew[:], False, True, [b_ones32, b_pnew], [bl])
            RECIP(recs[:], l_ps[:, 0:32], [bl], [b_recs])
            TT("dve", olat_s[:, :, 4 * b:4 * b + 4], o_ps[:, 0:32].rearrange("p (t h) -> p h t", h=8),
               recs[:].rearrange("p (t h) -> p h t", h=8), ALU.mult, [bo, b_recs], [b_olat_s])

        def mem_kv_prompt(l):
            if True:
                for mt in range(2):
                    LD("sp", xstage[:], memp[128 * mt:128 * mt + 128, :], [b_xst])
                    for half in range(2):
                        ps, bp = nb()
                        for c4 in range(4):
                            c = 4 * half + c4
                            TR(ps[:, 128 * c4:128 * c4 + 128], xstage[:, 128 * c:128 * c + 128], ident32[:], [b_xst, b_id32], [bp])
                        CP("dve", mempT[:, 4 * half:4 * half + 4, 128 * mt:128 * mt + 128],
                           ps[:].rearrange("p (c m) -> p c m", c=4), [bp], b_t1)
            for s in range(2):
                v8, _, bs = snext(("w_mk", l, s))
                for c4 in range(4):
                    ps, bp = nb()
                    for k in range(8):
                        MM(ps[:, 0:256], v8[:, k, 128 * c4:128 * c4 + 128], mempT[:, k, :], k == 0, k == 7, [bs] + b_t1, [bp])
                    CP("act", mkT[:, 4 * s + c4, :], ps[:, 0:256], [bp], [b_mkT])
                for mt in range(2):
                    ps, bp = nb()
                    for k in range(8):
                        MM(ps[:], mempT[:, k, 128 * mt:128 * mt + 128], v8[:, k, :], k == 0, k == 7, [bs] + b_t1, [bp])
                    CP("act", ystage[:, 0:512], ps[:], [bp], [b_yst])
                    STO(mk_p[l, 128 * mt:128 * mt + 128, 512 * s:512 * s + 512], ystage[:, 0:512], [b_yst])
            for s in range(2):
                v8, _, bs = snext(("w_mv", l, s))
                for mt in range(2):
                    ps, bp = nb()
                    for k in range(8):
                        MM(ps[:], mempT[:, k, 128 * mt:128 * mt + 128], v8[:, k, :], k == 0, k == 7, [bs] + b_t1, [bp])
                    CP("act", ystage[:, 0:512], ps[:], [bp], [b_yst])
                    CP("dve", mvp[:, mt, 512 * s:512 * s + 512], ps[:], [bp], [b_mvp])
                    STO(mv_p[l, 128 * mt:128 * mt + 128, 512 * s:512 * s + 512], ystage[:, 0:512], [b_yst])

        def load_small(l):
            for k in range(2):
                src = w_uq[l, 128 * k:128 * k + 128, :].rearrange("p (h d) -> p h d", h=8)
                LD("pool", wuq[:, k, 0:512].rearrange("p (h d) -> p h d", h=8), src[:, :, 0:64], [b_wuq])
                LD("pool", wuq[:, k, 512:768].rearrange("p (h d) -> p h d", h=8), src[:, :, 64:96], [b_wuq])
                rot = wuq[:, k, 768:1024].rearrange("p (h d) -> p h d", h=8)
                LD("pool", rot[:, :, 0:16], src[:, :, 80:96], [b_wuq])
                LD("pool", rot[:, :, 16:32], src[:, :, 64:80], [b_wuq])
            LD("pool", wuk[:], w_uk[l], [b_wuk])
            LD("pool", wuv[:], w_uv[l], [b_wuv])
            LD("pool", wpool[:], w_pool[l].rearrange("g c d -> c g d"), [b_wpool])
            for j in range(4):
                tp, btp = nb()
                tpb = tp[:].bitcast(BF16)
                TR(tpb[:, 0:128], wuk[:, 128 * j:128 * j + 128], identb[:], [b_wuk, b_idb], [btp])
                CP("dve", wukT[:, j, :], tpb[:, 0:128], [btp], [b_wukT])

        def phaseB(l, prm):
            N = 512 if prm else TS
            X, bX, XB, bXB = (xres, b_xres, xbf, b_xbf) if prm else (xres_s, b_xres_s, xbf_s, b_xbf_s)
            MIX, bMIX = (T2, [b_t2]) if prm else (mix_s, [b_mix_s])
            QM, bQM = (T1, b_t1) if prm else (qlat_s, [b_qlat_s])
            if not prm:
                for hp in range(4):
                    ps, bp = nb()
                    MM(ps[0:64, :N], wuv[:, 128 * hp:128 * hp + 64], olat_s[:, 2 * hp, :], True, True, [b_wuv, b_olat_s], [bp])
                    MM(ps[64:128, :N], wuv[:, 128 * hp + 64:128 * hp + 128], olat_s[:, 2 * hp + 1, :], True, True, [b_wuv, b_olat_s], [bp])
                    CP("act", mix_s[:, 4 + hp, :], ps[:, :N], [bp], [b_mix_s])
            proj_residual("w_o", l, MIX, bMIX, X, bX, N)
            layer_norm(l, 0, X, bX, XB, bXB, N)
            for s in range(2):
                v8, _, bs = snext(("w_mq", l, s))
                for c4 in range(4):
                    ps, bp = nb()
                    for k in range(8):
                        MM(ps[:, :N], v8[:, k, 128 * c4:128 * c4 + 128], XB[:, k, :N], k == 0, k == 7, [bs, bXB], [bp])
                    CP(ev_eng(), QM[:, 4 * s + c4, :N], ps[:, :N], [bp], bQM)
            if prm:
                mem_attn(QM, bQM, T2, [b_t2], 512, 0, mkT, b_mkT, mvp, b_mvp)
            else:
                for b in range(NSB):
                    LD("pool", memk_sb[:], memk[l, b].rearrange("(t p) n -> p t n", p=128), [b_memk])
                    LD("pool", memv_sb[:], memv[l, b].rearrange("(t p) n -> p t n", p=128), [b_memv])
                    for c in range(8):
                        tp, btp = nb()
                        tpb = tp[:].bitcast(BF16)
                        for mt in range(2):
                            TR(tpb[:, 128 * mt:128 * mt + 128], memk_sb[:, mt, 128 * c:128 * c + 128], identb[:], [b_memk, b_idb], [btp])
                        CP(ev_eng(), mkT_s[:, c, :], tpb[:, 0:256], [btp], [b_mkT_s])
                    mem_attn(QM, bQM, mix_s, [b_mix_s], 4, 4 * b, mkT_s, b_mkT_s, memv_sb, b_memv)
            proj_residual("w_mo", l, MIX, bMIX, X, bX, N)
            layer_norm(l, 1, X, bX, XB, bXB, N)
            mlp(l, XB, bXB, X, bX, N)
            layer_norm(l, 2, X, bX, XB, bXB, N)

        def prompt_load(l, blk):
            tok0 = 512 * blk
            if l == 0:
                for tt in range(4):
                    LD("sp", xstage[:], xp[tok0 + 128 * tt:tok0 + 128 * tt + 128, :], [b_xst])
                    for half in range(2):
                        ps, bp = nb()
                        for c4 in range(4):
                            c = 4 * half + c4
                            TR(ps[:, 128 * c4:128 * c4 + 128], xstage[:, 128 * c:128 * c + 128], ident32[:], [b_xst, b_id32], [bp])
                        CP("dve", xres[:, 4 * half:4 * half + 4, 128 * tt:128 * tt + 128], ps[:].rearrange("p (c m) -> p c m", c=4), [bp], [b_xres])
                        CP("dve", xbf[:, 4 * half:4 * half + 4, 128 * tt:128 * tt + 128], ps[:].rearrange("p (c m) -> p c m", c=4), [bp], [b_xbf])
            else:
                LD("sp", xres[:], scr[:, :, tok0:tok0 + 512], [b_xres])
                CP("dve", xbf[:], xres[:], [b_xres], [b_xbf])

        def prompt_store(l, blk):
            tok0 = 512 * blk
            if l == 0:
                mk.dma("sp", "dma_start", dict(out=scr[:, :, tok0:tok0 + 512], in_=xres[:]), ins=[b_xres])
            else:
                for tt in range(4):
                    for half in range(2):
                        ps, bp = nb()
                        for c4 in range(4):
                            c = 4 * half + c4
                            TR(ps[:, 128 * c4:128 * c4 + 128], xres[:, c, 128 * tt:128 * tt + 128], ident32[:], [b_xres, b_id32], [bp])
                        CP(ev_eng(), ystage[:, 512 * half:512 * half + 512], ps[:], [bp], [b_yst])
                    STO(y_p[tok0 + 128 * tt:tok0 + 128 * tt + 128, :], ystage[:], [b_yst])

        def sample_load():
            LD("sp", xstage[0:TS, :], xs, [b_xst])
            for half in range(2):
                ps, bp = nb()
                for c4 in range(4):
                    c = 4 * half + c4
                    TR(ps[:, TS * c4:TS * c4 + TS], xstage[0:TS, 128 * c:128 * c + 128], ident32[0:TS, 0:TS], [b_xst, b_id32], [bp])
                CP("act", xres_s[:, 4 * half:4 * half + 4, :], ps[:, 0:4 * TS].rearrange("p (c m) -> p c m", c=4), [bp], [b_xres_s])
                CP("dve", xbf_s[:, 4 * half:4 * half + 4, :], ps[:, 0:4 * TS].rearrange("p (c m) -> p c m", c=4), [bp], [b_xbf_s])

        def sample_store():
            for half in range(2):
                ps, bp = nb()
                for c4 in range(4):
                    c = 4 * half + c4
                    TR(ps[0:TS, 128 * c4:128 * c4 + 128], xres_s[:, c, :], ident32[:], [b_xres_s, b_id32], [bp])
                CP(ev_eng(), ystage[0:TS, 512 * half:512 * half + 512], ps[0:TS, :], [bp], [b_yst])
            STO(y_s, ystage[0:TS, :], [b_yst])

        import os
        KSTOP = int(os.environ.get("KSTOP", "100000"))
        stg = [0]

        def stage(name):
            stg[0] += 1
            if stg[0] >= KSTOP:
                print("STOP at stage", stg[0], name)
                raise _Stop()

        try:
            sample_load()
            stage("sample_load")
            for l in range(2):
                load_small(l)
                stage("load_small")
                mem_kv_prompt(l)
                stage("mem_kv")
                sq["issued"] = 0
                gather_ahead(l, 0)
                phaseA(l, False)
                stage("phaseA_s")
                done_b = 0
                for blk in range(NB):
                    prompt_load(l, blk)
                    stage("prompt_load")
                    phaseA(l, True, blk)
                    stage("phaseA_p")
                    prompt_attn(l, blk)
                    stage("attn_p")
                    phaseB(l, True)
                    stage("phaseB_p")
                    prompt_store(l, blk)
                    stage("store_p")
                    tgt = (NSB * (blk + 1)) // NB
                    while done_b < tgt:
                        sample_attn_batch(l, done_b)
                        done_b += 1
                    stage("sample_attn")
                phaseB(l, False)
                stage("phaseB_s")
            sample_store()
            assert ptr["c"] == len(plan), (ptr["c"], len(plan))
        except _Stop:
            pass
        mk.finish()
        mk.emit()
    return nc


def _consts(S, NSB, past_len):
    TS = 4 * NSB
    half = 16
    inv = (np.float32(10000.0) ** (-(np.arange(half, dtype=np.float32) / np.float32(half)))).astype(np.float32)

    def tables(pos):
        ang = pos.astype(np.float32)[:, None] * inv[None, :]
        c, s = np.cos(ang).astype(np.float32), np.sin(ang).astype(np.float32)
        cos32 = np.concatenate([c, c], axis=1).T
        sin32 = np.concatenate([-s, s], axis=1).T
        return np.tile(cos32, (4, 1)).copy(), np.tile(sin32, (4, 1)).copy()
    cosP, sinP = tables(np.arange(S))
    cs4, sn4 = tables(past_len + np.arange(4))
    cosS = np.tile(cs4, (1, NSB)).copy()
    sinS = np.tile(sn4, (1, NSB)).copy()
    p = np.arange(128)[:, None, None]
    j = np.arange(4)[None, :, None]
    n = np.arange(512)[None, None, :]
    maskD = (n >= 128 * j + p).astype(np.float32).reshape(128, 2048)
    jj = np.arange(4)[:, None]
    tt = (np.arange(32) // 8)[None, :]
    smask = (jj <= tt).astype(np.float32)
    corr = np.zeros((128, 4, 16), np.float32)
    for g in range(4):
        w = 2 << g
        corr[:, g, :] = (w / np.minimum(w, np.arange(16) + 1.0))[None, :]
    qoff = np.zeros((128, 4, NSB), np.int32)
    for q in range(4):
        qoff[:, q, :] = q
    return dict(ident=np.eye(128, dtype=np.float32), cosP=cosP, sinP=sinP, cosS=cosS, sinS=sinS, maskD=maskD,
                smask=smask, corr=corr.reshape(128, 64), qoff=qoff.reshape(128, 4 * NSB))


def _vecs(inp):
    v = np.zeros((128, 110), np.float32)
    for l in range(2):
        b = 55 * l
        for i, nm in enumerate(["ln1_g", "ln1_b", "ln2_g", "ln2_b", "ln3_g", "ln3_b"]):
            v[:, b + 8 * i:b + 8 * i + 8] = np.asarray(inp[nm][l]).reshape(8, 128).T
        v[:, b + 48:b + 52] = np.asarray(inp["pool_scale"][l]).reshape(4, 128).T
        v[:, b + 52:b + 54] = np.asarray(inp["g_q"][l]).reshape(2, 128).T
        v[:, b + 54] = np.asarray(inp["g_kv"][l])
    return v


_NC_CACHE = {}


def make_in_maps(inp, n_cores):
    xpr = np.asarray(inp["x_prompt"]); xsm = np.asarray(inp["x_sample"])
    B, S, _ = xpr.shape
    DB = xsm.shape[0]
    NSB = DB // n_cores
    pt = np.asarray(inp["page_table"])
    n_pages = pt.shape[1]
    assert n_pages == 128 and B == n_cores
    cck = np.asarray(inp["cache_ckv"]); ckr = np.asarray(inp["cache_krope"])
    NPHYS = cck.shape[1]
    cst = _consts(S, NSB, n_pages * 128)
    vec = _vecs(inp)
    f = lambda k: np.ascontiguousarray(np.asarray(inp[k]), dtype=np.float32)
    shared = dict(
        ckv0=cck[0].reshape(NPHYS * 4, 4096), ckv1=cck[1].reshape(NPHYS * 4, 4096),
        kr0=ckr[0].reshape(NPHYS * 4, 1024), kr1=ckr[1].reshape(NPHYS * 4, 1024),
        w_in=f("w_in"), w_uq=f("w_uq"), w_uk=f("w_uk").reshape(2, 128, 512), w_uv=f("w_uv").reshape(2, 128, 512),
        w_pool=f("w_pool"), w_o=f("w_o"), w_mq=f("w_mq"), w_mk=f("w_mk"), w_mv=f("w_mv"), w_mo=f("w_mo"),
        w1=f("w1"), w2=f("w2"), vecs=vec, **cst)
    mk_ = np.asarray(inp["cache_mem_k"]); mv_ = np.asarray(inp["cache_mem_v"]); sp_ = np.asarray(inp["state_pool"])
    mp_ = np.asarray(inp["mem_prompt"])
    maps = []
    for c in range(n_cores):
        sl = slice(NSB * c, NSB * c + NSB)
        m = dict(shared)
        m.update(
            xp=xpr[c], xs=xsm[sl].reshape(NSB * 4, 1024),
            memk=mk_[:, sl].reshape(2, NSB, 256, 1024), memv=mv_[:, sl].reshape(2, NSB, 256, 1024),
            spool=sp_[:, sl].reshape(2, NSB * 15, 512),
            ptT=np.ascontiguousarray(pt[sl].T.astype(np.int32)), memp=mp_[c])
        maps.append(m)
    return maps, (S, NSB, NPHYS)


def assemble(results, n_cores, S, NSB):
    DB = NSB * n_cores
    y_p = np.stack([r["y_p"] for r in results])
    y_s = np.concatenate([r["y_s"].reshape(NSB, 4, 1024) for r in results])
    ckv_p = np.stack([r["ckv_p"] for r in results], axis=1)
    kr_p = np.stack([r["kr_p"] for r in results], axis=1)
    mk_p = np.stack([r["mk_p"].reshape(2, 256, 4, 256) for r in results], axis=1)
    mv_p = np.stack([r["mv_p"].reshape(2, 256, 4, 256) for r in results], axis=1)
    pst_p = np.stack([r["pst_p"] for r in results], axis=1)
    ckv_s = np.concatenate([r["ckv_s"].reshape(2, NSB, 4, 128) for r in results], axis=1)
    kr_s = np.concatenate([r["kr_s"].reshape(2, NSB, 4, 32) for r in results], axis=1)
    pst_s = np.concatenate([r["pst_s"].reshape(2, NSB, 15, 512) for r in results], axis=1)
    return tuple(np.ascontiguousarray(a, dtype=np.float32) for a in
                 (y_p, y_s, ckv_p, kr_p, mk_p, mv_p, pst_p, ckv_s, kr_s, pst_s))


def kernel(**inputs):
    maps, (S, NSB, NPHYS) = make_in_maps(inputs, NCORES)
    key = (S, NSB, NPHYS)
    if key not in _NC_CACHE:
        _NC_CACHE[key] = build(S, NSB, NPHYS)
    nc = _NC_CACHE[key]
    res = run_bass_kernel_spmd(nc, maps, core_ids=list(range(NCORES)))
    return assemble(res.results, NCORES, S, NSB)
```
